# Optimizing a Trainium2 kernel written in Bass

```python
import math
import jax, jax.numpy as jnp
from jax import lax
import numpy as np

D_MODEL = 1024
BATCH = 2
SEQ = 8192
DEPTH = 2

GRID_W = 64
CTX_LEN = 256
HEAD_DIM = 64
N_HEADS_A = 8
N_KV_A = 2
N_HEADS_B = 8
N_HEADS_C = 8
N_KV_C = 2
MIX_WIDTH = HEAD_DIM * (N_HEADS_A + N_HEADS_B + N_HEADS_C)
Q_BLOCK = 128
NB_ROWS = 8
NB_COLS = 16
WINDOW = 128
ROPE_THETA = 10000.0
EPS = 1e-6
IN_SIZES = (
    N_HEADS_A * HEAD_DIM, N_KV_A * HEAD_DIM, N_KV_A * HEAD_DIM,
    N_HEADS_B * HEAD_DIM, N_HEADS_B * HEAD_DIM, N_HEADS_B * HEAD_DIM,
    N_HEADS_C * HEAD_DIM, N_KV_C * HEAD_DIM, N_KV_C * HEAD_DIM,
    MIX_WIDTH,
)
IN_WIDTH = sum(IN_SIZES)

kernel_name = "hybrid_parallel_groups_flow_backbone"


def rmsnorm(x, w):
    xf = x.astype(jnp.float32)
    y = xf * lax.rsqrt(jnp.mean(xf * xf, axis=-1, keepdims=True) + EPS)
    return y.astype(x.dtype) * w


def softmax_f32(s, dtype):
    return jax.nn.softmax(s.astype(jnp.float32), axis=-1).astype(dtype)


def heads(z, h):
    return z.reshape(z.shape[:-1] + (h, HEAD_DIM))


def split_cols(z):
    outs, off = [], 0
    for sz in IN_SIZES:
        outs.append(z[..., off:off + sz])
        off += sz
    return outs


def axial_rope(n, dtype):
    t = jnp.arange(n, dtype=jnp.int32)
    rows = (t // GRID_W).astype(jnp.float32)
    cols = (t % GRID_W).astype(jnp.float32)
    n_freq = HEAD_DIM // 4
    freq = ROPE_THETA ** (-jnp.arange(n_freq, dtype=jnp.float32) / n_freq)
    ang = jnp.concatenate([rows[:, None] * freq, cols[:, None] * freq], axis=-1)
    return jnp.cos(ang).astype(dtype), jnp.sin(ang).astype(dtype)


def apply_rope(x, cos, sin):
    x1, x2 = jnp.split(x, 2, axis=-1)
    c = cos[None, :, None, :]
    s = sin[None, :, None, :]
    return jnp.concatenate([x1 * c - x2 * s, x1 * s + x2 * c], axis=-1)


def global_gqa(q, k, v, qc, kc, vc, cos, sin, qn, kn, need_ctx):
    B, N = q.shape[0], q.shape[1]
    C = qc.shape[1]
    G = N_HEADS_A // N_KV_A
    scale = HEAD_DIM ** -0.5
    q = apply_rope(rmsnorm(heads(q, N_HEADS_A), qn), cos, sin) * scale
    k = apply_rope(rmsnorm(heads(k, N_KV_A), kn), cos, sin)
    v = heads(v, N_KV_A)
    qc = rmsnorm(heads(qc, N_HEADS_A), qn) * scale
    kc = rmsnorm(heads(kc, N_KV_A), kn)
    vc = heads(vc, N_KV_A)
    k_all = jnp.concatenate([k, kc], axis=1)
    v_all = jnp.concatenate([v, vc], axis=1)
    nblk = N // Q_BLOCK
    qb = q.reshape(B, nblk, Q_BLOCK, N_KV_A, G, HEAD_DIM).transpose(1, 0, 2, 3, 4, 5)

    def block(qi):
        s = jnp.einsum('bqhgd,bkhd->bhgqk', qi, k_all)
        p = softmax_f32(s, v_all.dtype)
        return jnp.einsum('bhgqk,bkhd->bqhgd', p, v_all)

    y = lax.map(block, qb).transpose(1, 0, 2, 3, 4, 5).reshape(B, N, N_HEADS_A * HEAD_DIM)
    yc = None
    if need_ctx:
        qcg = qc.reshape(B, C, N_KV_A, G, HEAD_DIM)
        p = softmax_f32(jnp.einsum('bqhgd,bkhd->bhgqk', qcg, kc), vc.dtype)
        yc = jnp.einsum('bhgqk,bkhd->bqhgd', p, vc).reshape(B, C, N_HEADS_A * HEAD_DIM)
    return y, yc


def neighbourhood_attn(q, k, v, qc, kc, vc, rpb, need_ctx):
    B, N = q.shape[0], q.shape[1]
    C = qc.shape[1]
    H = N_HEADS_B
    scale = HEAD_DIM ** -0.5
    rows = N // GRID_W
    kh = min(NB_ROWS, rows)
    kw = NB_COLS
    t = jnp.arange(N, dtype=jnp.int32)
    r = t // GRID_W
    col = t % GRID_W
    rs = jnp.clip(r - kh // 2, 0, rows - kh)
    cs = jnp.clip(col - kw // 2, 0, GRID_W - kw)
    key_r = rs[:, None, None] + jnp.arange(kh, dtype=jnp.int32)[None, :, None]
    key_c = cs[:, None, None] + jnp.arange(kw, dtype=jnp.int32)[None, None, :]
    idx = (key_r * GRID_W + key_c).reshape(N, kh * kw)
    rel = ((key_r - r[:, None, None] + NB_ROWS - 1) * (2 * NB_COLS - 1)
           + (key_c - col[:, None, None] + NB_COLS - 1)).reshape(N, kh * kw)
    rpb_flat = rpb.reshape(H, -1)
    q = heads(q, H) * scale
    k = heads(k, H)
    v = heads(v, H)
    qc = heads(qc, H) * scale
    kc = heads(kc, H)
    vc = heads(vc, H)
    nblk = N // Q_BLOCK
    kn = kh * kw
    xs = (q.reshape(B, nblk, Q_BLOCK, H, HEAD_DIM).transpose(1, 0, 2, 3, 4),
          idx.reshape(nblk, Q_BLOCK, kn), rel.reshape(nblk, Q_BLOCK, kn))

    def block(args):
        qi, ii, ri = args
        kg = jnp.take(k, ii, axis=1)
        vg = jnp.take(v, ii, axis=1)
        s_nb = jnp.einsum('bqhd,bqkhd->bhqk', qi, kg) + jnp.take(rpb_flat, ri, axis=1)[None]
        s_ctx = jnp.einsum('bqhd,bchd->bhqc', qi, kc)
        p = softmax_f32(jnp.concatenate([s_nb, s_ctx], axis=-1), v.dtype)
        return (jnp.einsum('bhqk,bqkhd->bqhd', p[..., :kn], vg)
                + jnp.einsum('bhqc,bchd->bqhd', p[..., kn:], vc))

    y = lax.map(block, xs).transpose(1, 0, 2, 3, 4).reshape(B, N, H * HEAD_DIM)
    yc = None
    if need_ctx:
        p = softmax_f32(jnp.einsum('bqhd,bchd->bhqc', qc, kc), vc.dtype)
        yc = jnp.einsum('bhqc,bchd->bqhd', p, vc).reshape(B, C, H * HEAD_DIM)
    return y, yc


def window_gqa(q, k, v, qc, kc, vc, cos, sin, sink, need_ctx):
    B, N = q.shape[0], q.shape[1]
    C = qc.shape[1]
    G = N_HEADS_C // N_KV_C
    scale = HEAD_DIM ** -0.5
    q = (apply_rope(heads(q, N_HEADS_C), cos, sin) * scale).reshape(B, N, N_KV_C, G, HEAD_DIM)
    k = apply_rope(heads(k, N_KV_C), cos, sin)
    v = heads(v, N_KV_C)
    qc = (heads(qc, N_HEADS_C) * scale).reshape(B, C, N_KV_C, G, HEAD_DIM)
    kc = heads(kc, N_KV_C)
    vc = heads(vc, N_KV_C)
    pad = ((0, 0), (WINDOW, WINDOW), (0, 0), (0, 0))
    kp = jnp.pad(k, pad)
    vp = jnp.pad(v, pad)
    span = Q_BLOCK + 2 * WINDOW
    sink_l = sink.astype(jnp.float32).reshape(N_KV_C, G)
    neg = jnp.finfo(jnp.float32).min
    nblk = N // Q_BLOCK
    qs = q.reshape(B, nblk, Q_BLOCK, N_KV_C, G, HEAD_DIM).transpose(1, 0, 2, 3, 4, 5)

    def block(args):
        i, qi = args
        start = i * Q_BLOCK
        kb = lax.dynamic_slice_in_dim(kp, start, span, axis=1)
        vb = lax.dynamic_slice_in_dim(vp, start, span, axis=1)
        qpos = start + jnp.arange(Q_BLOCK, dtype=jnp.int32)
        kpos = start - WINDOW + jnp.arange(span, dtype=jnp.int32)
        valid = (jnp.abs(qpos[:, None] - kpos[None, :]) <= WINDOW) & (kpos >= 0)[None, :] & (kpos < N)[None, :]
        s = jnp.where(valid, jnp.einsum('bqhgd,bkhd->bhgqk', qi, kb).astype(jnp.float32), neg)
        s_ctx = jnp.einsum('bqhgd,bkhd->bhgqk', qi, kc).astype(jnp.float32)
        s_sink = jnp.broadcast_to(sink_l[None, :, :, None, None], s.shape[:-1] + (1,))
        p = softmax_f32(jnp.concatenate([s, s_ctx, s_sink], axis=-1), v.dtype)
        return (jnp.einsum('bhgqk,bkhd->bqhgd', p[..., :span], vb)
                + jnp.einsum('bhgqk,bkhd->bqhgd', p[..., span:span + C], vc))

    y = lax.map(block, (jnp.arange(nblk, dtype=jnp.int32), qs))
    y = y.transpose(1, 0, 2, 3, 4, 5).reshape(B, N, N_HEADS_C * HEAD_DIM)
    yc = None
    if need_ctx:
        s = jnp.einsum('bqhgd,bkhd->bhgqk', qc, kc).astype(jnp.float32)
        s_sink = jnp.broadcast_to(sink_l[None, :, :, None, None], s.shape[:-1] + (1,))
        p = softmax_f32(jnp.concatenate([s, s_sink], axis=-1), vc.dtype)
        yc = jnp.einsum('bhgqk,bkhd->bqhgd', p[..., :C], vc).reshape(B, C, N_HEADS_C * HEAD_DIM)
    return y, yc


def layer(x, cx, c_silu, cctx_silu, norm_w, ada_w, ada_b, w_in, w_out, qn, kn, rpb, sink, cos, sin, need_ctx):
    mod = c_silu @ ada_w + ada_b
    mod_c = cctx_silu @ ada_w + ada_b
    shift, scale, gate = jnp.split(mod, 3, axis=-1)
    shift_c, scale_c, gate_c = jnp.split(mod_c, 3, axis=-1)
    hx = rmsnorm(x, norm_w) * (1 + scale[:, None, :]) + shift[:, None, :]
    hc = rmsnorm(cx, norm_w) * (1 + scale_c) + shift_c
    px = split_cols(hx @ w_in)
    pc = split_cols(hc @ w_in)
    ya, yca = global_gqa(px[0], px[1], px[2], pc[0], pc[1], pc[2], cos, sin, qn, kn, need_ctx)
    yb, ycb = neighbourhood_attn(px[3], px[4], px[5], pc[3], pc[4], pc[5], rpb, need_ctx)
    yc_, ycc = window_gqa(px[6], px[7], px[8], pc[6], pc[7], pc[8], cos, sin, sink, need_ctx)
    ux = jnp.concatenate([ya, yb, yc_], axis=-1) * jax.nn.silu(px[9])
    x = x + gate[:, None, :] * (ux @ w_out)
    if need_ctx:
        uc = jnp.concatenate([yca, ycb, ycc], axis=-1) * jax.nn.silu(pc[9])
        cx = cx + gate_c * (uc @ w_out)
    return x, cx


def setup_inputs(seed: int = 0) -> dict:
    key = jax.random.key(seed)
    ks = jax.random.split(key, 14)
    f32 = jnp.float32
    n_rel = (2 * NB_ROWS - 1, 2 * NB_COLS - 1)
    return {
        "x": jax.random.normal(ks[0], (BATCH, SEQ, D_MODEL), f32),
        "c": jax.random.normal(ks[1], (BATCH, D_MODEL), f32),
        "ctx": jax.random.normal(ks[2], (BATCH, CTX_LEN, D_MODEL), f32),
        "c_ctx": jax.random.normal(ks[3], (D_MODEL,), f32),
        "norm_w": 1.0 + 0.05 * jax.random.normal(ks[4], (DEPTH, D_MODEL), f32),
        "ada_w": jax.random.normal(ks[5], (DEPTH, D_MODEL, 3 * D_MODEL), f32) * D_MODEL ** -0.5,
        "ada_b": 0.02 * jax.random.normal(ks[6], (DEPTH, 3 * D_MODEL), f32),
        "w_in": jax.random.normal(ks[7], (DEPTH, D_MODEL, IN_WIDTH), f32) * D_MODEL ** -0.5,
        "w_out": jax.random.normal(ks[8], (DEPTH, MIX_WIDTH, D_MODEL), f32) * MIX_WIDTH ** -0.5,
        "q_norm_a": 1.0 + 0.05 * jax.random.normal(ks[9], (DEPTH, HEAD_DIM), f32),
        "k_norm_a": 1.0 + 0.05 * jax.random.normal(ks[10], (DEPTH, HEAD_DIM), f32),
        "rpb_b": 0.1 * jax.random.normal(ks[11], (DEPTH, N_HEADS_B) + n_rel, f32),
        "sink_c": 0.5 * jax.random.normal(ks[12], (DEPTH, N_HEADS_C), f32),
        "final_norm_w": 1.0 + 0.05 * jax.random.normal(ks[13], (D_MODEL,), f32),
    }


def reference(x, c, ctx, c_ctx, norm_w, ada_w, ada_b, w_in, w_out, q_norm_a, k_norm_a, rpb_b, sink_c, final_norm_w):
    n_tok = x.shape[1]
    cos, sin = axial_rope(n_tok, x.dtype)
    c_silu = jax.nn.silu(c)
    cctx_silu = jax.nn.silu(c_ctx)
    cx = ctx
    for l in range(DEPTH):
        x, cx = layer(x, cx, c_silu, cctx_silu, norm_w[l], ada_w[l], ada_b[l], w_in[l], w_out[l],
                      q_norm_a[l], k_norm_a[l], rpb_b[l], sink_c[l], cos, sin, l < DEPTH - 1)
    return rmsnorm(x, final_norm_w)
```

```python
import numpy as np
import concourse.bass as bass
import concourse.mybir as mybir
from concourse.bass_utils import run_bass_kernel_spmd

F32 = mybir.dt.float32
BF16 = mybir.dt.bfloat16
I32 = mybir.dt.int32
AF = mybir.ActivationFunctionType
ALU = mybir.AluOpType
AX = mybir.AxisListType

D = 1024
SEQ = 8192
NCORE = 8
RPB = 4
TOK = SEQ // RPB
NT = TOK // 128
NTC = NT + 2
NTOK = NTC * 128
EPS = 1e-6
NEG = -32768.0
CCW = 8704
IN_W = 4608
QOFF = {"A": 0, "B": 512, "C": 1024}
KVAC_OFF = 1536
KB_OFF = 2048
VB_OFF = 2560
G_OFF = 3072
MIXI = {"A": 0, "B": 1, "C": 2}

ENG_CHUNK = 30000


class Op:
    __slots__ = ("eng", "fn", "deps", "signal", "seq", "stream", "sid", "ninst", "cc")

    def __init__(self, eng, fn, stream=None, ninst=1, cc=False):
        self.eng = eng
        self.fn = fn
        self.deps = []
        self.signal = False
        self.seq = None
        self.stream = stream
        self.sid = None
        self.ninst = ninst
        self.cc = cc


class Prog:
    ENGS = ("pe", "act", "dve", "pool", "sp")
    SYNC_SAME = ("act", "dve", "pool")

    def __init__(self, nc):
        self.nc = nc
        self.ops = {e: [] for e in self.ENGS}
        self.order = []
        self.lastw = {}
        self.readers = {}
        self.last_on_stream = {}

    def _deps(self, op, reads, writes):
        reads = list(reads)
        writes = list(writes)
        for r in list(reads):
            if isinstance(r, str) and r.startswith("PS_"):
                reads.remove(r)
                if r not in writes:
                    writes.append(r)
        deps = set()
        for r in reads:
            w = self.lastw.get(r)
            if w is not None:
                deps.add(w)
        for w_ in writes:
            w = self.lastw.get(w_)
            if w is not None:
                deps.add(w)
            for rd in self.readers.get(w_, ()):
                deps.add(rd)
        deps.discard(op)
        op.deps = list(deps)
        for r in reads:
            self.readers.setdefault(r, []).append(op)
        for w_ in writes:
            self.lastw[w_] = op
            self.readers[w_] = []

    def op(self, eng, fn, reads=(), writes=()):
        o = Op(eng, fn)
        self._deps(o, reads, writes)
        self.ops[eng].append(o)
        self.order.append(o)
        return o

    def dma(self, queue, fn, reads=(), writes=(), stream=None, ninst=1, cc=False):
        o = Op(queue, fn, stream=stream, ninst=ninst, cc=cc)
        self._deps(o, reads, writes)
        prev = self.last_on_stream.get(stream)
        if prev is not None and prev not in o.deps:
            o.deps.append(prev)
        self.last_on_stream[stream] = o
        self.ops[queue].append(o)
        self.order.append(o)
        return o

    def barrier(self):
        deps = set()
        for r, w in self.lastw.items():
            if w is not None:
                deps.add(w)
        for r, rl in self.readers.items():
            for rd in rl:
                deps.add(rd)
        deps = list(deps)
        for e in self.ENGS:
            o = Op(e, None)
            o.deps = deps
            self.ops[e].append(o)
            self.order.append(o)
        self.lastw = {}
        self.readers = {}

    def emit(self, block):
        nc = self.nc
        semd = {}

        def sems(sid):
            if sid not in semd:
                semd[sid] = nc.alloc_semaphore("s_%s_%d" % sid)
            return semd[sid]

        for o in self.order:
            for d in o.deps:
                if d.stream is not None:
                    d.signal = True
                elif d.eng != o.eng or o.eng in self.SYNC_SAME or o.stream is not None:
                    d.signal = True
        cnt = {e: 0 for e in self.ENGS}
        scnt = {}
        for o in self.order:
            if o.stream is not None:
                if o.cc:
                    scnt[o.stream] = scnt.get(o.stream, 0) + 1
                    assert scnt[o.stream] == 1, "one collective per stream"
                    o.sid = ("dma_" + o.stream, 0)
                    o.seq = 1
                else:
                    scnt[o.stream] = scnt.get(o.stream, 0) + o.ninst
                    o.sid = ("dma_" + o.stream, 0)
                    o.seq = 16 * scnt[o.stream]
            elif o.signal:
                c = cnt[o.eng]
                o.sid = (o.eng, c // ENG_CHUNK)
                o.seq = c % ENG_CHUNK + 1
                cnt[o.eng] = c + 1

        def run_engine(ename, eng):
            known = {}
            for o in self.ops[ename]:
                need = {}
                for d in o.deps:
                    if d.seq is None:
                        continue
                    if (d.stream is None and d.eng == ename and o.stream is None
                            and ename not in self.SYNC_SAME):
                        continue
                    if need.get(d.sid, 0) < d.seq:
                        need[d.sid] = d.seq
                for sid, v in need.items():
                    if known.get(sid, 0) >= v:
                        continue
                    eng.wait_ge(sems(sid), v)
                    known[sid] = v
                if o.fn is None:
                    continue
                inst = o.fn(eng)
                if o.cc:
                    inst.then_inc(sems(o.sid))
                elif o.stream is not None:
                    insts = inst if isinstance(inst, (list, tuple)) else [inst]
                    assert len(insts) == o.ninst
                    for it in insts:
                        it.then_inc(sems(o.sid), 16)
                elif o.signal:
                    inst.then_inc(sems(o.sid), 1)

        @block.tensor
        def _(e):
            run_engine("pe", e)

        @block.scalar
        def _(e):
            run_engine("act", e)

        @block.vector
        def _(e):
            run_engine("dve", e)

        @block.gpsimd
        def _(e):
            run_engine("pool", e)

        @block.sync
        def _(e):
            run_engine("sp", e)


PAIR_HEADS = [0, 4, 1, 5, 2, 6, 3, 7]


def _perm_heads(w, off, nheads=8):
    cols = []
    for h in PAIR_HEADS:
        cols.append(w[..., off + h * 64: off + (h + 1) * 64])
    return np.concatenate(cols, axis=-1)


def _perm_w_in(w):
    oqa, oka, ova = 0, 512, 640
    oqb, okb, ovb = 768, 1280, 1792
    oqc, okc, ovc = 2304, 2816, 2944
    og = 3072
    parts = [
        _perm_heads(w, oqa), _perm_heads(w, oqb), _perm_heads(w, oqc),
        w[:, oka:oka + 128], w[:, okc:okc + 128], w[:, ova:ova + 128], w[:, ovc:ovc + 128],
        _perm_heads(w, okb), _perm_heads(w, ovb),
        _perm_heads(w, og), _perm_heads(w, og + 512), _perm_heads(w, og + 1024),
    ]
    return np.ascontiguousarray(np.concatenate(parts, axis=1))


def _perm_w_out(w):
    rows = []
    for m in range(3):
        for h in PAIR_HEADS:
            rows.append(w[m * 512 + h * 64: m * 512 + (h + 1) * 64, :])
    return np.ascontiguousarray(np.concatenate(rows, axis=0))


def _rope_tables(rank):
    t = (rank * TOK + np.arange(TOK)).astype(np.int32)
    rows = (t // 64).astype(np.float32)
    cols = (t % 64).astype(np.float32)
    nf = 16
    freq = (np.float32(10000.0) ** (-np.arange(nf, dtype=np.float32) / np.float32(nf))).astype(np.float32)
    ang = np.concatenate([rows[:, None] * freq, cols[:, None] * freq], axis=-1).astype(np.float32)
    c = np.cos(ang).astype(np.float32)
    s = np.sin(ang).astype(np.float32)
    c = np.concatenate([c, np.ones((256, 32), np.float32)], axis=0)
    s = np.concatenate([s, np.zeros((256, 32), np.float32)], axis=0)
    C2 = np.concatenate([c, c], axis=1)
    S2 = np.concatenate([-s, s], axis=1)
    tab = np.stack([C2, S2], axis=1)
    tab = tab.reshape(NTC, 128, 2, 64).transpose(1, 0, 2, 3)
    return np.ascontiguousarray(tab)


def _b_valid(i_q, i_k):
    q = np.arange(128)
    k = np.arange(128)
    rq = 2 * i_q + q // 64
    cq = q % 64
    rk = 2 * i_k + k // 64
    ck = k % 64
    rs = np.clip(rq - 4, 0, 120)
    cs = np.clip(cq - 8, 0, 48)
    ok = ((rk[:, None] >= rs[None, :]) & (rk[:, None] < rs[None, :] + 8) &
          (ck[:, None] >= cs[None, :]) & (ck[:, None] < cs[None, :] + 16) &
          (rk[:, None] >= 0) & (rk[:, None] < 128))
    return np.where(ok, np.float32(0.0), np.float32(NEG)).astype(np.float32)


def _b_masks(rank):
    gen = np.full((128, 11, 128), NEG, np.float32)
    for s in range(11):
        dl = 5 - s
        if abs(dl) <= 2:
            gen[:, s, :] = _b_valid(30, 30 + dl)
    first = np.zeros((128, 8, 2, 128), np.float32)
    last = np.zeros((128, 8, 2, 128), np.float32)
    for t in range(8):
        for b in range(2):
            cg = 4 * rank + 0
            first[:, t, b, :] = _b_valid(4 * cg + b, 4 * cg - 2 + t)
            cg = 4 * rank + 3
            last[:, t, b, :] = _b_valid(4 * cg + 2 + b, 4 * cg - 2 + t)
    return gen.reshape(128, 11 * 128), first.reshape(128, 8 * 256), last.reshape(128, 8 * 256)


def _b_bias(rpb_l):
    out = np.zeros((128, 8, 11, 128), np.float32)
    k = np.arange(128)
    q = np.arange(128)
    kr, kc = k // 64, k % 64
    qr, qc = q // 64, q % 64
    for s in range(11):
        dl = 5 - s
        dr = 2 * dl + kr[:, None] - qr[None, :] + 7
        dc = kc[:, None] - qc[None, :] + 15
        ok = (dr >= 0) & (dr <= 14) & (dc >= 0) & (dc <= 30)
        drc = np.clip(dr, 0, 14)
        dcc = np.clip(dc, 0, 30)
        for hi, h in enumerate(PAIR_HEADS):
            vals = rpb_l[h][drc, dcc]
            out[:, hi, s, :] = np.where(ok, vals, np.float32(0.0))
    return np.ascontiguousarray(out.reshape(128, 8, 11 * 128))


def _c_masks(rank):
    k = np.arange(128)[:, None]
    q = np.arange(128)[None, :]
    d1 = np.where(k <= q, 0.0, NEG).astype(np.float32)
    dm1 = np.where(k >= q, 0.0, NEG).astype(np.float32)
    d0 = np.zeros((128, 128), np.float32)
    M = np.full((128, 128), NEG, np.float32)
    gen = np.stack([M, M, M, d1, d0, dm1, M, M, M], axis=1).reshape(128, 9 * 128)
    first = np.stack([dm1 if rank > 0 else M, M, M, M], axis=1).reshape(128, 512)
    last = np.stack([M, M, M, d1 if rank < RPB - 1 else M], axis=1).reshape(128, 512)
    return gen, first, last


def build_program(layers=(0, 1), final_norm=True, dbg=None, stop_after=None):
    nc = bass.Bass("TRN2", target_bir_lowering=False)
    P = Prog(nc)
    dbg = dbg or ()

    def din(name, shape, dt=F32):
        return nc.dram_tensor(name, list(shape), dt, kind="ExternalInput").ap()

    xin = din("xin", [NTOK, D])
    cvec = din("cvec", [128, 16])
    rope_d = din("rope", [128, NTC * 2 * 64])
    selv_d = din("selv", [128, 8])
    fnw_d = din("fnw", [1, D])
    ident_d = din("ident", [128, 128])
    mzb_d = din("mzb", [128, 11 * 128])
    mvbf_d = din("mvbf", [128, 2048])
    mvbl_d = din("mvbl", [128, 2048])
    mzc_d = din("mzc", [128, 9 * 128])
    mvcf_d = din("mvcf", [128, 512])
    mvcl_d = din("mvcl", [128, 512])
    LW = {}
    for l in layers:
        LW[l] = dict(
            w_in=din("w_in%d" % l, [D, IN_W]),
            w_out=din("w_out%d" % l, [1536, D]),
            ada_w=din("ada_w%d" % l, [D, 3 * D]),
            adabT=din("adabT%d" % l, [128, 16]),
            adabg=din("adabg%d" % l, [1, D]),
            normT=din("normT%d" % l, [128, 8]),
            qn=din("qn%d" % l, [1, 64]),
            kn=din("kn%d" % l, [1, 64]),
            bzb=din("bzb%d" % l, [128, 8 * 11 * 128]),
            sink=din("sink%d" % l, [128, 4]),
        )
    out_d = nc.dram_tensor("out", [TOK, D], F32, kind="ExternalOutput").ap()
    dbg_t = {}
    for name, shape, dt in dbg:
        dbg_t[name] = nc.dram_tensor("dbg_" + name, list(shape), dt, kind="ExternalOutput").ap()

    x1_t = nc.dram_tensor("x1_scr", [NTOK, D], F32)
    CCP = {"A": (0, 4096), "F": (4096, 6400), "L": (6400, 8704)}
    cc_ins = {k: nc.dram_tensor("cc_in%s" % k, [128, c1 - c0], BF16) for k, (c0, c1) in CCP.items()}
    cc_outs = {l: {k: nc.dram_tensor("cc_out%d%s" % (l, k), [RPB * 128, c1 - c0], BF16) for k, (c0, c1) in CCP.items()}
               for l in layers}
    ut_t = nc.dram_tensor("ut_scr", [NTC, 128, 12, 128], BF16)

    def sb(name, shape, dt):
        return nc.alloc_sbuf_tensor(name, list(shape), dt)

    hxT = sb("hxT", [128, 8, NTOK], BF16)
    hx_f = hxT.bitcast(F32)
    hx_f2 = hx_f[:, :, :].rearrange("p a b -> p (a b)")
    gate_bc = [hx_f2[:, 0:1024], hx_f2[:, 1024:2048]]
    adabg = hx_f2[:, 2048:3072]
    fnw_sb = hx_f2[:, 3072:4096]
    wst = [sb("wst%d" % i, [128, 8, 512], BF16) for i in range(2)]
    wg = [sb("wg%d" % i, [128, 8, 128], BF16) for i in range(2)]
    rope = sb("rope_sb", [128, NTC, 2, 64], F32)
    ident_f = sb("ident_f", [128, 128], F32)
    ident_b = sb("ident_b", [128, 128], BF16)
    ones_b = sb("ones_b", [128, 128], BF16)
    cs_f = sb("cs_f", [128, 16], F32)
    cs_e = sb("cs_e", [128, 16], F32)
    cs_b = sb("cs_b", [128, 8, 2], BF16)
    modT = sb("modT", [128, 16, 2], F32)
    adabT = sb("adabT", [128, 16], F32)
    normT = sb("normT", [128, 8], F32)
    A1T = sb("A1T", [128, 8, 2], F32)
    qn_bc = sb("qn_bc", [128, 64], F32)
    kn_bc = sb("kn_bc", [128, 64], F32)
    ss = sb("ss", [128, NTC], F32)
    rstd = sb("rstd", [128, NTC], F32)
    selv = sb("selv_sb", [128, 8], F32)
    kTC = sb("kTC", [128, 20 * 128], BF16)
    vC = sb("vC", [128, 20, 128], BF16)
    kTA_ctx = sb("kTA_ctx", [128, 256], BF16)
    vA_ctx = sb("vA_ctx", [128, 2, 128], BF16)
    esink = sb("esink", [128, 4], F32)
    sm4 = sb("sm4", [128, 16], F32)
    eps_c = sb("eps_c", [128, 1], F32)
    sm4x = sb("sm4x", [128, 48], F32)
    KVR = sb("KVR", [128, 22528], BF16)
    kTB = KVR[:, 0:11264].rearrange("p (j n) -> p j n", j=4)
    vB = KVR[:, 11264:22528].rearrange("p (t n) -> p t n", t=22)
    kTA = KVR[:, 0:64 * 128]
    vA = KVR[:, 8192:8192 + 64 * 128].rearrange("p (t n) -> p t n", t=64)
    ARN = sb("ARN", [128, 26624], BF16)
    qTm = ARN[:, 0:4 * NTOK].rearrange("p (j n) -> p j n", j=4)
    TB0 = 4 * NTOK
    cc_st = ARN[:, TB0:TB0 + CCW]
    stg = [ARN[:, TB0 + k * 2304:TB0 + (k + 1) * 2304] for k in range(3)] + [ARN[:, 17920 + k * 2304:17920 + (k + 1) * 2304] for k in range(3)]
    bzb = ARN[:, TB0:TB0 + 8 * 1408].rearrange("p (h n) -> p h n", h=8)
    mzb = ARN[:, TB0 + 11264:TB0 + 11264 + 1408]
    mvbf = ARN[:, TB0 + 12672:TB0 + 12672 + 2048].rearrange("p (t n) -> p t n", t=8)
    mvbl = ARN[:, TB0 + 14720:TB0 + 14720 + 2048].rearrange("p (t n) -> p t n", t=8)
    mzc = ARN[:, TB0:TB0 + 1152]
    mvcf = ARN[:, TB0 + 1152:TB0 + 1664]
    mvcl = ARN[:, TB0 + 1664:TB0 + 2176]
    wout_sb = ARN[:, TB0:TB0 + 12 * 1024].rearrange("p (j n) -> p j n", j=12)
    utt = [ARN[:, i * 1536:(i + 1) * 1536].rearrange("p (j n) -> p j n", j=12) for i in range(2)]
    X8 = sb("X8", [128, 2048], F32)
    X8b = X8.bitcast(BF16)
    xt = [X8[:, 0:1024], X8[:, 1024:2048]]
    ge = X8[:, 0:512]
    gg = X8[:, 512:1024]
    rr = X8[:, 1024:1536]
    uTo = [X8b[:, 3072:3584], X8b[:, 3584:4096]]
    GB = sb("GB", [128, 2048], F32)
    gg2 = [GB[:, 0:512], GB[:, 512:1024]]
    ge2 = [GB[:, 1024:1536], GB[:, 1536:2048]]
    XN = sb("XN", [128, 2048], F32)
    xn = [XN[:, 0:1024], XN[:, 1024:2048]]
    wk = [XN[:, i * 512:(i + 1) * 512] for i in range(4)]
    TMB = sb("TMB", [128, 1024], BF16)
    tm_b = [TMB[:, 0:512], TMB[:, 512:1024]]
    junk = TMB[:, :]
    PTB = sb("PTB", [128, 2048], BF16)
    PT = [PTB[:, i * 1024:(i + 1) * 1024].rearrange("p (h n) -> p h n", h=2) for i in range(2)]
    cs_rep = PTB[:, :].rearrange("p (a c) -> p a c", a=16)

    PS_S = [nc.alloc_psum_tensor("ps_s%d" % i, [128, 1024], F32) for i in range(2)]
    PS_T = nc.alloc_psum_tensor("ps_t", [128, 512], F32)
    PS_SM = nc.alloc_psum_tensor("ps_sm", [128, 512], F32)
    PS_G = nc.alloc_psum_tensor("ps_g", [128, 512], F32)
    PS_X = nc.alloc_psum_tensor("ps_x", [128, 512], F32)
    PS_Xb = PS_X.bitcast(BF16)

    LOOK = 3
    PRE = {}
    ctr = {"wst": 0, "wg": 0, "acc": 0, "tm": 0, "pt": 0, "uto": 0, "wg_cur": 0, "att": 0, "tr": 0, "ts": 0}

    def load_w(c0, wsrc, slot_kind="wst", ncols=512):
        if slot_kind == "wst":
            i = ctr["wst"] % 2
            ctr["wst"] += 1
            dst = wst[i]
        else:
            i = ctr["wg"] % 2
            ctr["wg"] += 1
            dst = wg[i]
        res = (slot_kind, i)
        srcv = wsrc.rearrange("(kc p) n -> p kc n", p=128)[:, :, c0:c0 + ncols]
        P.dma("pool", lambda e, d=dst, s=srcv, n=ncols: e.dma_start(out=d[:, :, 0:n], in_=s),
              writes=[res], stream="%s%d" % (slot_kind, i))
        return dst, res

    def setup():
        P.dma("sp", lambda e: e.dma_start(out=ident_f[:, :], in_=ident_d), writes=["ident_f"], stream="misc3")
        P.op("pool", lambda e: e.tensor_copy(ident_b[:, :], ident_f[:, :]), reads=["ident_f"], writes=["ident_b"])
        P.op("pool", lambda e: e.memset(ones_b[:, :], 1.0), writes=["ones_b"])
        P.op("pool", lambda e: e.memset(eps_c[:, :], EPS), writes=["eps_c"])
        P.dma("sp", lambda e: e.dma_start(out=cs_f[:, :], in_=cvec), writes=["cs_f"], stream="misc0")
        P.dma("sp", lambda e: e.dma_start(out=rope[:, :, :, :].rearrange("p a b c -> p (a b c)"), in_=rope_d),
              writes=["rope"], stream="misc1")
        P.dma("sp", lambda e: e.dma_start(out=selv[:, :], in_=selv_d), writes=["selv"], stream="misc2")
        P.op("act", lambda e: e.activation(cs_e[:, :], cs_f[:, :], AF.Exp, scale=-1.0), reads=["cs_f"], writes=["cs_e"])
        P.op("dve", lambda e: e.tensor_scalar_add(cs_e[:, :], cs_e[:, :], 1.0), reads=["cs_e"], writes=["cs_e"])
        P.op("dve", lambda e: e.reciprocal(cs_e[:, :], cs_e[:, :]), reads=["cs_e"], writes=["cs_e"])
        P.op("dve", lambda e: e.tensor_tensor(cs_b[:, :, :].rearrange("p a b -> p (a b)"), cs_f[:, :], cs_e[:, :], ALU.mult),
             reads=["cs_e", "cs_f"], writes=["cs_b"])

    def phase_mod(l):
        W = LW[l]
        P.dma("sp", lambda e: e.dma_start(out=adabT[:, :], in_=W["adabT"]), writes=["adabT"], stream="misc0")
        P.dma("sp", lambda e: e.dma_start(out=normT[:, :], in_=W["normT"]), writes=["normT"], stream="misc1")
        P.dma("sp", lambda e: e.dma_start(out=qn_bc[:, :], in_=W["qn"].partition_broadcast(128)),
              writes=["qn_bc"], stream="misc3")
        P.dma("sp", lambda e: e.dma_start(out=kn_bc[:, :], in_=W["kn"].partition_broadcast(128)),
              writes=["kn_bc"], stream="misc4")
        for blk in range(4):
            wsb, wres = load_w(blk * 512, W["ada_w"])

            def f(e, wsb=wsb, blk=blk):
                inst = None
                for sub in range(4):
                    g = blk * 4 + sub
                    for kc in range(8):
                        inst = e.matmul(PS_X[:, g * 2:g * 2 + 2], lhsT=wsb[:, kc, sub * 128:(sub + 1) * 128],
                                        rhs=cs_b[:, kc, :], start=(kc == 0), stop=(kc == 7))
                return inst
            P.op("pe", f, reads=[wres, "cs_b"], writes=["PS_X"])
        P.op("dve", lambda e: e.tensor_tensor(modT[:, :, :], PS_X[:, 0:32].rearrange("p (g w) -> p g w", w=2),
                                              adabT[:, :].unsqueeze(2).broadcast_to([128, 16, 2]), ALU.add),
             reads=["PS_X", "adabT"], writes=["modT"])
        P.op("dve", lambda e: e.scalar_tensor_tensor(A1T[:, :, :], modT[:, 8:16, :], 1.0,
                                                     normT[:, :].unsqueeze(2).broadcast_to([128, 8, 2]),
                                                     ALU.add, ALU.mult),
             reads=["modT", "normT"], writes=["A1T"])

    def prefetch_kvac(l):
        PRE["kvac"] = load_w(KVAC_OFF, LW[l]["w_in"])

    def phase_gate(l, last):
        W = LW[l]
        P.dma("sp", lambda e: e.dma_start(out=adabg, in_=W["adabg"].partition_broadcast(128)),
              writes=["adabg"], stream="misc2")
        if last:
            P.dma("sp", lambda e: e.dma_start(out=fnw_sb, in_=fnw_d.partition_broadcast(128)), writes=["fnw"], stream="misc4")
        P.op("dve", lambda e: e.tensor_copy(cs_rep, cs_b[:, :, :].rearrange("p a b -> p (a b)").unsqueeze(2).broadcast_to([128, 16, 128])),
             reads=["cs_b"], writes=["cs_rep"])
        for blk in (4, 5):
            wsb, wres = PRE.pop("gate%d" % blk) if ("gate%d" % blk) in PRE else load_w(blk * 512, W["ada_w"])

            def f(e, wsb=wsb, blk=blk):
                inst = None
                for which in range(2):
                    for kc in range(8):
                        inst = e.matmul(PS_S[which][:, (blk - 4) * 512:(blk - 3) * 512], lhsT=cs_rep[:, kc * 2 + which, :],
                                        rhs=wsb[:, kc, :], start=(kc == 0), stop=(kc == 7))
                return inst
            P.op("pe", f, reads=[wres, "cs_rep"], writes=["PS_S0", "PS_S1"])
        for which in range(2):
            P.op("dve", lambda e, which=which: e.tensor_tensor(gate_bc[which], PS_S[which][:, :], adabg, ALU.add),
                 reads=["PS_S%d" % which, "adabg"], writes=["gate_bc%d" % which])

    def phase_norm(l, src):
        P.op("dve", lambda e: e.memset(ss[:, :], 0.0), writes=[("ss", t) for t in range(NTC)])
        for t in range(NTC):
            i = t % 2
            w = 0 if t < NT else 1
            P.dma("sp", lambda e, t=t, i=i: e.dma_start(out=xt[i], in_=src[t * 128:(t + 1) * 128, :]),
                  writes=[("xt", i)], stream="xt%d" % i)
            P.op("act", lambda e, t=t, i=i: e.activation(junk, xt[i], AF.Square, scale=1.0 / 32.0,
                                                        accum_out=ss[:, t:t + 1]),
                 reads=[("xt", i)], writes=["junk", ("ss", t)])
            P.op("act", lambda e, t=t: e.activation(rstd[:, t:t + 1], ss[:, t:t + 1], AF.Ln, bias=eps_c[:, 0:1]),
                 reads=[("ss", t)], writes=[("rstd", t)])
            P.op("act", lambda e, t=t: e.activation(rstd[:, t:t + 1], rstd[:, t:t + 1], AF.Exp, scale=-0.5),
                 reads=[("rstd", t)], writes=[("rstd", t)])
            P.op("act", lambda e, t=t, i=i: e.activation(xn[i], xt[i], AF.Copy, scale=rstd[:, t:t + 1]),
                 reads=[("xt", i), ("rstd", t)], writes=[("xn", i)])

            def ftr(e, i=i):
                inst = None
                for kc in range(8):
                    inst = e.transpose(PS_S[i][:, kc * 128:(kc + 1) * 128], xn[i][:, kc * 128:(kc + 1) * 128], ident_f[:, :])
                return inst
            P.op("pe", ftr, reads=[("xn", i), "ident_f"], writes=["PS_S%d" % i])
            P.op("dve", lambda e, i=i, w=w: e.tensor_tensor(
                xn[i].rearrange("p (k n) -> p k n", k=8), PS_S[i][:, :].rearrange("p (k n) -> p k n", k=8),
                A1T[:, :, w:w + 1].broadcast_to([128, 8, 128]), ALU.mult),
                reads=["PS_S%d" % i, "A1T"], writes=[("xn", i)])
            P.op("pool", lambda e, i=i, w=w, t=t: e.tensor_tensor(
                hxT[:, :, t * 128:(t + 1) * 128], xn[i].rearrange("p (k n) -> p k n", k=8),
                modT[:, 0:8, w:w + 1].broadcast_to([128, 8, 128]), ALU.add),
                reads=[("xn", i), "modT"], writes=[("hxT", t)])

    def proj_mm(t, wsb, wres, ps, psres):
        def f(e):
            inst = None
            for kc in range(8):
                inst = e.matmul(ps, lhsT=hxT[:, kc, t * 128:(t + 1) * 128], rhs=wsb[:, kc, :],
                                start=(kc == 0), stop=(kc == 7))
            return inst
        P.op("pe", f, reads=[wres, ("hxT", t)], writes=[psres])

    def rope_ops(xsrc, xres, nh, t, dst, dstres, tt_, ttn, uu_, uun, eng_a="pool", eng_b="dve"):
        C2 = rope[:, t, 0, :].unsqueeze(1).broadcast_to([128, nh, 64])
        S2a = rope[:, t, 1, 0:32].unsqueeze(1).broadcast_to([128, nh, 32])
        S2b = rope[:, t, 1, 32:64].unsqueeze(1).broadcast_to([128, nh, 32])
        x3 = xsrc.rearrange("p (h d) -> p h d", d=64)
        t3 = tt_[:, 0:nh * 64].rearrange("p (h d) -> p h d", d=64)
        u3 = uu_[:, 0:nh * 64].rearrange("p (h d) -> p h d", d=64)
        d3 = dst.rearrange("p (h d) -> p h d", d=64)
        xres = list(xres)
        P.op(eng_a, lambda e: e.tensor_tensor(t3, x3, C2, ALU.mult), reads=xres + ["rope"], writes=[ttn])
        P.op(eng_b, lambda e: e.tensor_tensor(u3[:, :, 0:32], x3[:, :, 32:64], S2a, ALU.mult), reads=xres + ["rope"], writes=[uun])
        P.op(eng_b, lambda e: e.tensor_tensor(u3[:, :, 32:64], x3[:, :, 0:32], S2b, ALU.mult), reads=xres + ["rope", uun], writes=[uun])
        P.op(eng_a, lambda e: e.tensor_tensor(d3, t3, u3, ALU.add), reads=[ttn, uun], writes=[dstres])

    def rms_heads(ps, psres, nh, wbc, wbcres, dst, dstres, scale, tt_, ttn, st_, stn):
        n = nh * 64
        P.op("act", lambda e: e.activation(tt_[:, 0:n], ps, AF.Square), reads=[psres], writes=[ttn])
        P.op("dve", lambda e: e.tensor_reduce(st_[:, 0:nh], tt_[:, 0:n].rearrange("p (h d) -> p h d", d=64), AX.X, ALU.add),
             reads=[ttn], writes=[stn])
        P.op("act", lambda e: e.activation(st_[:, 0:nh], st_[:, 0:nh], AF.Ln, bias=eps_c[:, 0:1], scale=1.0 / 64.0),
             reads=[stn], writes=[stn])
        P.op("act", lambda e: e.activation(st_[:, 0:nh], st_[:, 0:nh], AF.Exp, scale=-0.5),
             reads=[stn], writes=[stn])
        d3 = dst.rearrange("p (h d) -> p h d", d=64)
        P.op("dve", lambda e: e.tensor_tensor(d3, ps.rearrange("p (h d) -> p h d", d=64),
                                              st_[:, 0:nh].unsqueeze(2).broadcast_to([128, nh, 64]), ALU.mult),
             reads=[psres, stn], writes=[dstres])
        P.op("dve", lambda e: e.scalar_tensor_tensor(d3, d3, scale, wbc[:, :].unsqueeze(1).broadcast_to([128, nh, 64]),
                                                      ALU.mult, ALU.mult),
             reads=[dstres, wbcres], writes=[dstres])

    ACCS = [(PS_S[0][:, 0:512], "PS_S0a"), (PS_S[1][:, 0:512], "PS_S1a"), (PS_S[0][:, 512:1024], "PS_S0b"),
            (PS_S[1][:, 512:1024], "PS_S1b"), (PS_T[:, :], "PS_T"), (PS_SM[:, :], "PS_SM")]
    TRS = [(PS_X.bitcast(BF16), "PS_X"), (PS_G.bitcast(BF16), "PS_G")]
    TSETS = [
        (wk[0], "wk0", wk[2], "wk2", wk[3], "wk3"),
        (gg2[0], ("gg", 0), gg2[1], ("gg", 1), ge2[0], ("ge", 0)),
        (ge2[1], ("ge", 1), wk[1], "wk1", rr, "rr"),
    ]
    TMS = [(tm_b[0], ("tm", 0)), (tm_b[1], ("tm", 1)), (uTo[0], ("uTo", 0))]

    def next_acc():
        i = ctr["acc"] % len(ACCS)
        ctr["acc"] += 1
        return ACCS[i]

    def next_tr():
        i = ctr["tr"] % 2
        ctr["tr"] += 1
        return TRS[i]

    def next_tset():
        i = ctr["ts"] % 3
        ctr["ts"] += 1
        a, an, t_, tn, u, un = TSETS[i]
        return a, an, t_, tn, u, un, sm4x[:, i * 16:(i + 1) * 16], ("sm4", i)

    def next_tm():
        i = ctr["tm"] % 3
        ctr["tm"] += 1
        return TMS[i]

    def phase_kvproj(l):
        W = LW[l]
        wsb, wres = PRE.pop("kvac") if "kvac" in PRE else load_w(KVAC_OFF, W["w_in"])
        accq = {}
        for t in range(min(LOOK, NTC)):
            accq[t] = next_acc()
            proj_mm(t, wsb, wres, accq[t][0], accq[t][1])

        def front1(t):
            ps, psres = accq.pop(t)
            if t + LOOK < NTC:
                accq[t + LOOK] = next_acc()
                proj_mm(t + LOOK, wsb, wres, accq[t + LOOK][0], accq[t + LOOK][1])
            a_, an, t_, tn, u_, un, st_, stn = next_tset()
            if t < NT:
                vdst = cc_st[:, 2048 + t * 128:2048 + (t + 1) * 128]
                vres = ("cc_st", "vA", t)
                cslot = 1 + t
            else:
                vdst = vA_ctx[:, t - NT, :]
                vres = ("vA_ctx", t)
                cslot = 18 + (t - NT)
            P.op("act", lambda e, ps=ps, vdst=vdst: e.copy(vdst, ps[:, 256:384]), reads=[psres], writes=[vres])
            P.op("act", lambda e, ps=ps, cslot=cslot: e.copy(vC[:, cslot, :], ps[:, 384:512]), reads=[psres], writes=[("vC", cslot)])
            rms_heads(ps[:, 0:128], psres, 2, kn_bc, "kn_bc", a_[:, 0:128], an, 1.0, t_, tn, st_, stn)
            P.op("act", lambda e, ps=ps, a_=a_: e.copy(a_[:, 128:256], ps[:, 128:256]), reads=[psres, an], writes=[an])
            tmb, tmn = next_tm()
            rope_ops(a_[:, 0:256], [an], 4, t, tmb[:, 0:256], tmn, t_, tn, u_, un)
            return tmb, tmn, cslot

        def back1(t, tmb, tmn, cslot):
            trb, trn = next_tr()

            def ftr(e, tmb=tmb, trb=trb):
                e.transpose(trb[:, 0:128], tmb[:, 0:128], ident_b[:, :])
                return e.transpose(trb[:, 128:256], tmb[:, 128:256], ident_b[:, :])
            P.op("pe", ftr, reads=[tmn, "ident_b"], writes=[trn])
            if t < NT:
                kdst = cc_st[:, t * 128:(t + 1) * 128]
                kres = ("cc_st", "kA", t)
            else:
                kdst = kTA_ctx[:, (t - NT) * 128:(t - NT + 1) * 128]
                kres = ("kTA_ctx", t)
            P.op("act", lambda e, kdst=kdst, trb=trb: e.copy(kdst, trb[:, 0:128]), reads=[trn], writes=[kres])
            P.op("dve", lambda e, cslot=cslot, trb=trb: e.tensor_copy(kTC[:, cslot * 128:(cslot + 1) * 128], trb[:, 128:256]),
                 reads=[trn], writes=[("kTC", cslot)])
        f1 = {0: front1(0)}
        for t in range(NTC):
            if t + 1 < NTC:
                f1[t + 1] = front1(t + 1)
            back1(t, *f1.pop(t))
        wsb, wres = load_w(KB_OFF, W["w_in"])
        accq = {}
        for t in range(min(LOOK, NTC)):
            accq[t] = next_acc()
            proj_mm(t, wsb, wres, accq[t][0], accq[t][1])

        def front2(t):
            ps, psres = accq.pop(t)
            if t + LOOK < NTC:
                accq[t + LOOK] = next_acc()
                proj_mm(t + LOOK, wsb, wres, accq[t + LOOK][0], accq[t + LOOK][1])
            tmb, tmn = next_tm()
            if t % 2 == 0:
                P.op("act", lambda e, ps=ps, tmb=tmb: e.copy(tmb, ps), reads=[psres], writes=[tmn])
            else:
                P.op("dve", lambda e, ps=ps, tmb=tmb: e.tensor_copy(tmb, ps), reads=[psres], writes=[tmn])
            return tmb, tmn

        def back2(t, tmb, tmn):
            slot = 2 + t if t < NT else 20 + (t - NT)
            trb, trn = next_tr()

            def ftr(e, tmb=tmb, trb=trb):
                inst = None
                for j in range(4):
                    inst = e.transpose(trb[:, j * 128:(j + 1) * 128], tmb[:, j * 128:(j + 1) * 128], ident_b[:, :])
                return inst
            P.op("pe", ftr, reads=[tmn, "ident_b"], writes=[trn])
            if t % 2 == 0:
                P.op("dve", lambda e, slot=slot, trb=trb: e.tensor_copy(kTB[:, :, slot * 128:(slot + 1) * 128],
                                                                       trb[:, 0:512].rearrange("p (j n) -> p j n", j=4)),
                     reads=[trn], writes=[("kTB", slot)])
            else:
                P.op("act", lambda e, slot=slot, trb=trb: e.copy(kTB[:, :, slot * 128:(slot + 1) * 128],
                                                                trb[:, 0:512].rearrange("p (j n) -> p j n", j=4)),
                     reads=[trn], writes=[("kTB", slot)])
        f2 = {0: front2(0)}
        for t in range(NTC):
            if t + 1 < NTC:
                f2[t + 1] = front2(t + 1)
            back2(t, *f2.pop(t))
        wsb, wres = load_w(VB_OFF, W["w_in"])
        for t in range(NTC):
            ps, psres = next_acc()
            proj_mm(t, wsb, wres, ps, psres)
            slot = 2 + t if t < NT else 20 + (t - NT)
            if t % 2 == 0:
                P.op("act", lambda e, ps=ps, slot=slot: e.copy(vB[:, slot, :], ps), reads=[psres], writes=[("vB", slot)])
            else:
                P.op("dve", lambda e, ps=ps, slot=slot: e.tensor_copy(vB[:, slot, :], ps), reads=[psres], writes=[("vB", slot)])

    def phase_exchange(l):
        cc_out = cc_outs[l]
        PRE["qB"] = load_w(QOFF["B"], LW[l]["w_in"])
        ecp = [
            (cc_st[:, 4096:5120].rearrange("p (j n) -> p j n", j=4), kTB[:, :, 256:512], [("kTB", 2), ("kTB", 3)], "kBf"),
            (cc_st[:, 5120:6144].rearrange("p (t n) -> p t n", t=2), vB[:, 2:4, :], [("vB", 2), ("vB", 3)], "vBf"),
            (cc_st[:, 6144:6272], kTC[:, 128:256], [("kTC", 1)], "kCf"),
            (cc_st[:, 6272:6400], vC[:, 1, :], [("vC", 1)], "vCf"),
            (cc_st[:, 6400:7424].rearrange("p (j n) -> p j n", j=4), kTB[:, :, 2048:2304], [("kTB", 16), ("kTB", 17)], "kBl"),
            (cc_st[:, 7424:8448].rearrange("p (t n) -> p t n", t=2), vB[:, 16:18, :], [("vB", 16), ("vB", 17)], "vBl"),
            (cc_st[:, 8448:8576], kTC[:, 16 * 128:17 * 128], [("kTC", 16)], "kCl"),
            (cc_st[:, 8576:8704], vC[:, 16, :], [("vC", 16)], "vCl"),
        ]
        for (o_, i_, rd, nm) in ecp:
            P.op("pool", lambda e, o_=o_, i_=i_: e.tensor_copy(o_, i_), reads=rd, writes=[("cc_st", nm)])
        allcc = ([("cc_st", "kA", t) for t in range(NT)] + [("cc_st", "vA", t) for t in range(NT)] +
                 [("cc_st", e_[3]) for e_ in ecp])
        ccdeps = {"A": [("cc_st", "kA", t) for t in range(NT)] + [("cc_st", "vA", t) for t in range(NT)],
                  "F": [("cc_st", e_[3]) for e_ in ecp[0:4]], "L": [("cc_st", e_[3]) for e_ in ecp[4:8]]}
        for kk, (c0, c1) in CCP.items():
            P.dma("sp", lambda e, kk=kk, c0=c0, c1=c1: e.dma_start(out=cc_ins[kk][:, :], in_=cc_st[:, c0:c1]),
                  reads=ccdeps[kk] + ["tabregion"], writes=["cc_in" + kk], stream="ccin" + kk)
            P.dma("pool", lambda e, kk=kk: e.collective_compute("AllGather", ALU.bypass, replica_groups=[[0, 1, 2, 3], [4, 5, 6, 7]],
                                                                 ins=[cc_ins[kk].ap().opt()], outs=[cc_out[kk].ap().opt()]),
                  reads=["cc_in" + kk], writes=["cc_out" + kk], stream="cc%d%s" % (l, kk), cc=True)
        for k in range(3):
            P.dma("sp", lambda e, k=k: e.dma_start(out=stg[k], in_=cc_out["L"][k * 128:(k + 1) * 128, :]),
                  reads=["cc_outL", "cc_inA", "cc_inF", "cc_inL"], writes=[("stg", k)], stream="stg%d" % k)
            P.dma("sp", lambda e, k=k: e.dma_start(out=stg[3 + k], in_=cc_out["F"][(k + 1) * 128:(k + 2) * 128, :]),
                  reads=["cc_outF"], writes=[("stg", 3 + k)], stream="stg%d" % (3 + k))
        def pieces(base):
            return [
                (lambda a: a[:, 0:1024].rearrange("p (j n) -> p j n", j=4)),
                (lambda a: a[:, 1024:2048].rearrange("p (t n) -> p t n", t=2)),
                (lambda a: a[:, 2048:2176]),
                (lambda a: a[:, 2176:2304]),
            ]
        dsts_prev = [(kTB[:, :, 0:256], [("kTB", 0), ("kTB", 1)]), (vB[:, 0:2, :], [("vB", 0), ("vB", 1)]),
                     (kTC[:, 0:128], [("kTC", 0)]), (vC[:, 0, :], [("vC", 0)])]
        dsts_next = [(kTB[:, :, 2304:2560], [("kTB", 18), ("kTB", 19)]), (vB[:, 18:20, :], [("vB", 18), ("vB", 19)]),
                     (kTC[:, 17 * 128:18 * 128], [("kTC", 17)]), (vC[:, 17, :], [("vC", 17)])]
        for which, dsts in ((0, dsts_prev), (1, dsts_next)):
            for pi_, (dst, wr) in enumerate(dsts):
                view = pieces(0)[pi_]
                for k in range(3):
                    src_ = view(stg[which * 3 + k])
                    sc = selv[:, which * 3 + k:which * 3 + k + 1]
                    if k == 0:
                        P.op("dve", lambda e, dst=dst, src_=src_, sc=sc: e.tensor_scalar(dst, src_, sc, None, ALU.mult),
                             reads=[("stg", which * 3 + k), "selv"], writes=wr + ["halo_st"])
                    else:
                        P.op("dve", lambda e, dst=dst, src_=src_, sc=sc: e.scalar_tensor_tensor(dst, src_, sc, dst, ALU.mult, ALU.add),
                             reads=[("stg", which * 3 + k), "selv"] + wr, writes=wr + ["halo_st"])

    def attend(l, m, pair, q0, N, tiles, sinkcol=None):
        gi = MIXI[m] * 4 + pair
        nt = len(tiles)
        ak = ctr["att"]
        ctr["att"] += 1
        Tps, tres = (PS_T, "PS_T") if ak % 2 == 0 else (PS_X, "PS_X")
        ggk, gek = gg2[ak % 2], ge2[ak % 2]
        gres, eres = ("gg", ak % 2), ("ge", ak % 2)
        wgi = ctr["wg_cur"]

        def fg(e):
            inst = None
            for kc in range(8):
                inst = e.matmul(PS_G[:, 0:N], lhsT=wg[wgi][:, kc, :], rhs=hxT[:, kc, q0:q0 + N], start=(kc == 0), stop=(kc == 7))
            return inst
        P.op("pe", fg, reads=[("wg", wgi)] + [("hxT", t) for t in range(q0 // 128, (q0 + N) // 128)], writes=["PS_G"])
        P.op("act", lambda e: e.activation(gek[:, 0:N], PS_G[:, 0:N], AF.Exp, scale=-1.0), reads=["PS_G"], writes=[eres])
        P.op("dve", lambda e: e.tensor_scalar_add(gek[:, 0:N], gek[:, 0:N], 1.0), reads=[eres], writes=[eres])
        P.op("dve", lambda e: e.reciprocal(gek[:, 0:N], gek[:, 0:N]), reads=[eres], writes=[eres])
        P.op("dve", lambda e: e.tensor_tensor(ggk[:, 0:N], PS_G[:, 0:N], gek[:, 0:N], ALU.mult), reads=["PS_G", eres], writes=[gres])

        sbuf_i = []
        partial = any(("cols" in kt_) for kt_ in tiles)
        if partial:
            full = [kt_ for kt_ in tiles if "cols" not in kt_]
            rest = [kt_ for kt_ in tiles if "cols" in kt_]
            tiles = full[:1] + rest + full[1:]

        def emit_qk(i):
            kt = tiles[i]
            si = ctr["acc"] % 2
            ctr["acc"] += 1
            S = PS_S[si]
            sres = "PS_S%d" % si
            adds = kt.get("adds", [])

            adds = kt.get("adds", [])

            cl, ch = kt.get("cols", (0, N))

            def fqk(e, S=S, kt=kt, adds=adds, cl=cl, ch=ch):
                na = len(adds)
                e.matmul(S[:, cl:ch], lhsT=kt["kT"][0:64, :], rhs=qTm[0:64, pair, q0 + cl:q0 + ch], start=True, stop=(na == 0))
                inst = e.matmul(S[:, 512 + cl:512 + ch], lhsT=kt["kT"][64:128, :], rhs=qTm[64:128, pair, q0 + cl:q0 + ch],
                                start=True, stop=(na == 0))
                for ai, (c0, ncol, rx, ry, _r) in enumerate(adds):
                    lastf = (ai == na - 1)
                    e.matmul(S[:, c0:c0 + ncol], lhsT=ident_b[:, :], rhs=rx, start=False, stop=lastf)
                    inst = e.matmul(S[:, 512 + c0:512 + c0 + ncol], lhsT=ident_b[:, :], rhs=ry, start=False, stop=lastf)
                return inst
            rds = [kt["kres"], ("qTm", pair), "ident_b"] + [a[4] for a in adds]
            P.op("pe", fqk, reads=rds, writes=[sres])
            sbuf_i.append((S, sres))

        emit_qk(0)
        for i, kt in enumerate(tiles):
            if i + 1 < nt:
                emit_qk(i + 1)
            S, sres = sbuf_i[i]
            pi = ctr["pt"] % 2
            ctr["pt"] += 1
            Pt = PT[pi]
            cl, ch = kt.get("cols", (0, N))
            P.op("act", lambda e, S=S, Pt=Pt, cl=cl, ch=ch: e.activation(
                Pt[:, :, cl:ch], S[:, :].rearrange("p (h n) -> p h n", h=2)[:, :, cl:ch], AF.Exp),
                reads=[sres], writes=[("PT", pi)])

            def fpv(e, kt=kt, Pt=Pt, i=i, cl=cl, ch=ch):
                st, sp = (i == 0), (i == nt - 1)
                sk = partial
                e.matmul(Tps[0:64, cl:ch], lhsT=kt["v"][:, 0:64], rhs=Pt[:, 0, cl:ch], start=st, stop=sp, skip_group_check=sk)
                e.matmul(Tps[64:128, cl:ch], lhsT=kt["v"][:, 64:128], rhs=Pt[:, 1, cl:ch], start=st, stop=sp, tile_position=(0, 64),
                         skip_group_check=sk)
                e.matmul(PS_SM[0:64, cl:ch], lhsT=ones_b[:, 0:64], rhs=Pt[:, 0, cl:ch], start=st, stop=sp, skip_group_check=sk)
                return e.matmul(PS_SM[64:128, cl:ch], lhsT=ones_b[:, 64:128], rhs=Pt[:, 1, cl:ch], start=st, stop=sp,
                                tile_position=(0, 64), skip_group_check=sk)
            P.op("pe", fpv, reads=[kt["vres"], ("PT", pi), "ones_b"], writes=[tres, "PS_SM"])
        if sinkcol is not None:
            P.op("dve", lambda e: e.tensor_scalar_add(rr[:, 0:N], PS_SM[:, 0:N], esink[:, sinkcol:sinkcol + 1]),
                 reads=["PS_SM", "esink"], writes=["rr"])
            P.op("dve", lambda e: e.reciprocal(rr[:, 0:N], rr[:, 0:N]), reads=["rr"], writes=["rr"])
        else:
            P.op("dve", lambda e: e.reciprocal(rr[:, 0:N], PS_SM[:, 0:N]), reads=["PS_SM"], writes=["rr"])
        P.op("pool", lambda e: e.tensor_tensor(ggk[:, 0:N], ggk[:, 0:N], rr[:, 0:N], ALU.mult), reads=[gres, "rr"], writes=[gres])
        ui = ctr["uto"] % 2
        ctr["uto"] += 1
        P.op("dve", lambda e: e.tensor_tensor(uTo[ui][:, 0:N], Tps[:, 0:N], ggk[:, 0:N], ALU.mult),
             reads=[tres, gres], writes=[("uTo", ui)])
        t0 = q0 // 128
        ntile = N // 128
        dst = ut_t[t0:t0 + ntile, :, gi, :].rearrange("t f k -> f t k")
        P.dma("sp", lambda e: e.dma_start(out=dst, in_=uTo[ui][:, 0:N].rearrange("p (t k) -> p t k", k=128)),
              reads=[("uTo", ui)], writes=[("ut", t, gi) for t in range(t0, t0 + ntile)], stream="uto%d" % ui)
        if "uT" in dbg_t and l == layers[0]:
            P.dma("sp", lambda e: e.dma_start(out=dbg_t["uT"][gi, :, q0:q0 + N], in_=uTo[ui][:, 0:N]),
                  reads=[("uTo", ui)], writes=[("dbg_uT", gi, q0)], stream="dbg")

    def phase_mixer(l, m, need_ctx):
        W = LW[l]
        cc_out = cc_outs[l]
        ntq = NTC if need_ctx else NT
        wsb, wres = PRE.pop("q" + m) if ("q" + m) in PRE else load_w(QOFF[m], W["w_in"])
        if m == "B":
            P.dma("pool", lambda e: e.dma_start(out=bzb.rearrange("p h n -> p (h n)"), in_=W["bzb"]), writes=["bzb", "tabregion", "halo_st"], stream="tb0")
            P.dma("pool", lambda e: e.dma_start(out=mzb, in_=mzb_d), writes=["mzb", "halo_st"], stream="tb1")
            P.dma("pool", lambda e: e.dma_start(out=mvbf.rearrange("p t n -> p (t n)"), in_=mvbf_d), writes=["mvbf", "halo_st"], stream="tb2")
            P.dma("pool", lambda e: e.dma_start(out=mvbl.rearrange("p t n -> p (t n)"), in_=mvbl_d), writes=["mvbl", "halo_st"], stream="tb3")
        if m == "C":
            P.dma("pool", lambda e: e.dma_start(out=mzc, in_=mzc_d), writes=["mzc"], stream="tb0")
            P.dma("pool", lambda e: e.dma_start(out=mvcf, in_=mvcf_d), writes=["mvcf"], stream="tb1")
            P.dma("pool", lambda e: e.dma_start(out=mvcl, in_=mvcl_d), writes=["mvcl"], stream="tb2")
            P.dma("sp", lambda e: e.dma_start(out=esink[:, :], in_=W["sink"]), writes=["esink"], stream="misc0")
            P.op("act", lambda e: e.activation(esink[:, :], esink[:, :], AF.Exp), reads=["esink"], writes=["esink"])
        if m == "A":
            P.dma("sp", lambda e: e.dma_start(out=kTA.rearrange("p (r n) -> p r n", r=RPB),
                                             in_=cc_out["A"][:, 0:2048].rearrange("(r p) n -> p r n", p=128)),
                  reads=["cc_outA"], writes=["kTA"], stream="ldA0")
            P.dma("sp", lambda e: e.dma_start(out=vA.rearrange("p (r t) n -> p r (t n)", r=RPB),
                                             in_=cc_out["A"][:, 2048:4096].rearrange("(r p) n -> p r n", p=128)),
                  reads=["cc_outA"], writes=["vA"], stream="ldA1")
        accq = {}
        for t in range(min(LOOK, ntq)):
            accq[t] = next_acc()
            proj_mm(t, wsb, wres, accq[t][0], accq[t][1])

        def qfront(t):
            ps, psres = accq.pop(t)
            if t + LOOK < ntq:
                accq[t + LOOK] = next_acc()
                proj_mm(t + LOOK, wsb, wres, accq[t + LOOK][0], accq[t + LOOK][1])
            tmb, tmn = next_tm()
            if m == "A":
                a_, an, t_, tn, u_, un, st_, stn = next_tset()
                rms_heads(ps, psres, 8, qn_bc, "qn_bc", a_, an, 0.125, t_, tn, st_, stn)
                rope_ops(a_, [an], 8, t, tmb, tmn, t_, tn, u_, un)
            elif m == "C":
                a_, an, t_, tn, u_, un, st_, stn = next_tset()
                P.op("act", lambda e, ps=ps, a_=a_: e.mul(a_, ps, 0.125), reads=[psres], writes=[an])
                rope_ops(a_, [an], 8, t, tmb, tmn, t_, tn, u_, un)
            else:
                P.op("act", lambda e, ps=ps, tmb=tmb: e.mul(tmb, ps, 0.125), reads=[psres], writes=[tmn])
            return tmb, tmn

        def qback(t, tmb, tmn):
            trb, trn = next_tr()

            def ftr(e, tmb=tmb, trb=trb):
                inst = None
                for j in range(4):
                    inst = e.transpose(trb[:, j * 128:(j + 1) * 128], tmb[:, j * 128:(j + 1) * 128], ident_b[:, :])
                return inst
            P.op("pe", ftr, reads=[tmn, "ident_b"], writes=[trn])
            if True:
                P.op("act", lambda e, t=t, trb=trb: e.copy(qTm[:, :, t * 128:(t + 1) * 128], trb[:, 0:512].rearrange("p (j n) -> p j n", j=4)),
                     reads=[trn], writes=[("qTm", j) for j in range(4)])
            else:
                P.op("dve", lambda e, t=t, trb=trb: e.tensor_copy(qTm[:, :, t * 128:(t + 1) * 128], trb[:, 0:512].rearrange("p (j n) -> p j n", j=4)),
                     reads=[trn], writes=[("qTm", j) for j in range(4)])
        fq = {0: qfront(0)}
        for t in range(ntq):
            if t + 1 < ntq:
                fq[t + 1] = qfront(t + 1)
            qback(t, *fq.pop(t))
        P.barrier()
        nxtm = {"B": "C", "C": "A"}.get(m)
        if nxtm is not None:
            PRE["q" + nxtm] = load_w(QOFF[nxtm], W["w_in"])
        if m == "A":
            for j in range(3):
                P.dma("pool", lambda e, j=j: e.dma_start(out=wout_sb[:, 4 * j:4 * j + 4, :],
                                                          in_=W["w_out"][j * 512:(j + 1) * 512, :].rearrange("(j p) n -> p j n", p=128)),
                      writes=[("wout", j)], stream="wout%d" % j)
            PRE["gate4"] = load_w(4 * 512, W["ada_w"])
            PRE["gate5"] = load_w(5 * 512, W["ada_w"])
        wgq = {0: load_w(G_OFF + (MIXI[m] * 4) * 128, W["w_in"], slot_kind="wg", ncols=128)}
        for pair in range(4):
            if pair + 1 < 4:
                wgq[pair + 1] = load_w(G_OFF + (MIXI[m] * 4 + pair + 1) * 128, W["w_in"], slot_kind="wg", ncols=128)
            ctr["wg_cur"] = wgq[pair][1][1]
            chunks = [(c * 512, 512, c) for c in range(4)]
            if need_ctx:
                chunks.append((TOK, 256, None))
            for (q0, N, c) in chunks:
                tiles = []
                if m == "A":
                    ctxA = [dict(kT=kTA_ctx[:, j * 128:(j + 1) * 128], kres=("kTA_ctx", NT + j), v=vA_ctx[:, j, :], vres=("vA_ctx", NT + j))
                            for j in range(2)]
                    if c is not None:
                        for kt_i in range(64):
                            tiles.append(dict(kT=kTA[:, kt_i * 128:(kt_i + 1) * 128], kres="kTA", v=vA[:, kt_i, :], vres="vA"))
                    tiles += ctxA
                    sinkcol = None
                elif m == "C":
                    ctxC = [dict(kT=kTC[:, s_ * 128:(s_ + 1) * 128], kres=("kTC", s_), v=vC[:, s_, :], vres=("vC", s_)) for s_ in (18, 19)]
                    if c is not None:
                        for tt in range(6):
                            s_ = 4 * c + tt
                            if c == 0 and tt == 0:
                                rhs, rres = mvcf, "mvcf"
                            elif c == 3 and tt == 5:
                                rhs, rres = mvcl, "mvcl"
                            else:
                                rhs, rres = mzc[:, (5 - tt) * 128:(9 - tt) * 128], "mzc"
                            blo, bhi = max(0, tt - 2), min(3, tt)
                            cl_, ch_ = blo * 128, (bhi + 1) * 128
                            rsl = rhs[:, cl_:ch_]
                            tiles.append(dict(kT=kTC[:, s_ * 128:(s_ + 1) * 128], kres=("kTC", s_), v=vC[:, s_, :], vres=("vC", s_),
                                              adds=[(cl_, ch_ - cl_, rsl, rsl, rres)], cols=(cl_, ch_)))
                    tiles += ctxC
                    sinkcol = pair
                else:
                    ctxB = [dict(kT=kTB[:, pair, s_ * 128:(s_ + 1) * 128], kres=("kTB", s_), v=vB[:, s_, pair * 128:(pair + 1) * 128], vres=("vB", s_))
                            for s_ in (20, 21)]
                    if c is not None:
                        for tt in range(8):
                            s_ = 4 * c + tt
                            blo, bhi = max(0, tt - 4), min(3, tt)
                            if c == 0:
                                blo = max(0, tt - 5)
                            if c == 3:
                                bhi = min(3, tt + 1)
                            cl_, ch_ = blo * 128, (bhi + 1) * 128
                            bx = bzb[:, 2 * pair, (7 - tt) * 128 + cl_:(7 - tt) * 128 + ch_]
                            by = bzb[:, 2 * pair + 1, (7 - tt) * 128 + cl_:(7 - tt) * 128 + ch_]
                            adds = [(cl_, ch_ - cl_, bx, by, "bzb")]

                            def mrange(lo, hi, tab, res_, base):
                                a_, b_ = max(lo, cl_), min(hi, ch_)
                                if b_ > a_:
                                    sl = tab[:, a_ - base:b_ - base]
                                    adds.append((a_, b_ - a_, sl, sl, res_))
                            gfull = mzb[:, (7 - tt) * 128:(11 - tt) * 128]
                            if c == 0:
                                mrange(0, 256, mvbf[:, tt, :], "mvbf", 0)
                                mrange(256, 512, gfull, "mzb", 0)
                            elif c == 3:
                                mrange(0, 256, gfull, "mzb", 0)
                                mrange(256, 512, mvbl[:, tt, :], "mvbl", 256)
                            else:
                                mrange(0, 512, gfull, "mzb", 0)
                            tiles.append(dict(kT=kTB[:, pair, s_ * 128:(s_ + 1) * 128], kres=("kTB", s_),
                                              v=vB[:, s_, pair * 128:(pair + 1) * 128], vres=("vB", s_), adds=adds, cols=(cl_, ch_)))
                    tiles += ctxB
                    sinkcol = None
                attend(l, m, pair, q0, N, tiles, sinkcol)

    def phase_wout(l, src, dst_x, need_ctx, last):
        W = LW[l]
        phase_gate(l, last)
        if last:
            P.op("dve", lambda e: e.memset(ss[:, :], 0.0), writes=[("ss", t) for t in range(NTC)])
        ntw = NTC if need_ctx else NT

        def wloads(t):
            i = t % 2
            P.dma("sp", lambda e, t=t, i=i: e.dma_start(out=utt[i], in_=ut_t[t, :, :, :]),
                  reads=[("ut", t, g) for g in range(12)], writes=[("utt", i)], stream="utt%d" % i)
            P.dma("sp", lambda e, t=t, i=i: e.dma_start(out=xt[i], in_=src[t * 128:(t + 1) * 128, :]),
                  writes=[("xt", i)], stream="xt%d" % i)
        wloads(0)
        for t in range(ntw):
            i = t % 2
            w = 0 if t < NT else 1
            if t + 1 < ntw:
                wloads(t + 1)
            def f(e, i=i):
                inst = None
                for half in range(2):
                    for j in range(12):
                        inst = e.matmul(PS_S[i][:, half * 512:(half + 1) * 512], lhsT=utt[i][:, j, :],
                                        rhs=wout_sb[:, j, half * 512:(half + 1) * 512], start=(j == 0), stop=(j == 11))
                return inst
            P.op("pe", f, reads=[("utt", i)] + [("wout", j) for j in range(3)], writes=["PS_S%d" % i])
            P.op("dve", lambda e, i=i, w=w: e.tensor_tensor(xn[i], PS_S[i][:, :], gate_bc[w], ALU.mult),
                 reads=["PS_S%d" % i, "gate_bc%d" % w], writes=[("xn", i)])
            P.op("pool", lambda e, i=i: e.tensor_tensor(xn[i], xn[i], xt[i], ALU.add),
                 reads=[("xn", i), ("xt", i)], writes=[("xn", i)])
            if not last:
                P.dma("sp", lambda e, t=t, i=i: e.dma_start(out=dst_x[t * 128:(t + 1) * 128, :], in_=xn[i]),
                      reads=[("xn", i)], writes=[("x1", t)], stream="xo%d" % i)
            else:
                if final_norm:
                    P.op("act", lambda e, t=t, i=i: e.activation(junk, xn[i], AF.Square, scale=1.0 / 32.0,
                                                                accum_out=ss[:, t:t + 1]),
                         reads=[("xn", i)], writes=["junk", ("ss", t)])
                    P.op("act", lambda e, t=t: e.activation(rstd[:, t:t + 1], ss[:, t:t + 1], AF.Ln, bias=eps_c[:, 0:1]),
                         reads=[("ss", t)], writes=[("rstd", t)])
                    P.op("act", lambda e, t=t: e.activation(rstd[:, t:t + 1], rstd[:, t:t + 1], AF.Exp, scale=-0.5),
                         reads=[("rstd", t)], writes=[("rstd", t)])
                    P.op("act", lambda e, t=t, i=i: e.activation(xn[i], xn[i], AF.Copy, scale=rstd[:, t:t + 1]),
                         reads=[("xn", i), ("rstd", t)], writes=[("xn", i)])
                    P.op("pool", lambda e, i=i: e.tensor_tensor(xn[i], xn[i], fnw_sb, ALU.mult),
                         reads=[("xn", i), "fnw"], writes=[("xn", i)])
                P.dma("sp", lambda e, t=t, i=i: e.dma_start(out=out_d[t * 128:(t + 1) * 128, :], in_=xn[i]),
                      reads=[("xn", i)], writes=[("out", t)], stream="xo%d" % i)

    def dbg_dump(name, src_ap, reads):
        if name in dbg_t:
            P.dma("sp", lambda e: e.dma_start(out=dbg_t[name], in_=src_ap), reads=reads, writes=["dbg_" + name], stream="dbg")

    setup()
    src = xin
    stopped = False
    for li, l in enumerate(layers):
        last = (li == len(layers) - 1)
        need_ctx = not last
        dst_x = x1_t.ap()
        phase_mod(l)
        prefetch_kvac(l)
        phase_norm(l, src)
        P.barrier()
        if li == 0:
            dbg_dump("hxT", hxT[:, :, :].rearrange("p k n -> p (k n)"), [("hxT", t) for t in range(NTC)])
        if stop_after == "norm":
            stopped = True
            break
        phase_kvproj(l)
        phase_exchange(l)
        if li == 0:
            dbg_dump("kvr", KVR[:, :], [("kTB", s_) for s_ in range(22)] + [("vB", s_) for s_ in range(22)])
            dbg_dump("kc", kTC[:, :], [("kTC", s_) for s_ in range(20)])
            dbg_dump("vc", vC[:, :, :].rearrange("p t n -> p (t n)"), [("vC", s_) for s_ in range(20)])
            dbg_dump("ccst", cc_st, ["cc_inA", "cc_inF", "cc_inL"])
        if stop_after == "kv":
            stopped = True
            break
        for m in ("B", "C", "A"):
            if m == "A":
                P.barrier()
            phase_mixer(l, m, need_ctx)
            if li == 0:
                dbg_dump("qT" + m, qTm.rearrange("p j n -> p (j n)"), [("qTm", j) for j in range(4)])
            P.barrier()
            if stop_after == "mix" + m:
                stopped = True
                break
        if stopped:
            break
        phase_wout(l, src, dst_x, need_ctx, last)
        P.barrier()
        src = x1_t.ap()
    if stopped:
        P.op("pool", lambda e: e.memset(xn[0], 0.0), writes=[("xn", 0)])
        P.dma("sp", lambda e: e.dma_start(out=out_d[0:128, :], in_=xn[0]), reads=[("xn", 0)], writes=["outz"], stream="xo0")
        P.barrier()

    print("sbuf bytes remaining:", nc.sbuf_bytes_remaining)
    with nc.Block() as block:
        P.emit(block)
    return nc


def make_in_maps(inputs, layers=(0, 1)):
    x = np.asarray(inputs["x"], np.float32)
    c = np.asarray(inputs["c"], np.float32)
    ctx = np.asarray(inputs["ctx"], np.float32)
    c_ctx = np.asarray(inputs["c_ctx"], np.float32)
    shared = {}
    for l in layers:
        shared["w_in%d" % l] = _perm_w_in(np.asarray(inputs["w_in"][l], np.float32))
        shared["w_out%d" % l] = _perm_w_out(np.asarray(inputs["w_out"][l], np.float32))
        shared["ada_w%d" % l] = np.ascontiguousarray(np.asarray(inputs["ada_w"][l], np.float32))
        ab = np.asarray(inputs["ada_b"][l], np.float32)
        shared["adabT%d" % l] = np.ascontiguousarray(ab[0:2048].reshape(16, 128).T)
        shared["adabg%d" % l] = np.ascontiguousarray(ab[2048:3072].reshape(1, D))
        shared["normT%d" % l] = np.ascontiguousarray(np.asarray(inputs["norm_w"][l], np.float32).reshape(8, 128).T)
        shared["qn%d" % l] = np.asarray(inputs["q_norm_a"][l], np.float32).reshape(1, 64).copy()
        shared["kn%d" % l] = np.asarray(inputs["k_norm_a"][l], np.float32).reshape(1, 64).copy()
        shared["bzb%d" % l] = _b_bias(np.asarray(inputs["rpb_b"][l], np.float32)).reshape(128, 8 * 11 * 128)
        sk = np.asarray(inputs["sink_c"][l], np.float32)
        st = np.zeros((128, 4), np.float32)
        for j in range(4):
            st[0:64, j] = sk[j]
            st[64:128, j] = sk[j + 4]
        shared["sink%d" % l] = st
    shared["fnw"] = np.asarray(inputs["final_norm_w"], np.float32).reshape(1, D).copy()
    maps = []
    for core in range(NCORE):
        b, r = core // RPB, core % RPB
        m = dict(shared)
        m["xin"] = np.ascontiguousarray(np.concatenate([x[b, r * TOK:(r + 1) * TOK, :], ctx[b]], axis=0))
        cv = np.zeros((128, 8, 2), np.float32)
        cv[:, :, 0] = c[b].reshape(8, 128).T
        cv[:, :, 1] = c_ctx.reshape(8, 128).T
        m["cvec"] = cv.reshape(128, 16)
        m["rope"] = _rope_tables(r).reshape(128, NTC * 2 * 64)
        m["ident"] = np.eye(128, dtype=np.float32)
        sv = np.zeros((128, 8), np.float32)
        if r > 0:
            sv[:, r - 1] = 1.0
        if r < RPB - 1:
            sv[:, 3 + r] = 1.0
        m["selv"] = sv
        gen, first, last = _b_masks(r)
        m["mzb"], m["mvbf"], m["mvbl"] = gen, first, last
        gen, first, last = _c_masks(r)
        m["mzc"], m["mvcf"], m["mvcl"] = gen, first, last
        maps.append(m)
    return maps


_NC_CACHE = {}


def kernel(**inputs):
    if "full" not in _NC_CACHE:
        _NC_CACHE["full"] = build_program()
    nc = _NC_CACHE["full"]
    maps = make_in_maps(inputs)
    res = run_bass_kernel_spmd(nc, maps, core_ids=list(range(NCORE)))
    out = np.zeros((2, SEQ, D), np.float32)
    for core in range(NCORE):
        b, r = core // RPB, core % RPB
        out[b, r * TOK:(r + 1) * TOK, :] = res.results[core]["out"]
    return out
```

```python
import numpy as np
import concourse.bass as bass
import concourse.mybir as mybir
from concourse.bass_utils import run_bass_kernel_spmd

F32 = mybir.dt.float32
BF16 = mybir.dt.bfloat16
I32 = mybir.dt.int32
AF = mybir.ActivationFunctionType
ALU = mybir.AluOpType
AX = mybir.AxisListType

D = 1024
SEQ = 8192
NCORE = 8
RPB = 4
TOK = SEQ // RPB
NT = TOK // 128
NTC = NT + 2
NTOK = NTC * 128
EPS = 1e-6
NEG = -32768.0
CCW = 8704
IN_W = 4608
QOFF = {"A": 0, "B": 512, "C": 1024}
KVAC_OFF = 1536
KB_OFF = 2048
VB_OFF = 2560
G_OFF = 3072
MIXI = {"A": 0, "B": 1, "C": 2}

ENG_CHUNK = 30000


class Op:
    __slots__ = ("eng", "fn", "deps", "signal", "seq", "stream", "sid", "ninst", "cc")

    def __init__(self, eng, fn, stream=None, ninst=1, cc=False):
        self.eng = eng
        self.fn = fn
        self.deps = []
        self.signal = False
        self.seq = None
        self.stream = stream
        self.sid = None
        self.ninst = ninst
        self.cc = cc


class Prog:
    ENGS = ("pe", "act", "dve", "pool", "sp")
    SYNC_SAME = ("act", "dve", "pool")

    def __init__(self, nc):
        self.nc = nc
        self.ops = {e: [] for e in self.ENGS}
        self.order = []
        self.lastw = {}
        self.readers = {}
        self.last_on_stream = {}

    def _deps(self, op, reads, writes):
        reads = list(reads)
        writes = list(writes)
        for r in list(reads):
            if isinstance(r, str) and r.startswith("PS_"):
                reads.remove(r)
                if r not in writes:
                    writes.append(r)
        deps = set()
        for r in reads:
            w = self.lastw.get(r)
            if w is not None:
                deps.add(w)
        for w_ in writes:
            w = self.lastw.get(w_)
            if w is not None:
                deps.add(w)
            for rd in self.readers.get(w_, ()):
                deps.add(rd)
        deps.discard(op)
        op.deps = list(deps)
        for r in reads:
            self.readers.setdefault(r, []).append(op)
        for w_ in writes:
            self.lastw[w_] = op
            self.readers[w_] = []

    def op(self, eng, fn, reads=(), writes=()):
        o = Op(eng, fn)
        self._deps(o, reads, writes)
        self.ops[eng].append(o)
        self.order.append(o)
        return o

    def dma(self, queue, fn, reads=(), writes=(), stream=None, ninst=1, cc=False):
        o = Op(queue, fn, stream=stream, ninst=ninst, cc=cc)
        self._deps(o, reads, writes)
        prev = self.last_on_stream.get(stream)
        if prev is not None and prev not in o.deps:
            o.deps.append(prev)
        self.last_on_stream[stream] = o
        self.ops[queue].append(o)
        self.order.append(o)
        return o

    def barrier(self):
        deps = set()
        for r, w in self.lastw.items():
            if w is not None:
                deps.add(w)
        for r, rl in self.readers.items():
            for rd in rl:
                deps.add(rd)
        deps = list(deps)
        for e in self.ENGS:
            o = Op(e, None)
            o.deps = deps
            self.ops[e].append(o)
            self.order.append(o)
        self.lastw = {}
        self.readers = {}

    def emit(self, block):
        nc = self.nc
        semd = {}

        def sems(sid):
            if sid not in semd:
                semd[sid] = nc.alloc_semaphore("s_%s_%d" % sid)
            return semd[sid]

        for o in self.order:
            for d in o.deps:
                if d.stream is not None:
                    d.signal = True
                elif d.eng != o.eng or o.eng in self.SYNC_SAME or o.stream is not None:
                    d.signal = True
        cnt = {e: 0 for e in self.ENGS}
        scnt = {}
        for o in self.order:
            if o.stream is not None:
                if o.cc:
                    scnt[o.stream] = scnt.get(o.stream, 0) + 1
                    assert scnt[o.stream] == 1, "one collective per stream"
                    o.sid = ("dma_" + o.stream, 0)
                    o.seq = 1
                else:
                    scnt[o.stream] = scnt.get(o.stream, 0) + o.ninst
                    o.sid = ("dma_" + o.stream, 0)
                    o.seq = 16 * scnt[o.stream]
            elif o.signal:
                c = cnt[o.eng]
                o.sid = (o.eng, c // ENG_CHUNK)
                o.seq = c % ENG_CHUNK + 1
                cnt[o.eng] = c + 1

        def run_engine(ename, eng):
            known = {}
            for o in self.ops[ename]:
                need = {}
                for d in o.deps:
                    if d.seq is None:
                        continue
                    if (d.stream is None and d.eng == ename and o.stream is None
                            and ename not in self.SYNC_SAME):
                        continue
                    if need.get(d.sid, 0) < d.seq:
                        need[d.sid] = d.seq
                for sid, v in need.items():
                    if known.get(sid, 0) >= v:
                        continue
                    eng.wait_ge(sems(sid), v)
                    known[sid] = v
                if o.fn is None:
                    continue
                inst = o.fn(eng)
                if o.cc:
                    inst.then_inc(sems(o.sid))
                elif o.stream is not None:
                    insts = inst if isinstance(inst, (list, tuple)) else [inst]
                    assert len(insts) == o.ninst
                    for it in insts:
                        it.then_inc(sems(o.sid), 16)
                elif o.signal:
                    inst.then_inc(sems(o.sid), 1)

        @block.tensor
        def _(e):
            run_engine("pe", e)

        @block.scalar
        def _(e):
            run_engine("act", e)

        @block.vector
        def _(e):
            run_engine("dve", e)

        @block.gpsimd
        def _(e):
            run_engine("pool", e)

        @block.sync
        def _(e):
            run_engine("sp", e)


PAIR_HEADS = [0, 4, 1, 5, 2, 6, 3, 7]


def _perm_heads(w, off, nheads=8):
    cols = []
    for h in PAIR_HEADS:
        cols.append(w[..., off + h * 64: off + (h + 1) * 64])
    return np.concatenate(cols, axis=-1)


def _perm_w_in(w):
    oqa, oka, ova = 0, 512, 640
    oqb, okb, ovb = 768, 1280, 1792
    oqc, okc, ovc = 2304, 2816, 2944
    og = 3072
    parts = [
        _perm_heads(w, oqa), _perm_heads(w, oqb), _perm_heads(w, oqc),
        w[:, oka:oka + 128], w[:, okc:okc + 128], w[:, ova:ova + 128], w[:, ovc:ovc + 128],
        _perm_heads(w, okb), _perm_heads(w, ovb),
        _perm_heads(w, og), _perm_heads(w, og + 512), _perm_heads(w, og + 1024),
    ]
    return np.ascontiguousarray(np.concatenate(parts, axis=1))


def _perm_w_out(w):
    rows = []
    for m in range(3):
        for h in PAIR_HEADS:
            rows.append(w[m * 512 + h * 64: m * 512 + (h + 1) * 64, :])
    return np.ascontiguousarray(np.concatenate(rows, axis=0))


def _rope_tables(rank):
    t = (rank * TOK + np.arange(TOK)).astype(np.int32)
    rows = (t // 64).astype(np.float32)
    cols = (t % 64).astype(np.float32)
    nf = 16
    freq = (np.float32(10000.0) ** (-np.arange(nf, dtype=np.float32) / np.float32(nf))).astype(np.float32)
    ang = np.concatenate([rows[:, None] * freq, cols[:, None] * freq], axis=-1).astype(np.float32)
    c = np.cos(ang).astype(np.float32)
    s = np.sin(ang).astype(np.float32)
    c = np.concatenate([c, np.ones((256, 32), np.float32)], axis=0)
    s = np.concatenate([s, np.zeros((256, 32), np.float32)], axis=0)
    C2 = np.concatenate([c, c], axis=1)
    S2 = np.concatenate([-s, s], axis=1)
    tab = np.stack([C2, S2], axis=1)
    tab = tab.reshape(NTC, 128, 2, 64).transpose(1, 0, 2, 3)
    return np.ascontiguousarray(tab)


def _b_valid(i_q, i_k):
    q = np.arange(128)
    k = np.arange(128)
    rq = 2 * i_q + q // 64
    cq = q % 64
    rk = 2 * i_k + k // 64
    ck = k % 64
    rs = np.clip(rq - 4, 0, 120)
    cs = np.clip(cq - 8, 0, 48)
    ok = ((rk[:, None] >= rs[None, :]) & (rk[:, None] < rs[None, :] + 8) &
          (ck[:, None] >= cs[None, :]) & (ck[:, None] < cs[None, :] + 16) &
          (rk[:, None] >= 0) & (rk[:, None] < 128))
    return np.where(ok, np.float32(0.0), np.float32(NEG)).astype(np.float32)


def _b_masks(rank):
    gen = np.full((128, 11, 128), NEG, np.float32)
    for s in range(11):
        dl = 5 - s
        if abs(dl) <= 2:
            gen[:, s, :] = _b_valid(30, 30 + dl)
    first = np.zeros((128, 8, 2, 128), np.float32)
    last = np.zeros((128, 8, 2, 128), np.float32)
    for t in range(8):
        for b in range(2):
            cg = 4 * rank + 0
            first[:, t, b, :] = _b_valid(4 * cg + b, 4 * cg - 2 + t)
            cg = 4 * rank + 3
            last[:, t, b, :] = _b_valid(4 * cg + 2 + b, 4 * cg - 2 + t)
    return gen.reshape(128, 11 * 128), first.reshape(128, 8 * 256), last.reshape(128, 8 * 256)


def _b_bias(rpb_l):
    out = np.zeros((128, 8, 11, 128), np.float32)
    k = np.arange(128)
    q = np.arange(128)
    kr, kc = k // 64, k % 64
    qr, qc = q // 64, q % 64
    for s in range(11):
        dl = 5 - s
        dr = 2 * dl + kr[:, None] - qr[None, :] + 7
        dc = kc[:, None] - qc[None, :] + 15
        ok = (dr >= 0) & (dr <= 14) & (dc >= 0) & (dc <= 30)
        drc = np.clip(dr, 0, 14)
        dcc = np.clip(dc, 0, 30)
        for hi, h in enumerate(PAIR_HEADS):
            vals = rpb_l[h][drc, dcc]
            out[:, hi, s, :] = np.where(ok, vals, np.float32(0.0))
    return np.ascontiguousarray(out.reshape(128, 8, 11 * 128))


def _c_masks(rank):
    k = np.arange(128)[:, None]
    q = np.arange(128)[None, :]
    d1 = np.where(k <= q, 0.0, NEG).astype(np.float32)
    dm1 = np.where(k >= q, 0.0, NEG).astype(np.float32)
    d0 = np.zeros((128, 128), np.float32)
    M = np.full((128, 128), NEG, np.float32)
    gen = np.stack([M, M, M, d1, d0, dm1, M, M, M], axis=1).reshape(128, 9 * 128)
    first = np.stack([dm1 if rank > 0 else M, M, M, M], axis=1).reshape(128, 512)
    last = np.stack([M, M, M, d1 if rank < RPB - 1 else M], axis=1).reshape(128, 512)
    return gen, first, last


def build_program(layers=(0, 1), final_norm=True, dbg=None, stop_after=None):
    nc = bass.Bass("TRN2", target_bir_lowering=False)
    P = Prog(nc)
    dbg = dbg or ()

    def din(name, shape, dt=F32):
        return nc.dram_tensor(name, list(shape), dt, kind="ExternalInput").ap()

    xin = din("xin", [NTOK, D])
    cvec = din("cvec", [128, 16])
    rope_d = din("rope", [128, NTC * 2 * 64])
    selv_d = din("selv", [128, 8])
    fnw_d = din("fnw", [1, D])
    ident_d = din("ident", [128, 128])
    mzb_d = din("mzb", [128, 11 * 128])
    mvbf_d = din("mvbf", [128, 2048])
    mvbl_d = din("mvbl", [128, 2048])
    mzc_d = din("mzc", [128, 9 * 128])
    mvcf_d = din("mvcf", [128, 512])
    mvcl_d = din("mvcl", [128, 512])
    LW = {}
    for l in layers:
        LW[l] = dict(
            w_in=din("w_in%d" % l, [D, IN_W]),
            w_out=din("w_out%d" % l, [1536, D]),
            ada_w=din("ada_w%d" % l, [D, 3 * D]),
            adabT=din("adabT%d" % l, [128, 16]),
            adabg=din("adabg%d" % l, [1, D]),
            normT=din("normT%d" % l, [128, 8]),
            qn=din("qn%d" % l, [1, 64]),
            kn=din("kn%d" % l, [1, 64]),
            bzb=din("bzb%d" % l, [128, 8 * 11 * 128]),
            sink=din("sink%d" % l, [128, 4]),
        )
    out_d = nc.dram_tensor("out", [TOK, D], F32, kind="ExternalOutput").ap()
    dbg_t = {}
    for name, shape, dt in dbg:
        dbg_t[name] = nc.dram_tensor("dbg_" + name, list(shape), dt, kind="ExternalOutput").ap()

    x1_t = nc.dram_tensor("x1_scr", [NTOK, D], F32)
    CCP = {"F": (4096, 6400), "L": (6400, 8704), "A": (0, 4096)}
    cc_ins = {k: nc.dram_tensor("cc_in%s" % k, [128, c1 - c0], BF16) for k, (c0, c1) in CCP.items()}
    cc_outs = {l: {k: nc.dram_tensor("cc_out%d%s" % (l, k), [RPB * 128, c1 - c0], BF16) for k, (c0, c1) in CCP.items()}
               for l in layers}
    ut_t = nc.dram_tensor("ut_scr", [NTC, 128, 12, 128], BF16)

    def sb(name, shape, dt):
        return nc.alloc_sbuf_tensor(name, list(shape), dt)

    hxT = sb("hxT", [128, 8, NTOK], BF16)
    hx_f = hxT.bitcast(F32)
    hx_f2 = hx_f[:, :, :].rearrange("p a b -> p (a b)")
    gate_bc = [hx_f2[:, 0:1024], hx_f2[:, 1024:2048]]
    adabg = hx_f2[:, 2048:3072]
    fnw_sb = hx_f2[:, 3072:4096]
    wst = [sb("wst%d" % i, [128, 8, 512], BF16) for i in range(2)]
    wg = [sb("wg%d" % i, [128, 8, 128], BF16) for i in range(2)]
    rope = sb("rope_sb", [128, NTC, 2, 64], F32)
    ident_f = sb("ident_f", [128, 128], F32)
    ident_b = sb("ident_b", [128, 128], BF16)
    ones_b = sb("ones_b", [128, 128], BF16)
    cs_f = sb("cs_f", [128, 16], F32)
    cs_e = sb("cs_e", [128, 16], F32)
    cs_b = sb("cs_b", [128, 8, 2], BF16)
    modT = sb("modT", [128, 16, 2], F32)
    adabT = sb("adabT", [128, 16], F32)
    normT = sb("normT", [128, 8], F32)
    A1T = sb("A1T", [128, 8, 2], F32)
    qn_bc = sb("qn_bc", [128, 64], F32)
    kn_bc = sb("kn_bc", [128, 64], F32)
    ss = sb("ss", [128, NTC], F32)
    rstd = sb("rstd", [128, NTC], F32)
    selv = sb("selv_sb", [128, 8], F32)
    kTC = sb("kTC", [128, 20 * 128], BF16)
    vC = sb("vC", [128, 20, 128], BF16)
    kTA_ctx = sb("kTA_ctx", [128, 256], BF16)
    vA_ctx = sb("vA_ctx", [128, 2, 128], BF16)
    esink = sb("esink", [128, 4], F32)
    sm4 = sb("sm4", [128, 16], F32)
    eps_c = sb("eps_c", [128, 1], F32)
    sm4x = sb("sm4x", [128, 48], F32)
    KVR = sb("KVR", [128, 22528], BF16)
    kTB = KVR[:, 0:11264].rearrange("p (j n) -> p j n", j=4)
    vB = KVR[:, 11264:22528].rearrange("p (t n) -> p t n", t=22)
    kTA = KVR[:, 0:64 * 128]
    vA = KVR[:, 8192:8192 + 64 * 128].rearrange("p (t n) -> p t n", t=64)
    ARN = sb("ARN", [128, 26624], BF16)
    qTm = ARN[:, 0:4 * NTOK].rearrange("p (j n) -> p j n", j=4)
    TB0 = 4 * NTOK
    cc_st = ARN[:, TB0:TB0 + CCW]
    stg = [ARN[:, TB0 + k * 2304:TB0 + (k + 1) * 2304] for k in range(3)] + [ARN[:, 17920 + k * 2304:17920 + (k + 1) * 2304] for k in range(3)]
    bzb = ARN[:, TB0:TB0 + 8 * 1408].rearrange("p (h n) -> p h n", h=8)
    mzb = ARN[:, TB0 + 11264:TB0 + 11264 + 1408]
    mvbf = ARN[:, TB0 + 12672:TB0 + 12672 + 2048].rearrange("p (t n) -> p t n", t=8)
    mvbl = ARN[:, TB0 + 14720:TB0 + 14720 + 2048].rearrange("p (t n) -> p t n", t=8)
    mzc = ARN[:, TB0:TB0 + 1152]
    mvcf = ARN[:, TB0 + 1152:TB0 + 1664]
    mvcl = ARN[:, TB0 + 1664:TB0 + 2176]
    wout_sb = ARN[:, TB0:TB0 + 12 * 1024].rearrange("p (j n) -> p j n", j=12)
    utt = [ARN[:, i * 1536:(i + 1) * 1536].rearrange("p (j n) -> p j n", j=12) for i in range(2)]
    X8 = sb("X8", [128, 2048], F32)
    X8b = X8.bitcast(BF16)
    xt = [X8[:, 0:1024], X8[:, 1024:2048]]
    ge = X8[:, 0:512]
    gg = X8[:, 512:1024]
    rr = X8[:, 1024:1536]
    uTo = [X8b[:, 3072:3584], X8b[:, 3584:4096]]
    GB = sb("GB", [128, 2048], F32)
    gg2 = [GB[:, 0:512], GB[:, 512:1024]]
    ge2 = [GB[:, 1024:1536], GB[:, 1536:2048]]
    XN = sb("XN", [128, 2048], F32)
    xn = [XN[:, 0:1024], XN[:, 1024:2048]]
    wk = [XN[:, i * 512:(i + 1) * 512] for i in range(4)]
    TMB = sb("TMB", [128, 1024], BF16)
    tm_b = [TMB[:, 0:512], TMB[:, 512:1024]]
    junk = TMB[:, :]
    PTB = sb("PTB", [128, 2048], BF16)
    PT = [PTB[:, i * 1024:(i + 1) * 1024].rearrange("p (h n) -> p h n", h=2) for i in range(2)]
    cs_rep = PTB[:, :].rearrange("p (a c) -> p a c", a=16)

    PS_S = [nc.alloc_psum_tensor("ps_s%d" % i, [128, 1024], F32) for i in range(2)]
    PS_T = nc.alloc_psum_tensor("ps_t", [128, 512], F32)
    PS_SM = nc.alloc_psum_tensor("ps_sm", [128, 512], F32)
    PS_G = nc.alloc_psum_tensor("ps_g", [128, 512], F32)
    PS_X = nc.alloc_psum_tensor("ps_x", [128, 512], F32)
    PS_Xb = PS_X.bitcast(BF16)

    LOOK = 3
    PRE = {}
    ctr = {"wst": 0, "wg": 0, "acc": 0, "tm": 0, "pt": 0, "uto": 0, "wg_cur": 0, "att": 0, "tr": 0, "ts": 0}

    def load_w(c0, wsrc, slot_kind="wst", ncols=512):
        if slot_kind == "wst":
            i = ctr["wst"] % 2
            ctr["wst"] += 1
            dst = wst[i]
        else:
            i = ctr["wg"] % 2
            ctr["wg"] += 1
            dst = wg[i]
        res = (slot_kind, i)
        srcv = wsrc.rearrange("(kc p) n -> p kc n", p=128)[:, :, c0:c0 + ncols]
        P.dma("pool", lambda e, d=dst, s=srcv, n=ncols: e.dma_start(out=d[:, :, 0:n], in_=s),
              writes=[res], stream="%s%d" % (slot_kind, i))
        return dst, res

    def setup():
        P.dma("sp", lambda e: e.dma_start(out=ident_f[:, :], in_=ident_d), writes=["ident_f"], stream="misc3")
        P.op("pool", lambda e: e.tensor_copy(ident_b[:, :], ident_f[:, :]), reads=["ident_f"], writes=["ident_b"])
        P.op("pool", lambda e: e.memset(ones_b[:, :], 1.0), writes=["ones_b"])
        P.op("pool", lambda e: e.memset(eps_c[:, :], EPS), writes=["eps_c"])
        P.dma("sp", lambda e: e.dma_start(out=cs_f[:, :], in_=cvec), writes=["cs_f"], stream="misc0")
        P.dma("sp", lambda e: e.dma_start(out=rope[:, :, :, :].rearrange("p a b c -> p (a b c)"), in_=rope_d),
              writes=["rope"], stream="misc1")
        P.dma("sp", lambda e: e.dma_start(out=selv[:, :], in_=selv_d), writes=["selv"], stream="misc2")
        P.op("act", lambda e: e.activation(cs_e[:, :], cs_f[:, :], AF.Exp, scale=-1.0), reads=["cs_f"], writes=["cs_e"])
        P.op("dve", lambda e: e.tensor_scalar_add(cs_e[:, :], cs_e[:, :], 1.0), reads=["cs_e"], writes=["cs_e"])
        P.op("dve", lambda e: e.reciprocal(cs_e[:, :], cs_e[:, :]), reads=["cs_e"], writes=["cs_e"])
        P.op("dve", lambda e: e.tensor_tensor(cs_b[:, :, :].rearrange("p a b -> p (a b)"), cs_f[:, :], cs_e[:, :], ALU.mult),
             reads=["cs_e", "cs_f"], writes=["cs_b"])

    def phase_mod(l):
        W = LW[l]
        P.dma("sp", lambda e: e.dma_start(out=adabT[:, :], in_=W["adabT"]), writes=["adabT"], stream="misc0")
        P.dma("sp", lambda e: e.dma_start(out=normT[:, :], in_=W["normT"]), writes=["normT"], stream="misc1")
        P.dma("sp", lambda e: e.dma_start(out=qn_bc[:, :], in_=W["qn"].partition_broadcast(128)),
              writes=["qn_bc"], stream="misc3")
        P.dma("sp", lambda e: e.dma_start(out=kn_bc[:, :], in_=W["kn"].partition_broadcast(128)),
              writes=["kn_bc"], stream="misc4")
        for blk in range(4):
            wsb, wres = load_w(blk * 512, W["ada_w"])

            def f(e, wsb=wsb, blk=blk):
                inst = None
                for sub in range(4):
                    g = blk * 4 + sub
                    for kc in range(8):
                        inst = e.matmul(PS_X[:, g * 2:g * 2 + 2], lhsT=wsb[:, kc, sub * 128:(sub + 1) * 128],
                                        rhs=cs_b[:, kc, :], start=(kc == 0), stop=(kc == 7))
                return inst
            P.op("pe", f, reads=[wres, "cs_b"], writes=["PS_X"])
        P.op("dve", lambda e: e.tensor_tensor(modT[:, :, :], PS_X[:, 0:32].rearrange("p (g w) -> p g w", w=2),
                                              adabT[:, :].unsqueeze(2).broadcast_to([128, 16, 2]), ALU.add),
             reads=["PS_X", "adabT"], writes=["modT"])
        P.op("dve", lambda e: e.scalar_tensor_tensor(A1T[:, :, :], modT[:, 8:16, :], 1.0,
                                                     normT[:, :].unsqueeze(2).broadcast_to([128, 8, 2]),
                                                     ALU.add, ALU.mult),
             reads=["modT", "normT"], writes=["A1T"])

    def prefetch_kvac(l):
        PRE["kvac"] = load_w(KVAC_OFF, LW[l]["w_in"])

    def phase_gate(l, last):
        W = LW[l]
        P.dma("sp", lambda e: e.dma_start(out=adabg, in_=W["adabg"].partition_broadcast(128)),
              writes=["adabg"], stream="misc2")
        if last:
            P.dma("sp", lambda e: e.dma_start(out=fnw_sb, in_=fnw_d.partition_broadcast(128)), writes=["fnw"], stream="misc4")
        P.op("dve", lambda e: e.tensor_copy(cs_rep, cs_b[:, :, :].rearrange("p a b -> p (a b)").unsqueeze(2).broadcast_to([128, 16, 128])),
             reads=["cs_b"], writes=["cs_rep"])
        for blk in (4, 5):
            wsb, wres = PRE.pop("gate%d" % blk) if ("gate%d" % blk) in PRE else load_w(blk * 512, W["ada_w"])

            def f(e, wsb=wsb, blk=blk):
                inst = None
                for which in range(2):
                    for kc in range(8):
                        inst = e.matmul(PS_S[which][:, (blk - 4) * 512:(blk - 3) * 512], lhsT=cs_rep[:, kc * 2 + which, :],
                                        rhs=wsb[:, kc, :], start=(kc == 0), stop=(kc == 7))
                return inst
            P.op("pe", f, reads=[wres, "cs_rep"], writes=["PS_S0", "PS_S1"])
        for which in range(2):
            P.op("dve", lambda e, which=which: e.tensor_tensor(gate_bc[which], PS_S[which][:, :], adabg, ALU.add),
                 reads=["PS_S%d" % which, "adabg"], writes=["gate_bc%d" % which])

    def phase_norm(l, src):
        P.op("dve", lambda e: e.memset(ss[:, :], 0.0), writes=[("ss", t) for t in range(NTC)])
        for t in range(NTC):
            i = t % 2
            w = 0 if t < NT else 1
            P.dma("sp", lambda e, t=t, i=i: e.dma_start(out=xt[i], in_=src[t * 128:(t + 1) * 128, :]),
                  writes=[("xt", i)], stream="xt%d" % i)
            P.op("act", lambda e, t=t, i=i: e.activation(junk, xt[i], AF.Square, scale=1.0 / 32.0,
                                                        accum_out=ss[:, t:t + 1]),
                 reads=[("xt", i)], writes=["junk", ("ss", t)])
            P.op("act", lambda e, t=t: e.activation(rstd[:, t:t + 1], ss[:, t:t + 1], AF.Ln, bias=eps_c[:, 0:1]),
                 reads=[("ss", t)], writes=[("rstd", t)])
            P.op("act", lambda e, t=t: e.activation(rstd[:, t:t + 1], rstd[:, t:t + 1], AF.Exp, scale=-0.5),
                 reads=[("rstd", t)], writes=[("rstd", t)])
            P.op("act", lambda e, t=t, i=i: e.activation(xn[i], xt[i], AF.Copy, scale=rstd[:, t:t + 1]),
                 reads=[("xt", i), ("rstd", t)], writes=[("xn", i)])

            def ftr(e, i=i):
                inst = None
                for kc in range(8):
                    inst = e.transpose(PS_S[i][:, kc * 128:(kc + 1) * 128], xn[i][:, kc * 128:(kc + 1) * 128], ident_f[:, :])
                return inst
            P.op("pe", ftr, reads=[("xn", i), "ident_f"], writes=["PS_S%d" % i])
            P.op("dve", lambda e, i=i, w=w: e.tensor_tensor(
                xn[i].rearrange("p (k n) -> p k n", k=8), PS_S[i][:, :].rearrange("p (k n) -> p k n", k=8),
                A1T[:, :, w:w + 1].broadcast_to([128, 8, 128]), ALU.mult),
                reads=["PS_S%d" % i, "A1T"], writes=[("xn", i)])
            P.op("pool", lambda e, i=i, w=w, t=t: e.tensor_tensor(
                hxT[:, :, t * 128:(t + 1) * 128], xn[i].rearrange("p (k n) -> p k n", k=8),
                modT[:, 0:8, w:w + 1].broadcast_to([128, 8, 128]), ALU.add),
                reads=[("xn", i), "modT"], writes=[("hxT", t)])

    def proj_mm(t, wsb, wres, ps, psres):
        def f(e):
            inst = None
            for kc in range(8):
                inst = e.matmul(ps, lhsT=hxT[:, kc, t * 128:(t + 1) * 128], rhs=wsb[:, kc, :],
                                start=(kc == 0), stop=(kc == 7))
            return inst
        P.op("pe", f, reads=[wres, ("hxT", t)], writes=[psres])

    def rope_ops(xsrc, xres, nh, t, dst, dstres, tt_, ttn, uu_, uun, eng_a="pool", eng_b="dve"):
        C2 = rope[:, t, 0, :].unsqueeze(1).broadcast_to([128, nh, 64])
        S2a = rope[:, t, 1, 0:32].unsqueeze(1).broadcast_to([128, nh, 32])
        S2b = rope[:, t, 1, 32:64].unsqueeze(1).broadcast_to([128, nh, 32])
        x3 = xsrc.rearrange("p (h d) -> p h d", d=64)
        t3 = tt_[:, 0:nh * 64].rearrange("p (h d) -> p h d", d=64)
        u3 = uu_[:, 0:nh * 64].rearrange("p (h d) -> p h d", d=64)
        d3 = dst.rearrange("p (h d) -> p h d", d=64)
        xres = list(xres)
        P.op(eng_a, lambda e: e.tensor_tensor(t3, x3, C2, ALU.mult), reads=xres + ["rope"], writes=[ttn])
        P.op(eng_b, lambda e: e.tensor_tensor(u3[:, :, 0:32], x3[:, :, 32:64], S2a, ALU.mult), reads=xres + ["rope"], writes=[uun])
        P.op(eng_b, lambda e: e.tensor_tensor(u3[:, :, 32:64], x3[:, :, 0:32], S2b, ALU.mult), reads=xres + ["rope", uun], writes=[uun])
        P.op(eng_a, lambda e: e.tensor_tensor(d3, t3, u3, ALU.add), reads=[ttn, uun], writes=[dstres])

    def rms_heads(ps, psres, nh, wbc, wbcres, dst, dstres, scale, tt_, ttn, st_, stn):
        n = nh * 64
        P.op("act", lambda e: e.activation(tt_[:, 0:n], ps, AF.Square), reads=[psres], writes=[ttn])
        P.op("dve", lambda e: e.tensor_reduce(st_[:, 0:nh], tt_[:, 0:n].rearrange("p (h d) -> p h d", d=64), AX.X, ALU.add),
             reads=[ttn], writes=[stn])
        P.op("act", lambda e: e.activation(st_[:, 0:nh], st_[:, 0:nh], AF.Ln, bias=eps_c[:, 0:1], scale=1.0 / 64.0),
             reads=[stn], writes=[stn])
        P.op("act", lambda e: e.activation(st_[:, 0:nh], st_[:, 0:nh], AF.Exp, scale=-0.5),
             reads=[stn], writes=[stn])
        d3 = dst.rearrange("p (h d) -> p h d", d=64)
        P.op("dve", lambda e: e.tensor_tensor(d3, ps.rearrange("p (h d) -> p h d", d=64),
                                              st_[:, 0:nh].unsqueeze(2).broadcast_to([128, nh, 64]), ALU.mult),
             reads=[psres, stn], writes=[dstres])
        P.op("dve", lambda e: e.scalar_tensor_tensor(d3, d3, scale, wbc[:, :].unsqueeze(1).broadcast_to([128, nh, 64]),
                                                      ALU.mult, ALU.mult),
             reads=[dstres, wbcres], writes=[dstres])

    ACCS = [(PS_S[0][:, 0:512], "PS_S0a"), (PS_S[1][:, 0:512], "PS_S1a"), (PS_S[0][:, 512:1024], "PS_S0b"),
            (PS_S[1][:, 512:1024], "PS_S1b"), (PS_T[:, :], "PS_T"), (PS_SM[:, :], "PS_SM")]
    TRS = [(PS_X.bitcast(BF16), "PS_X"), (PS_G.bitcast(BF16), "PS_G")]
    TSETS = [
        (wk[0], "wk0", wk[2], "wk2", wk[3], "wk3"),
        (gg2[0], ("gg", 0), gg2[1], ("gg", 1), ge2[0], ("ge", 0)),
        (ge2[1], ("ge", 1), wk[1], "wk1", rr, "rr"),
    ]
    TMS = [(tm_b[0], ("tm", 0)), (tm_b[1], ("tm", 1)), (uTo[0], ("uTo", 0))]

    def next_acc():
        i = ctr["acc"] % len(ACCS)
        ctr["acc"] += 1
        return ACCS[i]

    def next_tr():
        i = ctr["tr"] % 2
        ctr["tr"] += 1
        return TRS[i]

    def next_tset():
        i = ctr["ts"] % 3
        ctr["ts"] += 1
        a, an, t_, tn, u, un = TSETS[i]
        return a, an, t_, tn, u, un, sm4x[:, i * 16:(i + 1) * 16], ("sm4", i)

    def next_tm():
        i = ctr["tm"] % 3
        ctr["tm"] += 1
        return TMS[i]

    def phase_kvproj(l):
        W = LW[l]
        wsb, wres = PRE.pop("kvac") if "kvac" in PRE else load_w(KVAC_OFF, W["w_in"])
        accq = {}
        for t in range(min(LOOK, NTC)):
            accq[t] = next_acc()
            proj_mm(t, wsb, wres, accq[t][0], accq[t][1])

        def front1(t):
            ps, psres = accq.pop(t)
            if t + LOOK < NTC:
                accq[t + LOOK] = next_acc()
                proj_mm(t + LOOK, wsb, wres, accq[t + LOOK][0], accq[t + LOOK][1])
            a_, an, t_, tn, u_, un, st_, stn = next_tset()
            if t < NT:
                vdst = cc_st[:, 2048 + t * 128:2048 + (t + 1) * 128]
                vres = ("cc_st", "vA", t)
                cslot = 1 + t
            else:
                vdst = vA_ctx[:, t - NT, :]
                vres = ("vA_ctx", t)
                cslot = 18 + (t - NT)
            P.op("act", lambda e, ps=ps, vdst=vdst: e.copy(vdst, ps[:, 256:384]), reads=[psres], writes=[vres])
            P.op("act", lambda e, ps=ps, cslot=cslot: e.copy(vC[:, cslot, :], ps[:, 384:512]), reads=[psres], writes=[("vC", cslot)])
            rms_heads(ps[:, 0:128], psres, 2, kn_bc, "kn_bc", a_[:, 0:128], an, 1.0, t_, tn, st_, stn)
            P.op("act", lambda e, ps=ps, a_=a_: e.copy(a_[:, 128:256], ps[:, 128:256]), reads=[psres, an], writes=[an])
            tmb, tmn = next_tm()
            rope_ops(a_[:, 0:256], [an], 4, t, tmb[:, 0:256], tmn, t_, tn, u_, un)
            return tmb, tmn, cslot

        def back1(t, tmb, tmn, cslot):
            trb, trn = next_tr()

            def ftr(e, tmb=tmb, trb=trb):
                e.transpose(trb[:, 0:128], tmb[:, 0:128], ident_b[:, :])
                return e.transpose(trb[:, 128:256], tmb[:, 128:256], ident_b[:, :])
            P.op("pe", ftr, reads=[tmn, "ident_b"], writes=[trn])
            if t < NT:
                kdst = cc_st[:, t * 128:(t + 1) * 128]
                kres = ("cc_st", "kA", t)
            else:
                kdst = kTA_ctx[:, (t - NT) * 128:(t - NT + 1) * 128]
                kres = ("kTA_ctx", t)
            P.op("act", lambda e, kdst=kdst, trb=trb: e.copy(kdst, trb[:, 0:128]), reads=[trn], writes=[kres])
            P.op("dve", lambda e, cslot=cslot, trb=trb: e.tensor_copy(kTC[:, cslot * 128:(cslot + 1) * 128], trb[:, 128:256]),
                 reads=[trn], writes=[("kTC", cslot)])
        f1 = {0: front1(0)}
        for t in range(NTC):
            if t + 1 < NTC:
                f1[t + 1] = front1(t + 1)
            back1(t, *f1.pop(t))
        wsb, wres = load_w(KB_OFF, W["w_in"])
        accq = {}
        for t in range(min(LOOK, NTC)):
            accq[t] = next_acc()
            proj_mm(t, wsb, wres, accq[t][0], accq[t][1])

        def front2(t):
            ps, psres = accq.pop(t)
            if t + LOOK < NTC:
                accq[t + LOOK] = next_acc()
                proj_mm(t + LOOK, wsb, wres, accq[t + LOOK][0], accq[t + LOOK][1])
            tmb, tmn = next_tm()
            if t % 2 == 0:
                P.op("act", lambda e, ps=ps, tmb=tmb: e.copy(tmb, ps), reads=[psres], writes=[tmn])
            else:
                P.op("dve", lambda e, ps=ps, tmb=tmb: e.tensor_copy(tmb, ps), reads=[psres], writes=[tmn])
            return tmb, tmn

        def back2(t, tmb, tmn):
            slot = 2 + t if t < NT else 20 + (t - NT)
            trb, trn = next_tr()

            def ftr(e, tmb=tmb, trb=trb):
                inst = None
                for j in range(4):
                    inst = e.transpose(trb[:, j * 128:(j + 1) * 128], tmb[:, j * 128:(j + 1) * 128], ident_b[:, :])
                return inst
            P.op("pe", ftr, reads=[tmn, "ident_b"], writes=[trn])
            if t % 2 == 0:
                P.op("dve", lambda e, slot=slot, trb=trb: e.tensor_copy(kTB[:, :, slot * 128:(slot + 1) * 128],
                                                                       trb[:, 0:512].rearrange("p (j n) -> p j n", j=4)),
                     reads=[trn], writes=[("kTB", slot)])
            else:
                P.op("act", lambda e, slot=slot, trb=trb: e.copy(kTB[:, :, slot * 128:(slot + 1) * 128],
                                                                trb[:, 0:512].rearrange("p (j n) -> p j n", j=4)),
                     reads=[trn], writes=[("kTB", slot)])
        f2 = {0: front2(0)}
        for t in range(NTC):
            if t + 1 < NTC:
                f2[t + 1] = front2(t + 1)
            back2(t, *f2.pop(t))
        wsb, wres = load_w(VB_OFF, W["w_in"])
        for t in range(NTC):
            ps, psres = next_acc()
            proj_mm(t, wsb, wres, ps, psres)
            slot = 2 + t if t < NT else 20 + (t - NT)
            if t % 2 == 0:
                P.op("act", lambda e, ps=ps, slot=slot: e.copy(vB[:, slot, :], ps), reads=[psres], writes=[("vB", slot)])
            else:
                P.op("dve", lambda e, ps=ps, slot=slot: e.tensor_copy(vB[:, slot, :], ps), reads=[psres], writes=[("vB", slot)])

    def phase_exchange(l):
        cc_out = cc_outs[l]
        PRE["qB"] = load_w(QOFF["B"], LW[l]["w_in"])
        ecp = [
            (cc_st[:, 4096:5120].rearrange("p (j n) -> p j n", j=4), kTB[:, :, 256:512], [("kTB", 2), ("kTB", 3)], "kBf"),
            (cc_st[:, 5120:6144].rearrange("p (t n) -> p t n", t=2), vB[:, 2:4, :], [("vB", 2), ("vB", 3)], "vBf"),
            (cc_st[:, 6144:6272], kTC[:, 128:256], [("kTC", 1)], "kCf"),
            (cc_st[:, 6272:6400], vC[:, 1, :], [("vC", 1)], "vCf"),
            (cc_st[:, 6400:7424].rearrange("p (j n) -> p j n", j=4), kTB[:, :, 2048:2304], [("kTB", 16), ("kTB", 17)], "kBl"),
            (cc_st[:, 7424:8448].rearrange("p (t n) -> p t n", t=2), vB[:, 16:18, :], [("vB", 16), ("vB", 17)], "vBl"),
            (cc_st[:, 8448:8576], kTC[:, 16 * 128:17 * 128], [("kTC", 16)], "kCl"),
            (cc_st[:, 8576:8704], vC[:, 16, :], [("vC", 16)], "vCl"),
        ]
        for (o_, i_, rd, nm) in ecp:
            P.op("pool", lambda e, o_=o_, i_=i_: e.tensor_copy(o_, i_), reads=rd, writes=[("cc_st", nm)])
        allcc = ([("cc_st", "kA", t) for t in range(NT)] + [("cc_st", "vA", t) for t in range(NT)] +
                 [("cc_st", e_[3]) for e_ in ecp])
        ccdeps = {"A": [("cc_st", "kA", t) for t in range(NT)] + [("cc_st", "vA", t) for t in range(NT)],
                  "F": [("cc_st", e_[3]) for e_ in ecp[0:4]], "L": [("cc_st", e_[3]) for e_ in ecp[4:8]]}
        for kk, (c0, c1) in CCP.items():
            P.dma("sp", lambda e, kk=kk, c0=c0, c1=c1: e.dma_start(out=cc_ins[kk][:, :], in_=cc_st[:, c0:c1]),
                  reads=ccdeps[kk] + ["tabregion"], writes=["cc_in" + kk], stream="ccin" + kk)
            P.dma("pool", lambda e, kk=kk: e.collective_compute("AllGather", ALU.bypass, replica_groups=[[0, 1, 2, 3], [4, 5, 6, 7]],
                                                                 ins=[cc_ins[kk].ap().opt()], outs=[cc_out[kk].ap().opt()]),
                  reads=["cc_in" + kk], writes=["cc_out" + kk], stream="cc%d%s" % (l, kk), cc=True)
        for k in range(3):
            P.dma("sp", lambda e, k=k: e.dma_start(out=stg[k], in_=cc_out["L"][k * 128:(k + 1) * 128, :]),
                  reads=["cc_outL", "cc_inA", "cc_inF", "cc_inL"], writes=[("stg", k)], stream="stg%d" % k)
            P.dma("sp", lambda e, k=k: e.dma_start(out=stg[3 + k], in_=cc_out["F"][(k + 1) * 128:(k + 2) * 128, :]),
                  reads=["cc_outF"], writes=[("stg", 3 + k)], stream="stg%d" % (3 + k))
        def pieces(base):
            return [
                (lambda a: a[:, 0:1024].rearrange("p (j n) -> p j n", j=4)),
                (lambda a: a[:, 1024:2048].rearrange("p (t n) -> p t n", t=2)),
                (lambda a: a[:, 2048:2176]),
                (lambda a: a[:, 2176:2304]),
            ]
        dsts_prev = [(kTB[:, :, 0:256], [("kTB", 0), ("kTB", 1)]), (vB[:, 0:2, :], [("vB", 0), ("vB", 1)]),
                     (kTC[:, 0:128], [("kTC", 0)]), (vC[:, 0, :], [("vC", 0)])]
        dsts_next = [(kTB[:, :, 2304:2560], [("kTB", 18), ("kTB", 19)]), (vB[:, 18:20, :], [("vB", 18), ("vB", 19)]),
                     (kTC[:, 17 * 128:18 * 128], [("kTC", 17)]), (vC[:, 17, :], [("vC", 17)])]
        for which, dsts in ((0, dsts_prev), (1, dsts_next)):
            for pi_, (dst, wr) in enumerate(dsts):
                view = pieces(0)[pi_]
                for k in range(3):
                    src_ = view(stg[which * 3 + k])
                    sc = selv[:, which * 3 + k:which * 3 + k + 1]
                    if k == 0:
                        P.op("dve", lambda e, dst=dst, src_=src_, sc=sc: e.tensor_scalar(dst, src_, sc, None, ALU.mult),
                             reads=[("stg", which * 3 + k), "selv"], writes=wr + ["halo_st"])
                    else:
                        P.op("dve", lambda e, dst=dst, src_=src_, sc=sc: e.scalar_tensor_tensor(dst, src_, sc, dst, ALU.mult, ALU.add),
                             reads=[("stg", which * 3 + k), "selv"] + wr, writes=wr + ["halo_st"])

    def attend(l, m, pair, q0, N, tiles, sinkcol=None):
        gi = MIXI[m] * 4 + pair
        nt = len(tiles)
        ak = ctr["att"]
        ctr["att"] += 1
        Tps, tres = (PS_T, "PS_T") if ak % 2 == 0 else (PS_X, "PS_X")
        ggk, gek = gg2[ak % 2], ge2[ak % 2]
        gres, eres = ("gg", ak % 2), ("ge", ak % 2)
        wgi = ctr["wg_cur"]

        def fg(e):
            inst = None
            for kc in range(8):
                inst = e.matmul(PS_G[:, 0:N], lhsT=wg[wgi][:, kc, :], rhs=hxT[:, kc, q0:q0 + N], start=(kc == 0), stop=(kc == 7))
            return inst
        P.op("pe", fg, reads=[("wg", wgi)] + [("hxT", t) for t in range(q0 // 128, (q0 + N) // 128)], writes=["PS_G"])
        P.op("act", lambda e: e.activation(gek[:, 0:N], PS_G[:, 0:N], AF.Exp, scale=-1.0), reads=["PS_G"], writes=[eres])
        P.op("dve", lambda e: e.tensor_scalar_add(gek[:, 0:N], gek[:, 0:N], 1.0), reads=[eres], writes=[eres])
        P.op("dve", lambda e: e.reciprocal(gek[:, 0:N], gek[:, 0:N]), reads=[eres], writes=[eres])
        P.op("dve", lambda e: e.tensor_tensor(ggk[:, 0:N], PS_G[:, 0:N], gek[:, 0:N], ALU.mult), reads=["PS_G", eres], writes=[gres])

        sbuf_i = []
        partial = any(("cols" in kt_) for kt_ in tiles)
        if partial:
            full = [kt_ for kt_ in tiles if "cols" not in kt_]
            rest = [kt_ for kt_ in tiles if "cols" in kt_]
            tiles = full[:1] + rest + full[1:]

        def emit_qk(i):
            kt = tiles[i]
            si = ctr["acc"] % 2
            ctr["acc"] += 1
            S = PS_S[si]
            sres = "PS_S%d" % si
            adds = kt.get("adds", [])

            adds = kt.get("adds", [])

            cl, ch = kt.get("cols", (0, N))

            def fqk(e, S=S, kt=kt, adds=adds, cl=cl, ch=ch):
                na = len(adds)
                e.matmul(S[:, cl:ch], lhsT=kt["kT"][0:64, :], rhs=qTm[0:64, pair, q0 + cl:q0 + ch], start=True, stop=(na == 0))
                inst = e.matmul(S[:, 512 + cl:512 + ch], lhsT=kt["kT"][64:128, :], rhs=qTm[64:128, pair, q0 + cl:q0 + ch],
                                start=True, stop=(na == 0))
                for ai, (c0, ncol, rx, ry, _r) in enumerate(adds):
                    lastf = (ai == na - 1)
                    e.matmul(S[:, c0:c0 + ncol], lhsT=ident_b[:, :], rhs=rx, start=False, stop=lastf)
                    inst = e.matmul(S[:, 512 + c0:512 + c0 + ncol], lhsT=ident_b[:, :], rhs=ry, start=False, stop=lastf)
                return inst
            rds = [kt["kres"], ("qTm", pair), "ident_b"] + [a[4] for a in adds]
            P.op("pe", fqk, reads=rds, writes=[sres])
            sbuf_i.append((S, sres))

        emit_qk(0)
        for i, kt in enumerate(tiles):
            if i + 1 < nt:
                emit_qk(i + 1)
            S, sres = sbuf_i[i]
            pi = ctr["pt"] % 2
            ctr["pt"] += 1
            Pt = PT[pi]
            cl, ch = kt.get("cols", (0, N))
            P.op("act", lambda e, S=S, Pt=Pt, cl=cl, ch=ch: e.activation(
                Pt[:, :, cl:ch], S[:, :].rearrange("p (h n) -> p h n", h=2)[:, :, cl:ch], AF.Exp),
                reads=[sres], writes=[("PT", pi)])

            def fpv(e, kt=kt, Pt=Pt, i=i, cl=cl, ch=ch):
                st, sp = (i == 0), (i == nt - 1)
                sk = partial
                e.matmul(Tps[0:64, cl:ch], lhsT=kt["v"][:, 0:64], rhs=Pt[:, 0, cl:ch], start=st, stop=sp, skip_group_check=sk)
                e.matmul(Tps[64:128, cl:ch], lhsT=kt["v"][:, 64:128], rhs=Pt[:, 1, cl:ch], start=st, stop=sp, tile_position=(0, 64),
                         skip_group_check=sk)
                e.matmul(PS_SM[0:64, cl:ch], lhsT=ones_b[:, 0:64], rhs=Pt[:, 0, cl:ch], start=st, stop=sp, skip_group_check=sk)
                return e.matmul(PS_SM[64:128, cl:ch], lhsT=ones_b[:, 64:128], rhs=Pt[:, 1, cl:ch], start=st, stop=sp,
                                tile_position=(0, 64), skip_group_check=sk)
            P.op("pe", fpv, reads=[kt["vres"], ("PT", pi), "ones_b"], writes=[tres, "PS_SM"])
        if sinkcol is not None:
            P.op("dve", lambda e: e.tensor_scalar_add(rr[:, 0:N], PS_SM[:, 0:N], esink[:, sinkcol:sinkcol + 1]),
                 reads=["PS_SM", "esink"], writes=["rr"])
            P.op("dve", lambda e: e.reciprocal(rr[:, 0:N], rr[:, 0:N]), reads=["rr"], writes=["rr"])
        else:
            P.op("dve", lambda e: e.reciprocal(rr[:, 0:N], PS_SM[:, 0:N]), reads=["PS_SM"], writes=["rr"])
        P.op("pool", lambda e: e.tensor_tensor(ggk[:, 0:N], ggk[:, 0:N], rr[:, 0:N], ALU.mult), reads=[gres, "rr"], writes=[gres])
        ui = ctr["uto"] % 2
        ctr["uto"] += 1
        P.op("dve", lambda e: e.tensor_tensor(uTo[ui][:, 0:N], Tps[:, 0:N], ggk[:, 0:N], ALU.mult),
             reads=[tres, gres], writes=[("uTo", ui)])
        t0 = q0 // 128
        ntile = N // 128
        dst = ut_t[t0:t0 + ntile, :, gi, :].rearrange("t f k -> f t k")
        P.dma("sp", lambda e: e.dma_start(out=dst, in_=uTo[ui][:, 0:N].rearrange("p (t k) -> p t k", k=128)),
              reads=[("uTo", ui)], writes=[("ut", t, gi) for t in range(t0, t0 + ntile)], stream="uto%d" % ui)
        if "uT" in dbg_t and l == layers[0]:
            P.dma("sp", lambda e: e.dma_start(out=dbg_t["uT"][gi, :, q0:q0 + N], in_=uTo[ui][:, 0:N]),
                  reads=[("uTo", ui)], writes=[("dbg_uT", gi, q0)], stream="dbg")

    def phase_mixer(l, m, need_ctx):
        W = LW[l]
        cc_out = cc_outs[l]
        ntq = NTC if need_ctx else NT
        wsb, wres = PRE.pop("q" + m) if ("q" + m) in PRE else load_w(QOFF[m], W["w_in"])
        if m == "B":
            P.dma("pool", lambda e: e.dma_start(out=bzb.rearrange("p h n -> p (h n)"), in_=W["bzb"]), writes=["bzb", "tabregion", "halo_st"], stream="tb0")
            P.dma("pool", lambda e: e.dma_start(out=mzb, in_=mzb_d), writes=["mzb", "halo_st"], stream="tb1")
            P.dma("pool", lambda e: e.dma_start(out=mvbf.rearrange("p t n -> p (t n)"), in_=mvbf_d), writes=["mvbf", "halo_st"], stream="tb2")
            P.dma("pool", lambda e: e.dma_start(out=mvbl.rearrange("p t n -> p (t n)"), in_=mvbl_d), writes=["mvbl", "halo_st"], stream="tb3")
        if m == "C":
            P.dma("pool", lambda e: e.dma_start(out=mzc, in_=mzc_d), writes=["mzc"], stream="tb0")
            P.dma("pool", lambda e: e.dma_start(out=mvcf, in_=mvcf_d), writes=["mvcf"], stream="tb1")
            P.dma("pool", lambda e: e.dma_start(out=mvcl, in_=mvcl_d), writes=["mvcl"], stream="tb2")
            P.dma("sp", lambda e: e.dma_start(out=esink[:, :], in_=W["sink"]), writes=["esink"], stream="misc0")
            P.op("act", lambda e: e.activation(esink[:, :], esink[:, :], AF.Exp), reads=["esink"], writes=["esink"])
        if m == "A":
            P.dma("sp", lambda e: e.dma_start(out=kTA.rearrange("p (r n) -> p r n", r=RPB),
                                             in_=cc_out["A"][:, 0:2048].rearrange("(r p) n -> p r n", p=128)),
                  reads=["cc_outA"], writes=["kTA"], stream="ldA0")
            P.dma("sp", lambda e: e.dma_start(out=vA.rearrange("p (r t) n -> p r (t n)", r=RPB),
                                             in_=cc_out["A"][:, 2048:4096].rearrange("(r p) n -> p r n", p=128)),
                  reads=["cc_outA"], writes=["vA"], stream="ldA1")
        accq = {}
        for t in range(min(LOOK, ntq)):
            accq[t] = next_acc()
            proj_mm(t, wsb, wres, accq[t][0], accq[t][1])

        def qfront(t):
            ps, psres = accq.pop(t)
            if t + LOOK < ntq:
                accq[t + LOOK] = next_acc()
                proj_mm(t + LOOK, wsb, wres, accq[t + LOOK][0], accq[t + LOOK][1])
            tmb, tmn = next_tm()
            if m == "A":
                a_, an, t_, tn, u_, un, st_, stn = next_tset()
                rms_heads(ps, psres, 8, qn_bc, "qn_bc", a_, an, 0.125, t_, tn, st_, stn)
                rope_ops(a_, [an], 8, t, tmb, tmn, t_, tn, u_, un)
            elif m == "C":
                a_, an, t_, tn, u_, un, st_, stn = next_tset()
                P.op("act", lambda e, ps=ps, a_=a_: e.mul(a_, ps, 0.125), reads=[psres], writes=[an])
                rope_ops(a_, [an], 8, t, tmb, tmn, t_, tn, u_, un)
            else:
                P.op("act", lambda e, ps=ps, tmb=tmb: e.mul(tmb, ps, 0.125), reads=[psres], writes=[tmn])
            return tmb, tmn

        def qback(t, tmb, tmn):
            trb, trn = next_tr()

            def ftr(e, tmb=tmb, trb=trb):
                inst = None
                for j in range(4):
                    inst = e.transpose(trb[:, j * 128:(j + 1) * 128], tmb[:, j * 128:(j + 1) * 128], ident_b[:, :])
                return inst
            P.op("pe", ftr, reads=[tmn, "ident_b"], writes=[trn])
            if True:
                P.op("act", lambda e, t=t, trb=trb: e.copy(qTm[:, :, t * 128:(t + 1) * 128], trb[:, 0:512].rearrange("p (j n) -> p j n", j=4)),
                     reads=[trn], writes=[("qTm", j) for j in range(4)])
            else:
                P.op("dve", lambda e, t=t, trb=trb: e.tensor_copy(qTm[:, :, t * 128:(t + 1) * 128], trb[:, 0:512].rearrange("p (j n) -> p j n", j=4)),
                     reads=[trn], writes=[("qTm", j) for j in range(4)])
        fq = {0: qfront(0)}
        for t in range(ntq):
            if t + 1 < ntq:
                fq[t + 1] = qfront(t + 1)
            qback(t, *fq.pop(t))
        P.barrier()
        nxtm = {"B": "C", "C": "A"}.get(m)
        if nxtm is not None:
            PRE["q" + nxtm] = load_w(QOFF[nxtm], W["w_in"])
        if m == "A":
            for j in range(3):
                P.dma("pool", lambda e, j=j: e.dma_start(out=wout_sb[:, 4 * j:4 * j + 4, :],
                                                          in_=W["w_out"][j * 512:(j + 1) * 512, :].rearrange("(j p) n -> p j n", p=128)),
                      writes=[("wout", j)], stream="wout%d" % j)
            PRE["gate4"] = load_w(4 * 512, W["ada_w"])
            PRE["gate5"] = load_w(5 * 512, W["ada_w"])
        wgq = {0: load_w(G_OFF + (MIXI[m] * 4) * 128, W["w_in"], slot_kind="wg", ncols=128)}
        for pair in range(4):
            if pair + 1 < 4:
                wgq[pair + 1] = load_w(G_OFF + (MIXI[m] * 4 + pair + 1) * 128, W["w_in"], slot_kind="wg", ncols=128)
            ctr["wg_cur"] = wgq[pair][1][1]
            chunks = [(c * 512, 512, c) for c in range(4)]
            if need_ctx:
                chunks.append((TOK, 256, None))
            for (q0, N, c) in chunks:
                tiles = []
                if m == "A":
                    ctxA = [dict(kT=kTA_ctx[:, j * 128:(j + 1) * 128], kres=("kTA_ctx", NT + j), v=vA_ctx[:, j, :], vres=("vA_ctx", NT + j))
                            for j in range(2)]
                    if c is not None:
                        for kt_i in range(64):
                            tiles.append(dict(kT=kTA[:, kt_i * 128:(kt_i + 1) * 128], kres="kTA", v=vA[:, kt_i, :], vres="vA"))
                    tiles += ctxA
                    sinkcol = None
                elif m == "C":
                    ctxC = [dict(kT=kTC[:, s_ * 128:(s_ + 1) * 128], kres=("kTC", s_), v=vC[:, s_, :], vres=("vC", s_)) for s_ in (18, 19)]
                    if c is not None:
                        for tt in range(6):
                            s_ = 4 * c + tt
                            if c == 0 and tt == 0:
                                rhs, rres = mvcf, "mvcf"
                            elif c == 3 and tt == 5:
                                rhs, rres = mvcl, "mvcl"
                            else:
                                rhs, rres = mzc[:, (5 - tt) * 128:(9 - tt) * 128], "mzc"
                            blo, bhi = max(0, tt - 2), min(3, tt)
                            cl_, ch_ = blo * 128, (bhi + 1) * 128
                            rsl = rhs[:, cl_:ch_]
                            tiles.append(dict(kT=kTC[:, s_ * 128:(s_ + 1) * 128], kres=("kTC", s_), v=vC[:, s_, :], vres=("vC", s_),
                                              adds=[(cl_, ch_ - cl_, rsl, rsl, rres)], cols=(cl_, ch_)))
                    tiles += ctxC
                    sinkcol = pair
                else:
                    ctxB = [dict(kT=kTB[:, pair, s_ * 128:(s_ + 1) * 128], kres=("kTB", s_), v=vB[:, s_, pair * 128:(pair + 1) * 128], vres=("vB", s_))
                            for s_ in (20, 21)]
                    if c is not None:
                        for tt in range(8):
                            s_ = 4 * c + tt
                            blo, bhi = max(0, tt - 4), min(3, tt)
                            if c == 0:
                                blo = max(0, tt - 5)
                            if c == 3:
                                bhi = min(3, tt + 1)
                            cl_, ch_ = blo * 128, (bhi + 1) * 128
                            bx = bzb[:, 2 * pair, (7 - tt) * 128 + cl_:(7 - tt) * 128 + ch_]
                            by = bzb[:, 2 * pair + 1, (7 - tt) * 128 + cl_:(7 - tt) * 128 + ch_]
                            adds = [(cl_, ch_ - cl_, bx, by, "bzb")]

                            def mrange(lo, hi, tab, res_, base):
                                a_, b_ = max(lo, cl_), min(hi, ch_)
                                if b_ > a_:
                                    sl = tab[:, a_ - base:b_ - base]
                                    adds.append((a_, b_ - a_, sl, sl, res_))
                            gfull = mzb[:, (7 - tt) * 128:(11 - tt) * 128]
                            if c == 0:
                                mrange(0, 256, mvbf[:, tt, :], "mvbf", 0)
                                mrange(256, 512, gfull, "mzb", 0)
                            elif c == 3:
                                mrange(0, 256, gfull, "mzb", 0)
                                mrange(256, 512, mvbl[:, tt, :], "mvbl", 256)
                            else:
                                mrange(0, 512, gfull, "mzb", 0)
                            tiles.append(dict(kT=kTB[:, pair, s_ * 128:(s_ + 1) * 128], kres=("kTB", s_),
                                              v=vB[:, s_, pair * 128:(pair + 1) * 128], vres=("vB", s_), adds=adds, cols=(cl_, ch_)))
                    tiles += ctxB
                    sinkcol = None
                attend(l, m, pair, q0, N, tiles, sinkcol)

    def phase_wout(l, src, dst_x, need_ctx, last):
        W = LW[l]
        phase_gate(l, last)
        if last:
            P.op("dve", lambda e: e.memset(ss[:, :], 0.0), writes=[("ss", t) for t in range(NTC)])
        ntw = NTC if need_ctx else NT

        def wloads(t):
            i = t % 2
            P.dma("sp", lambda e, t=t, i=i: e.dma_start(out=utt[i], in_=ut_t[t, :, :, :]),
                  reads=[("ut", t, g) for g in range(12)], writes=[("utt", i)], stream="utt%d" % i)
            P.dma("sp", lambda e, t=t, i=i: e.dma_start(out=xt[i], in_=src[t * 128:(t + 1) * 128, :]),
                  writes=[("xt", i)], stream="xt%d" % i)
        wloads(0)
        for t in range(ntw):
            i = t % 2
            w = 0 if t < NT else 1
            if t + 1 < ntw:
                wloads(t + 1)
            def f(e, i=i):
                inst = None
                for half in range(2):
                    for j in range(12):
                        inst = e.matmul(PS_S[i][:, half * 512:(half + 1) * 512], lhsT=utt[i][:, j, :],
                                        rhs=wout_sb[:, j, half * 512:(half + 1) * 512], start=(j == 0), stop=(j == 11))
                return inst
            P.op("pe", f, reads=[("utt", i)] + [("wout", j) for j in range(3)], writes=["PS_S%d" % i])
            P.op("dve", lambda e, i=i, w=w: e.tensor_tensor(xn[i], PS_S[i][:, :], gate_bc[w], ALU.mult),
                 reads=["PS_S%d" % i, "gate_bc%d" % w], writes=[("xn", i)])
            P.op("pool", lambda e, i=i: e.tensor_tensor(xn[i], xn[i], xt[i], ALU.add),
                 reads=[("xn", i), ("xt", i)], writes=[("xn", i)])
            if not last:
                P.dma("sp", lambda e, t=t, i=i: e.dma_start(out=dst_x[t * 128:(t + 1) * 128, :], in_=xn[i]),
                      reads=[("xn", i)], writes=[("x1", t)], stream="xo%d" % i)
            else:
                if final_norm:
                    P.op("act", lambda e, t=t, i=i: e.activation(junk, xn[i], AF.Square, scale=1.0 / 32.0,
                                                                accum_out=ss[:, t:t + 1]),
                         reads=[("xn", i)], writes=["junk", ("ss", t)])
                    P.op("act", lambda e, t=t: e.activation(rstd[:, t:t + 1], ss[:, t:t + 1], AF.Ln, bias=eps_c[:, 0:1]),
                         reads=[("ss", t)], writes=[("rstd", t)])
                    P.op("act", lambda e, t=t: e.activation(rstd[:, t:t + 1], rstd[:, t:t + 1], AF.Exp, scale=-0.5),
                         reads=[("rstd", t)], writes=[("rstd", t)])
                    P.op("act", lambda e, t=t, i=i: e.activation(xn[i], xn[i], AF.Copy, scale=rstd[:, t:t + 1]),
                         reads=[("xn", i), ("rstd", t)], writes=[("xn", i)])
                    P.op("pool", lambda e, i=i: e.tensor_tensor(xn[i], xn[i], fnw_sb, ALU.mult),
                         reads=[("xn", i), "fnw"], writes=[("xn", i)])
                P.dma("sp", lambda e, t=t, i=i: e.dma_start(out=out_d[t * 128:(t + 1) * 128, :], in_=xn[i]),
                      reads=[("xn", i)], writes=[("out", t)], stream="xo%d" % i)

    def dbg_dump(name, src_ap, reads):
        if name in dbg_t:
            P.dma("sp", lambda e: e.dma_start(out=dbg_t[name], in_=src_ap), reads=reads, writes=["dbg_" + name], stream="dbg")

    setup()
    src = xin
    stopped = False
    for li, l in enumerate(layers):
        last = (li == len(layers) - 1)
        need_ctx = not last
        dst_x = x1_t.ap()
        phase_mod(l)
        prefetch_kvac(l)
        phase_norm(l, src)
        P.barrier()
        if li == 0:
            dbg_dump("hxT", hxT[:, :, :].rearrange("p k n -> p (k n)"), [("hxT", t) for t in range(NTC)])
        if stop_after == "norm":
            stopped = True
            break
        phase_kvproj(l)
        phase_exchange(l)
        if li == 0:
            dbg_dump("kvr", KVR[:, :], [("kTB", s_) for s_ in range(22)] + [("vB", s_) for s_ in range(22)])
            dbg_dump("kc", kTC[:, :], [("kTC", s_) for s_ in range(20)])
            dbg_dump("vc", vC[:, :, :].rearrange("p t n -> p (t n)"), [("vC", s_) for s_ in range(20)])
            dbg_dump("ccst", cc_st, ["cc_inA", "cc_inF", "cc_inL"])
        if stop_after == "kv":
            stopped = True
            break
        for m in ("B", "C", "A"):
            if m == "A":
                P.barrier()
            phase_mixer(l, m, need_ctx)
            if li == 0:
                dbg_dump("qT" + m, qTm.rearrange("p j n -> p (j n)"), [("qTm", j) for j in range(4)])
            P.barrier()
            if stop_after == "mix" + m:
                stopped = True
                break
        if stopped:
            break
        phase_wout(l, src, dst_x, need_ctx, last)
        P.barrier()
        src = x1_t.ap()
    if stopped:
        P.op("pool", lambda e: e.memset(xn[0], 0.0), writes=[("xn", 0)])
        P.dma("sp", lambda e: e.dma_start(out=out_d[0:128, :], in_=xn[0]), reads=[("xn", 0)], writes=["outz"], stream="xo0")
        P.barrier()

    print("sbuf bytes remaining:", nc.sbuf_bytes_remaining)
    with nc.Block() as block:
        P.emit(block)
    return nc


def make_in_maps(inputs, layers=(0, 1)):
    x = np.asarray(inputs["x"], np.float32)
    c = np.asarray(inputs["c"], np.float32)
    ctx = np.asarray(inputs["ctx"], np.float32)
    c_ctx = np.asarray(inputs["c_ctx"], np.float32)
    shared = {}
    for l in layers:
        shared["w_in%d" % l] = _perm_w_in(np.asarray(inputs["w_in"][l], np.float32))
        shared["w_out%d" % l] = _perm_w_out(np.asarray(inputs["w_out"][l], np.float32))
        shared["ada_w%d" % l] = np.ascontiguousarray(np.asarray(inputs["ada_w"][l], np.float32))
        ab = np.asarray(inputs["ada_b"][l], np.float32)
        shared["adabT%d" % l] = np.ascontiguousarray(ab[0:2048].reshape(16, 128).T)
        shared["adabg%d" % l] = np.ascontiguousarray(ab[2048:3072].reshape(1, D))
        shared["normT%d" % l] = np.ascontiguousarray(np.asarray(inputs["norm_w"][l], np.float32).reshape(8, 128).T)
        shared["qn%d" % l] = np.asarray(inputs["q_norm_a"][l], np.float32).reshape(1, 64).copy()
        shared["kn%d" % l] = np.asarray(inputs["k_norm_a"][l], np.float32).reshape(1, 64).copy()
        shared["bzb%d" % l] = _b_bias(np.asarray(inputs["rpb_b"][l], np.float32)).reshape(128, 8 * 11 * 128)
        sk = np.asarray(inputs["sink_c"][l], np.float32)
        st = np.zeros((128, 4), np.float32)
        for j in range(4):
            st[0:64, j] = sk[j]
            st[64:128, j] = sk[j + 4]
        shared["sink%d" % l] = st
    shared["fnw"] = np.asarray(inputs["final_norm_w"], np.float32).reshape(1, D).copy()
    maps = []
    for core in range(NCORE):
        b, r = core // RPB, core % RPB
        m = dict(shared)
        m["xin"] = np.ascontiguousarray(np.concatenate([x[b, r * TOK:(r + 1) * TOK, :], ctx[b]], axis=0))
        cv = np.zeros((128, 8, 2), np.float32)
        cv[:, :, 0] = c[b].reshape(8, 128).T
        cv[:, :, 1] = c_ctx.reshape(8, 128).T
        m["cvec"] = cv.reshape(128, 16)
        m["rope"] = _rope_tables(r).reshape(128, NTC * 2 * 64)
        m["ident"] = np.eye(128, dtype=np.float32)
        sv = np.zeros((128, 8), np.float32)
        if r > 0:
            sv[:, r - 1] = 1.0
        if r < RPB - 1:
            sv[:, 3 + r] = 1.0
        m["selv"] = sv
        gen, first, last = _b_masks(r)
        m["mzb"], m["mvbf"], m["mvbl"] = gen, first, last
        gen, first, last = _c_masks(r)
        m["mzc"], m["mvcf"], m["mvcl"] = gen, first, last
        maps.append(m)
    return maps


_NC_CACHE = {}


def kernel(**inputs):
    if "full" not in _NC_CACHE:
        _NC_CACHE["full"] = build_program()
    nc = _NC_CACHE["full"]
    maps = make_in_maps(inputs)
    res = run_bass_kernel_spmd(nc, maps, core_ids=list(range(NCORE)))
    out = np.zeros((2, SEQ, D), np.float32)
    for core in range(NCORE):
        b, r = core // RPB, core % RPB
        out[b, r * TOK:(r + 1) * TOK, :] = res.results[core]["out"]
    return out
```

```python
import numpy as np
import concourse.bass as bass
import concourse.mybir as mybir
from concourse.bass_utils import run_bass_kernel_spmd

F32 = mybir.dt.float32
BF16 = mybir.dt.bfloat16
I32 = mybir.dt.int32
AF = mybir.ActivationFunctionType
ALU = mybir.AluOpType
AX = mybir.AxisListType

D = 1024
SEQ = 8192
NCORE = 8
RPB = 4
TOK = SEQ // RPB
NT = TOK // 128
NTC = NT + 2
NTOK = NTC * 128
EPS = 1e-6
NEG = -32768.0
CCW = 8704
IN_W = 4608
QOFF = {"A": 0, "B": 512, "C": 1024}
KVAC_OFF = 1536
KB_OFF = 2048
VB_OFF = 2560
G_OFF = 3072
MIXI = {"A": 0, "B": 1, "C": 2}

ENG_CHUNK = 30000


class Op:
    __slots__ = ("eng", "fn", "deps", "signal", "seq", "stream", "sid", "ninst", "cc")

    def __init__(self, eng, fn, stream=None, ninst=1, cc=False):
        self.eng = eng
        self.fn = fn
        self.deps = []
        self.signal = False
        self.seq = None
        self.stream = stream
        self.sid = None
        self.ninst = ninst
        self.cc = cc


class Prog:
    ENGS = ("pe", "act", "dve", "pool", "sp")
    SYNC_SAME = ("act", "dve", "pool")

    def __init__(self, nc):
        self.nc = nc
        self.ops = {e: [] for e in self.ENGS}
        self.order = []
        self.lastw = {}
        self.readers = {}
        self.last_on_stream = {}

    def _deps(self, op, reads, writes):
        reads = list(reads)
        writes = list(writes)
        for r in list(reads):
            if isinstance(r, str) and r.startswith("PS_"):
                reads.remove(r)
                if r not in writes:
                    writes.append(r)
        deps = set()
        for r in reads:
            w = self.lastw.get(r)
            if w is not None:
                deps.add(w)
        for w_ in writes:
            w = self.lastw.get(w_)
            if w is not None:
                deps.add(w)
            for rd in self.readers.get(w_, ()):
                deps.add(rd)
        deps.discard(op)
        op.deps = list(deps)
        for r in reads:
            self.readers.setdefault(r, []).append(op)
        for w_ in writes:
            self.lastw[w_] = op
            self.readers[w_] = []

    def op(self, eng, fn, reads=(), writes=()):
        o = Op(eng, fn)
        self._deps(o, reads, writes)
        self.ops[eng].append(o)
        self.order.append(o)
        return o

    def dma(self, queue, fn, reads=(), writes=(), stream=None, ninst=1, cc=False):
        o = Op(queue, fn, stream=stream, ninst=ninst, cc=cc)
        self._deps(o, reads, writes)
        prev = self.last_on_stream.get(stream)
        if prev is not None and prev not in o.deps:
            o.deps.append(prev)
        self.last_on_stream[stream] = o
        self.ops[queue].append(o)
        self.order.append(o)
        return o

    def barrier(self):
        deps = set()
        for r, w in self.lastw.items():
            if w is not None:
                deps.add(w)
        for r, rl in self.readers.items():
            for rd in rl:
                deps.add(rd)
        deps = list(deps)
        for e in self.ENGS:
            o = Op(e, None)
            o.deps = deps
            self.ops[e].append(o)
            self.order.append(o)
        self.lastw = {}
        self.readers = {}

    def emit(self, block):
        nc = self.nc
        semd = {}

        def sems(sid):
            if sid not in semd:
                semd[sid] = nc.alloc_semaphore("s_%s_%d" % sid)
            return semd[sid]

        for o in self.order:
            for d in o.deps:
                if d.stream is not None:
                    d.signal = True
                elif d.eng != o.eng or o.eng in self.SYNC_SAME or o.stream is not None:
                    d.signal = True
        cnt = {e: 0 for e in self.ENGS}
        scnt = {}
        for o in self.order:
            if o.stream is not None:
                if o.cc:
                    scnt[o.stream] = scnt.get(o.stream, 0) + 1
                    assert scnt[o.stream] == 1, "one collective per stream"
                    o.sid = ("dma_" + o.stream, 0)
                    o.seq = 1
                else:
                    scnt[o.stream] = scnt.get(o.stream, 0) + o.ninst
                    o.sid = ("dma_" + o.stream, 0)
                    o.seq = 16 * scnt[o.stream]
            elif o.signal:
                c = cnt[o.eng]
                o.sid = (o.eng, c // ENG_CHUNK)
                o.seq = c % ENG_CHUNK + 1
                cnt[o.eng] = c + 1

        def run_engine(ename, eng):
            known = {}
            for o in self.ops[ename]:
                need = {}
                for d in o.deps:
                    if d.seq is None:
                        continue
                    if (d.stream is None and d.eng == ename and o.stream is None
                            and ename not in self.SYNC_SAME):
                        continue
                    if need.get(d.sid, 0) < d.seq:
                        need[d.sid] = d.seq
                for sid, v in need.items():
                    if known.get(sid, 0) >= v:
                        continue
                    eng.wait_ge(sems(sid), v)
                    known[sid] = v
                if o.fn is None:
                    continue
                inst = o.fn(eng)
                if o.cc:
                    inst.then_inc(sems(o.sid))
                elif o.stream is not None:
                    insts = inst if isinstance(inst, (list, tuple)) else [inst]
                    assert len(insts) == o.ninst
                    for it in insts:
                        it.then_inc(sems(o.sid), 16)
                elif o.signal:
                    inst.then_inc(sems(o.sid), 1)

        @block.tensor
        def _(e):
            run_engine("pe", e)

        @block.scalar
        def _(e):
            run_engine("act", e)

        @block.vector
        def _(e):
            run_engine("dve", e)

        @block.gpsimd
        def _(e):
            run_engine("pool", e)

        @block.sync
        def _(e):
            run_engine("sp", e)


PAIR_HEADS = [0, 4, 1, 5, 2, 6, 3, 7]


def _perm_heads(w, off, nheads=8):
    cols = []
    for h in PAIR_HEADS:
        cols.append(w[..., off + h * 64: off + (h + 1) * 64])
    return np.concatenate(cols, axis=-1)


def _perm_w_in(w):
    oqa, oka, ova = 0, 512, 640
    oqb, okb, ovb = 768, 1280, 1792
    oqc, okc, ovc = 2304, 2816, 2944
    og = 3072
    parts = [
        _perm_heads(w, oqa), _perm_heads(w, oqb), _perm_heads(w, oqc),
        w[:, oka:oka + 128], w[:, okc:okc + 128], w[:, ova:ova + 128], w[:, ovc:ovc + 128],
        _perm_heads(w, okb), _perm_heads(w, ovb),
        _perm_heads(w, og), _perm_heads(w, og + 512), _perm_heads(w, og + 1024),
    ]
    return np.ascontiguousarray(np.concatenate(parts, axis=1))


def _perm_w_out(w):
    rows = []
    for m in range(3):
        for h in PAIR_HEADS:
            rows.append(w[m * 512 + h * 64: m * 512 + (h + 1) * 64, :])
    return np.ascontiguousarray(np.concatenate(rows, axis=0))


def _rope_tables(rank):
    t = (rank * TOK + np.arange(TOK)).astype(np.int32)
    rows = (t // 64).astype(np.float32)
    cols = (t % 64).astype(np.float32)
    nf = 16
    freq = (np.float32(10000.0) ** (-np.arange(nf, dtype=np.float32) / np.float32(nf))).astype(np.float32)
    ang = np.concatenate([rows[:, None] * freq, cols[:, None] * freq], axis=-1).astype(np.float32)
    c = np.cos(ang).astype(np.float32)
    s = np.sin(ang).astype(np.float32)
    c = np.concatenate([c, np.ones((256, 32), np.float32)], axis=0)
    s = np.concatenate([s, np.zeros((256, 32), np.float32)], axis=0)
    C2 = np.concatenate([c, c], axis=1)
    S2 = np.concatenate([-s, s], axis=1)
    tab = np.stack([C2, S2], axis=1)
    tab = tab.reshape(NTC, 128, 2, 64).transpose(1, 0, 2, 3)
    return np.ascontiguousarray(tab)


def _b_valid(i_q, i_k):
    q = np.arange(128)
    k = np.arange(128)
    rq = 2 * i_q + q // 64
    cq = q % 64
    rk = 2 * i_k + k // 64
    ck = k % 64
    rs = np.clip(rq - 4, 0, 120)
    cs = np.clip(cq - 8, 0, 48)
    ok = ((rk[:, None] >= rs[None, :]) & (rk[:, None] < rs[None, :] + 8) &
          (ck[:, None] >= cs[None, :]) & (ck[:, None] < cs[None, :] + 16) &
          (rk[:, None] >= 0) & (rk[:, None] < 128))
    return np.where(ok, np.float32(0.0), np.float32(NEG)).astype(np.float32)


def _b_masks(rank):
    gen = np.full((128, 11, 128), NEG, np.float32)
    for s in range(11):
        dl = 5 - s
        if abs(dl) <= 2:
            gen[:, s, :] = _b_valid(30, 30 + dl)
    first = np.zeros((128, 8, 2, 128), np.float32)
    last = np.zeros((128, 8, 2, 128), np.float32)
    for t in range(8):
        for b in range(2):
            cg = 4 * rank + 0
            first[:, t, b, :] = _b_valid(4 * cg + b, 4 * cg - 2 + t)
            cg = 4 * rank + 3
            last[:, t, b, :] = _b_valid(4 * cg + 2 + b, 4 * cg - 2 + t)
    return gen.reshape(128, 11 * 128), first.reshape(128, 8 * 256), last.reshape(128, 8 * 256)


def _b_bias(rpb_l):
    out = np.zeros((128, 8, 11, 128), np.float32)
    k = np.arange(128)
    q = np.arange(128)
    kr, kc = k // 64, k % 64
    qr, qc = q // 64, q % 64
    for s in range(11):
        dl = 5 - s
        dr = 2 * dl + kr[:, None] - qr[None, :] + 7
        dc = kc[:, None] - qc[None, :] + 15
        ok = (dr >= 0) & (dr <= 14) & (dc >= 0) & (dc <= 30)
        drc = np.clip(dr, 0, 14)
        dcc = np.clip(dc, 0, 30)
        for hi, h in enumerate(PAIR_HEADS):
            vals = rpb_l[h][drc, dcc]
            out[:, hi, s, :] = np.where(ok, vals, np.float32(0.0))
    return np.ascontiguousarray(out.reshape(128, 8, 11 * 128))


def _c_masks(rank):
    k = np.arange(128)[:, None]
    q = np.arange(128)[None, :]
    d1 = np.where(k <= q, 0.0, NEG).astype(np.float32)
    dm1 = np.where(k >= q, 0.0, NEG).astype(np.float32)
    d0 = np.zeros((128, 128), np.float32)
    M = np.full((128, 128), NEG, np.float32)
    gen = np.stack([M, M, M, d1, d0, dm1, M, M, M], axis=1).reshape(128, 9 * 128)
    first = np.stack([dm1 if rank > 0 else M, M, M, M], axis=1).reshape(128, 512)
    last = np.stack([M, M, M, d1 if rank < RPB - 1 else M], axis=1).reshape(128, 512)
    return gen, first, last


def build_program(layers=(0, 1), final_norm=True, dbg=None, stop_after=None):
    nc = bass.Bass("TRN2", target_bir_lowering=False)
    P = Prog(nc)
    dbg = dbg or ()

    def din(name, shape, dt=F32):
        return nc.dram_tensor(name, list(shape), dt, kind="ExternalInput").ap()

    xin = din("xin", [NTOK, D])
    cvec = din("cvec", [128, 16])
    rope_d = din("rope", [128, NTC * 2 * 64])
    selv_d = din("selv", [128, 8])
    fnw_d = din("fnw", [1, D])
    ident_d = din("ident", [128, 128])
    mzb_d = din("mzb", [128, 11 * 128])
    mvbf_d = din("mvbf", [128, 2048])
    mvbl_d = din("mvbl", [128, 2048])
    mzc_d = din("mzc", [128, 9 * 128])
    mvcf_d = din("mvcf", [128, 512])
    mvcl_d = din("mvcl", [128, 512])
    LW = {}
    for l in layers:
        LW[l] = dict(
            w_in=din("w_in%d" % l, [D, IN_W]),
            w_out=din("w_out%d" % l, [1536, D]),
            ada_w=din("ada_w%d" % l, [D, 3 * D]),
            adabT=din("adabT%d" % l, [128, 16]),
            adabg=din("adabg%d" % l, [1, D]),
            normT=din("normT%d" % l, [128, 8]),
            qn=din("qn%d" % l, [1, 64]),
            kn=din("kn%d" % l, [1, 64]),
            bzb=din("bzb%d" % l, [128, 8 * 11 * 128]),
            sink=din("sink%d" % l, [128, 4]),
        )
    out_d = nc.dram_tensor("out", [TOK, D], F32, kind="ExternalOutput").ap()
    dbg_t = {}
    for name, shape, dt in dbg:
        dbg_t[name] = nc.dram_tensor("dbg_" + name, list(shape), dt, kind="ExternalOutput").ap()

    x1_t = nc.dram_tensor("x1_scr", [NTOK, D], F32)
    CCP = {"F": (4096, 6400), "L": (6400, 8704), "A": (0, 4096)}
    cc_ins = {k: nc.dram_tensor("cc_in%s" % k, [128, c1 - c0], BF16) for k, (c0, c1) in CCP.items()}
    cc_outs = {l: {k: nc.dram_tensor("cc_out%d%s" % (l, k), [RPB * 128, c1 - c0], BF16) for k, (c0, c1) in CCP.items()}
               for l in layers}
    ut_t = nc.dram_tensor("ut_scr", [NTC, 128, 12, 128], BF16)

    def sb(name, shape, dt):
        return nc.alloc_sbuf_tensor(name, list(shape), dt)

    hxT = sb("hxT", [128, 8, NTOK], BF16)
    hx_f = hxT.bitcast(F32)
    hx_f2 = hx_f[:, :, :].rearrange("p a b -> p (a b)")
    gate_bc = [hx_f2[:, 0:1024], hx_f2[:, 1024:2048]]
    adabg = hx_f2[:, 2048:3072]
    fnw_sb = hx_f2[:, 3072:4096]
    wst = [sb("wst%d" % i, [128, 8, 512], BF16) for i in range(2)]
    wg = [sb("wg%d" % i, [128, 8, 128], BF16) for i in range(2)]
    rope = sb("rope_sb", [128, NTC, 2, 64], F32)
    ident_f = sb("ident_f", [128, 128], F32)
    ident_b = sb("ident_b", [128, 128], BF16)
    ones_b = sb("ones_b", [128, 128], BF16)
    cs_f = sb("cs_f", [128, 16], F32)
    cs_e = sb("cs_e", [128, 16], F32)
    cs_b = sb("cs_b", [128, 8, 2], BF16)
    modT = sb("modT", [128, 16, 2], F32)
    adabT = sb("adabT", [128, 16], F32)
    normT = sb("normT", [128, 8], F32)
    A1T = sb("A1T", [128, 8, 2], F32)
    qn_bc = sb("qn_bc", [128, 64], F32)
    kn_bc = sb("kn_bc", [128, 64], F32)
    ss = sb("ss", [128, NTC], F32)
    rstd = sb("rstd", [128, NTC], F32)
    selv = sb("selv_sb", [128, 8], F32)
    kTC = sb("kTC", [128, 20 * 128], BF16)
    vC = sb("vC", [128, 20, 128], BF16)
    kTA_ctx = sb("kTA_ctx", [128, 256], BF16)
    vA_ctx = sb("vA_ctx", [128, 2, 128], BF16)
    esink = sb("esink", [128, 4], F32)
    sm4 = sb("sm4", [128, 16], F32)
    eps_c = sb("eps_c", [128, 1], F32)
    sm4x = sb("sm4x", [128, 48], F32)
    KVR = sb("KVR", [128, 22528], BF16)
    kTB = KVR[:, 0:11264].rearrange("p (j n) -> p j n", j=4)
    vB = KVR[:, 11264:22528].rearrange("p (t n) -> p t n", t=22)
    kTA = KVR[:, 0:64 * 128]
    vA = KVR[:, 8192:8192 + 64 * 128].rearrange("p (t n) -> p t n", t=64)
    ARN = sb("ARN", [128, 26624], BF16)
    qTm = ARN[:, 0:4 * NTOK].rearrange("p (j n) -> p j n", j=4)
    TB0 = 4 * NTOK
    cc_st = ARN[:, TB0:TB0 + CCW]
    stg = [ARN[:, TB0 + k * 2304:TB0 + (k + 1) * 2304] for k in range(3)] + [ARN[:, 17920 + k * 2304:17920 + (k + 1) * 2304] for k in range(3)]
    bzb = ARN[:, TB0:TB0 + 8 * 1408].rearrange("p (h n) -> p h n", h=8)
    mzb = ARN[:, TB0 + 11264:TB0 + 11264 + 1408]
    mvbf = ARN[:, TB0 + 12672:TB0 + 12672 + 2048].rearrange("p (t n) -> p t n", t=8)
    mvbl = ARN[:, TB0 + 14720:TB0 + 14720 + 2048].rearrange("p (t n) -> p t n", t=8)
    mzc = ARN[:, TB0:TB0 + 1152]
    mvcf = ARN[:, TB0 + 1152:TB0 + 1664]
    mvcl = ARN[:, TB0 + 1664:TB0 + 2176]
    wout_sb = ARN[:, TB0:TB0 + 12 * 1024].rearrange("p (j n) -> p j n", j=12)
    utt = [ARN[:, i * 1536:(i + 1) * 1536].rearrange("p (j n) -> p j n", j=12) for i in range(2)]
    X8 = sb("X8", [128, 2048], F32)
    X8b = X8.bitcast(BF16)
    xt = [X8[:, 0:1024], X8[:, 1024:2048]]
    ge = X8[:, 0:512]
    gg = X8[:, 512:1024]
    rr = X8[:, 1024:1536]
    uTo = [X8b[:, 3072:3584], X8b[:, 3584:4096]]
    GB = sb("GB", [128, 2048], F32)
    gg2 = [GB[:, 0:512], GB[:, 512:1024]]
    ge2 = [GB[:, 1024:1536], GB[:, 1536:2048]]
    XN = sb("XN", [128, 2048], F32)
    xn = [XN[:, 0:1024], XN[:, 1024:2048]]
    wk = [XN[:, i * 512:(i + 1) * 512] for i in range(4)]
    TMB = sb("TMB", [128, 1024], BF16)
    tm_b = [TMB[:, 0:512], TMB[:, 512:1024]]
    junk = TMB[:, :]
    PTB = sb("PTB", [128, 2048], BF16)
    PT = [PTB[:, i * 1024:(i + 1) * 1024].rearrange("p (h n) -> p h n", h=2) for i in range(2)]
    cs_rep = PTB[:, :].rearrange("p (a c) -> p a c", a=16)

    PS_S = [nc.alloc_psum_tensor("ps_s%d" % i, [128, 1024], F32) for i in range(2)]
    PS_T = nc.alloc_psum_tensor("ps_t", [128, 512], F32)
    PS_SM = nc.alloc_psum_tensor("ps_sm", [128, 512], F32)
    PS_G = nc.alloc_psum_tensor("ps_g", [128, 512], F32)
    PS_X = nc.alloc_psum_tensor("ps_x", [128, 512], F32)
    PS_Xb = PS_X.bitcast(BF16)

    LOOK = 3
    PRE = {}
    ctr = {"wst": 0, "wg": 0, "acc": 0, "tm": 0, "pt": 0, "uto": 0, "wg_cur": 0, "att": 0, "tr": 0, "ts": 0}

    def load_w(c0, wsrc, slot_kind="wst", ncols=512):
        if slot_kind == "wst":
            i = ctr["wst"] % 2
            ctr["wst"] += 1
            dst = wst[i]
        else:
            i = ctr["wg"] % 2
            ctr["wg"] += 1
            dst = wg[i]
        res = (slot_kind, i)
        srcv = wsrc.rearrange("(kc p) n -> p kc n", p=128)[:, :, c0:c0 + ncols]
        P.dma("pool", lambda e, d=dst, s=srcv, n=ncols: e.dma_start(out=d[:, :, 0:n], in_=s),
              writes=[res], stream="%s%d" % (slot_kind, i))
        return dst, res

    def setup():
        P.dma("sp", lambda e: e.dma_start(out=ident_f[:, :], in_=ident_d), writes=["ident_f"], stream="misc3")
        P.op("pool", lambda e: e.tensor_copy(ident_b[:, :], ident_f[:, :]), reads=["ident_f"], writes=["ident_b"])
        P.op("pool", lambda e: e.memset(ones_b[:, :], 1.0), writes=["ones_b"])
        P.op("pool", lambda e: e.memset(eps_c[:, :], EPS), writes=["eps_c"])
        P.dma("sp", lambda e: e.dma_start(out=cs_f[:, :], in_=cvec), writes=["cs_f"], stream="misc0")
        P.dma("sp", lambda e: e.dma_start(out=rope[:, :, :, :].rearrange("p a b c -> p (a b c)"), in_=rope_d),
              writes=["rope"], stream="misc1")
        P.dma("sp", lambda e: e.dma_start(out=selv[:, :], in_=selv_d), writes=["selv"], stream="misc2")
        P.op("act", lambda e: e.activation(cs_e[:, :], cs_f[:, :], AF.Exp, scale=-1.0), reads=["cs_f"], writes=["cs_e"])
        P.op("dve", lambda e: e.tensor_scalar_add(cs_e[:, :], cs_e[:, :], 1.0), reads=["cs_e"], writes=["cs_e"])
        P.op("dve", lambda e: e.reciprocal(cs_e[:, :], cs_e[:, :]), reads=["cs_e"], writes=["cs_e"])
        P.op("dve", lambda e: e.tensor_tensor(cs_b[:, :, :].rearrange("p a b -> p (a b)"), cs_f[:, :], cs_e[:, :], ALU.mult),
             reads=["cs_e", "cs_f"], writes=["cs_b"])

    def phase_mod(l):
        W = LW[l]
        P.dma("sp", lambda e: e.dma_start(out=adabT[:, :], in_=W["adabT"]), writes=["adabT"], stream="misc0")
        P.dma("sp", lambda e: e.dma_start(out=normT[:, :], in_=W["normT"]), writes=["normT"], stream="misc1")
        P.dma("sp", lambda e: e.dma_start(out=qn_bc[:, :], in_=W["qn"].partition_broadcast(128)),
              writes=["qn_bc"], stream="misc3")
        P.dma("sp", lambda e: e.dma_start(out=kn_bc[:, :], in_=W["kn"].partition_broadcast(128)),
              writes=["kn_bc"], stream="misc4")
        for blk in range(4):
            wsb, wres = load_w(blk * 512, W["ada_w"])

            def f(e, wsb=wsb, blk=blk):
                inst = None
                for sub in range(4):
                    g = blk * 4 + sub
                    for kc in range(8):
                        inst = e.matmul(PS_X[:, g * 2:g * 2 + 2], lhsT=wsb[:, kc, sub * 128:(sub + 1) * 128],
                                        rhs=cs_b[:, kc, :], start=(kc == 0), stop=(kc == 7))
                return inst
            P.op("pe", f, reads=[wres, "cs_b"], writes=["PS_X"])
        P.op("dve", lambda e: e.tensor_tensor(modT[:, :, :], PS_X[:, 0:32].rearrange("p (g w) -> p g w", w=2),
                                              adabT[:, :].unsqueeze(2).broadcast_to([128, 16, 2]), ALU.add),
             reads=["PS_X", "adabT"], writes=["modT"])
        P.op("dve", lambda e: e.scalar_tensor_tensor(A1T[:, :, :], modT[:, 8:16, :], 1.0,
                                                     normT[:, :].unsqueeze(2).broadcast_to([128, 8, 2]),
                                                     ALU.add, ALU.mult),
             reads=["modT", "normT"], writes=["A1T"])

    def prefetch_kvac(l):
        PRE["kvac"] = load_w(KVAC_OFF, LW[l]["w_in"])

    def phase_gate(l, last):
        W = LW[l]
        P.dma("sp", lambda e: e.dma_start(out=adabg, in_=W["adabg"].partition_broadcast(128)),
              writes=["adabg"], stream="misc2")
        if last:
            P.dma("sp", lambda e: e.dma_start(out=fnw_sb, in_=fnw_d.partition_broadcast(128)), writes=["fnw"], stream="misc4")
        P.op("dve", lambda e: e.tensor_copy(cs_rep, cs_b[:, :, :].rearrange("p a b -> p (a b)").unsqueeze(2).broadcast_to([128, 16, 128])),
             reads=["cs_b"], writes=["cs_rep"])
        for blk in (4, 5):
            wsb, wres = PRE.pop("gate%d" % blk) if ("gate%d" % blk) in PRE else load_w(blk * 512, W["ada_w"])

            def f(e, wsb=wsb, blk=blk):
                inst = None
                for which in range(2):
                    for kc in range(8):
                        inst = e.matmul(PS_S[which][:, (blk - 4) * 512:(blk - 3) * 512], lhsT=cs_rep[:, kc * 2 + which, :],
                                        rhs=wsb[:, kc, :], start=(kc == 0), stop=(kc == 7))
                return inst
            P.op("pe", f, reads=[wres, "cs_rep"], writes=["PS_S0", "PS_S1"])
        for which in range(2):
            P.op("dve", lambda e, which=which: e.tensor_tensor(gate_bc[which], PS_S[which][:, :], adabg, ALU.add),
                 reads=["PS_S%d" % which, "adabg"], writes=["gate_bc%d" % which])

    def phase_norm(l, src):
        P.op("dve", lambda e: e.memset(ss[:, :], 0.0), writes=[("ss", t) for t in range(NTC)])
        for t in range(NTC):
            i = t % 2
            w = 0 if t < NT else 1
            P.dma("sp", lambda e, t=t, i=i: e.dma_start(out=xt[i], in_=src[t * 128:(t + 1) * 128, :]),
                  writes=[("xt", i)], stream="xt%d" % i)
            P.op("act", lambda e, t=t, i=i: e.activation(junk, xt[i], AF.Square, scale=1.0 / 32.0,
                                                        accum_out=ss[:, t:t + 1]),
                 reads=[("xt", i)], writes=["junk", ("ss", t)])
            P.op("act", lambda e, t=t: e.activation(rstd[:, t:t + 1], ss[:, t:t + 1], AF.Ln, bias=eps_c[:, 0:1]),
                 reads=[("ss", t)], writes=[("rstd", t)])
            P.op("act", lambda e, t=t: e.activation(rstd[:, t:t + 1], rstd[:, t:t + 1], AF.Exp, scale=-0.5),
                 reads=[("rstd", t)], writes=[("rstd", t)])
            P.op("act", lambda e, t=t, i=i: e.activation(xn[i], xt[i], AF.Copy, scale=rstd[:, t:t + 1]),
                 reads=[("xt", i), ("rstd", t)], writes=[("xn", i)])

            def ftr(e, i=i):
                inst = None
                for kc in range(8):
                    inst = e.transpose(PS_S[i][:, kc * 128:(kc + 1) * 128], xn[i][:, kc * 128:(kc + 1) * 128], ident_f[:, :])
                return inst
            P.op("pe", ftr, reads=[("xn", i), "ident_f"], writes=["PS_S%d" % i])
            P.op("dve", lambda e, i=i, w=w: e.tensor_tensor(
                xn[i].rearrange("p (k n) -> p k n", k=8), PS_S[i][:, :].rearrange("p (k n) -> p k n", k=8),
                A1T[:, :, w:w + 1].broadcast_to([128, 8, 128]), ALU.mult),
                reads=["PS_S%d" % i, "A1T"], writes=[("xn", i)])
            P.op("pool", lambda e, i=i, w=w, t=t: e.tensor_tensor(
                hxT[:, :, t * 128:(t + 1) * 128], xn[i].rearrange("p (k n) -> p k n", k=8),
                modT[:, 0:8, w:w + 1].broadcast_to([128, 8, 128]), ALU.add),
                reads=[("xn", i), "modT"], writes=[("hxT", t)])

    def proj_mm(t, wsb, wres, ps, psres):
        def f(e):
            inst = None
            for kc in range(8):
                inst = e.matmul(ps, lhsT=hxT[:, kc, t * 128:(t + 1) * 128], rhs=wsb[:, kc, :],
                                start=(kc == 0), stop=(kc == 7))
            return inst
        P.op("pe", f, reads=[wres, ("hxT", t)], writes=[psres])

    def rope_ops(xsrc, xres, nh, t, dst, dstres, tt_, ttn, uu_, uun, eng_a="pool", eng_b="dve"):
        C2 = rope[:, t, 0, :].unsqueeze(1).broadcast_to([128, nh, 64])
        S2a = rope[:, t, 1, 0:32].unsqueeze(1).broadcast_to([128, nh, 32])
        S2b = rope[:, t, 1, 32:64].unsqueeze(1).broadcast_to([128, nh, 32])
        x3 = xsrc.rearrange("p (h d) -> p h d", d=64)
        t3 = tt_[:, 0:nh * 64].rearrange("p (h d) -> p h d", d=64)
        u3 = uu_[:, 0:nh * 64].rearrange("p (h d) -> p h d", d=64)
        d3 = dst.rearrange("p (h d) -> p h d", d=64)
        xres = list(xres)
        P.op(eng_a, lambda e: e.tensor_tensor(t3, x3, C2, ALU.mult), reads=xres + ["rope"], writes=[ttn])
        P.op(eng_b, lambda e: e.tensor_tensor(u3[:, :, 0:32], x3[:, :, 32:64], S2a, ALU.mult), reads=xres + ["rope"], writes=[uun])
        P.op(eng_b, lambda e: e.tensor_tensor(u3[:, :, 32:64], x3[:, :, 0:32], S2b, ALU.mult), reads=xres + ["rope", uun], writes=[uun])
        P.op(eng_a, lambda e: e.tensor_tensor(d3, t3, u3, ALU.add), reads=[ttn, uun], writes=[dstres])

    def rms_heads(ps, psres, nh, wbc, wbcres, dst, dstres, scale, tt_, ttn, st_, stn):
        n = nh * 64
        P.op("act", lambda e: e.activation(tt_[:, 0:n], ps, AF.Square), reads=[psres], writes=[ttn])
        P.op("dve", lambda e: e.tensor_reduce(st_[:, 0:nh], tt_[:, 0:n].rearrange("p (h d) -> p h d", d=64), AX.X, ALU.add),
             reads=[ttn], writes=[stn])
        P.op("act", lambda e: e.activation(st_[:, 0:nh], st_[:, 0:nh], AF.Ln, bias=eps_c[:, 0:1], scale=1.0 / 64.0),
             reads=[stn], writes=[stn])
        P.op("act", lambda e: e.activation(st_[:, 0:nh], st_[:, 0:nh], AF.Exp, scale=-0.5),
             reads=[stn], writes=[stn])
        d3 = dst.rearrange("p (h d) -> p h d", d=64)
        P.op("dve", lambda e: e.tensor_tensor(d3, ps.rearrange("p (h d) -> p h d", d=64),
                                              st_[:, 0:nh].unsqueeze(2).broadcast_to([128, nh, 64]), ALU.mult),
             reads=[psres, stn], writes=[dstres])
        P.op("dve", lambda e: e.scalar_tensor_tensor(d3, d3, scale, wbc[:, :].unsqueeze(1).broadcast_to([128, nh, 64]),
                                                      ALU.mult, ALU.mult),
             reads=[dstres, wbcres], writes=[dstres])

    ACCS = [(PS_S[0][:, 0:512], "PS_S0a"), (PS_S[1][:, 0:512], "PS_S1a"), (PS_S[0][:, 512:1024], "PS_S0b"),
            (PS_S[1][:, 512:1024], "PS_S1b"), (PS_T[:, :], "PS_T"), (PS_SM[:, :], "PS_SM")]
    TRS = [(PS_X.bitcast(BF16), "PS_X"), (PS_G.bitcast(BF16), "PS_G")]
    TSETS = [
        (wk[0], "wk0", wk[2], "wk2", wk[3], "wk3"),
        (gg2[0], ("gg", 0), gg2[1], ("gg", 1), ge2[0], ("ge", 0)),
        (ge2[1], ("ge", 1), wk[1], "wk1", rr, "rr"),
    ]
    TMS = [(tm_b[0], ("tm", 0)), (tm_b[1], ("tm", 1)), (uTo[0], ("uTo", 0))]

    def next_acc():
        i = ctr["acc"] % len(ACCS)
        ctr["acc"] += 1
        return ACCS[i]

    def next_tr():
        i = ctr["tr"] % 2
        ctr["tr"] += 1
        return TRS[i]

    def next_tset():
        i = ctr["ts"] % 3
        ctr["ts"] += 1
        a, an, t_, tn, u, un = TSETS[i]
        return a, an, t_, tn, u, un, sm4x[:, i * 16:(i + 1) * 16], ("sm4", i)

    def next_tm():
        i = ctr["tm"] % 3
        ctr["tm"] += 1
        return TMS[i]

    def phase_kvproj(l):
        W = LW[l]
        wsb, wres = PRE.pop("kvac") if "kvac" in PRE else load_w(KVAC_OFF, W["w_in"])
        accq = {}
        for t in range(min(LOOK, NTC)):
            accq[t] = next_acc()
            proj_mm(t, wsb, wres, accq[t][0], accq[t][1])

        def front1(t):
            ps, psres = accq.pop(t)
            if t + LOOK < NTC:
                accq[t + LOOK] = next_acc()
                proj_mm(t + LOOK, wsb, wres, accq[t + LOOK][0], accq[t + LOOK][1])
            a_, an, t_, tn, u_, un, st_, stn = next_tset()
            if t < NT:
                vdst = cc_st[:, 2048 + t * 128:2048 + (t + 1) * 128]
                vres = ("cc_st", "vA", t)
                cslot = 1 + t
            else:
                vdst = vA_ctx[:, t - NT, :]
                vres = ("vA_ctx", t)
                cslot = 18 + (t - NT)
            P.op("act", lambda e, ps=ps, vdst=vdst: e.copy(vdst, ps[:, 256:384]), reads=[psres], writes=[vres])
            P.op("act", lambda e, ps=ps, cslot=cslot: e.copy(vC[:, cslot, :], ps[:, 384:512]), reads=[psres], writes=[("vC", cslot)])
            rms_heads(ps[:, 0:128], psres, 2, kn_bc, "kn_bc", a_[:, 0:128], an, 1.0, t_, tn, st_, stn)
            P.op("act", lambda e, ps=ps, a_=a_: e.copy(a_[:, 128:256], ps[:, 128:256]), reads=[psres, an], writes=[an])
            tmb, tmn = next_tm()
            rope_ops(a_[:, 0:256], [an], 4, t, tmb[:, 0:256], tmn, t_, tn, u_, un)
            return tmb, tmn, cslot

        def back1(t, tmb, tmn, cslot):
            trb, trn = next_tr()

            def ftr(e, tmb=tmb, trb=trb):
                e.transpose(trb[:, 0:128], tmb[:, 0:128], ident_b[:, :])
                return e.transpose(trb[:, 128:256], tmb[:, 128:256], ident_b[:, :])
            P.op("pe", ftr, reads=[tmn, "ident_b"], writes=[trn])
            if t < NT:
                kdst = cc_st[:, t * 128:(t + 1) * 128]
                kres = ("cc_st", "kA", t)
            else:
                kdst = kTA_ctx[:, (t - NT) * 128:(t - NT + 1) * 128]
                kres = ("kTA_ctx", t)
            P.op("act", lambda e, kdst=kdst, trb=trb: e.copy(kdst, trb[:, 0:128]), reads=[trn], writes=[kres])
            P.op("dve", lambda e, cslot=cslot, trb=trb: e.tensor_copy(kTC[:, cslot * 128:(cslot + 1) * 128], trb[:, 128:256]),
                 reads=[trn], writes=[("kTC", cslot)])
        f1 = {0: front1(0)}
        for t in range(NTC):
            if t + 1 < NTC:
                f1[t + 1] = front1(t + 1)
            back1(t, *f1.pop(t))
        wsb, wres = load_w(KB_OFF, W["w_in"])
        accq = {}
        for t in range(min(LOOK, NTC)):
            accq[t] = next_acc()
            proj_mm(t, wsb, wres, accq[t][0], accq[t][1])

        def front2(t):
            ps, psres = accq.pop(t)
            if t + LOOK < NTC:
                accq[t + LOOK] = next_acc()
                proj_mm(t + LOOK, wsb, wres, accq[t + LOOK][0], accq[t + LOOK][1])
            tmb, tmn = next_tm()
            if t % 2 == 0:
                P.op("act", lambda e, ps=ps, tmb=tmb: e.copy(tmb, ps), reads=[psres], writes=[tmn])
            else:
                P.op("dve", lambda e, ps=ps, tmb=tmb: e.tensor_copy(tmb, ps), reads=[psres], writes=[tmn])
            return tmb, tmn

        def back2(t, tmb, tmn):
            slot = 2 + t if t < NT else 20 + (t - NT)
            trb, trn = next_tr()

            def ftr(e, tmb=tmb, trb=trb):
                inst = None
                for j in range(4):
                    inst = e.transpose(trb[:, j * 128:(j + 1) * 128], tmb[:, j * 128:(j + 1) * 128], ident_b[:, :])
                return inst
            P.op("pe", ftr, reads=[tmn, "ident_b"], writes=[trn])
            if t % 2 == 0:
                P.op("dve", lambda e, slot=slot, trb=trb: e.tensor_copy(kTB[:, :, slot * 128:(slot + 1) * 128],
                                                                       trb[:, 0:512].rearrange("p (j n) -> p j n", j=4)),
                     reads=[trn], writes=[("kTB", slot)])
            else:
                P.op("act", lambda e, slot=slot, trb=trb: e.copy(kTB[:, :, slot * 128:(slot + 1) * 128],
                                                                trb[:, 0:512].rearrange("p (j n) -> p j n", j=4)),
                     reads=[trn], writes=[("kTB", slot)])
        f2 = {0: front2(0)}
        for t in range(NTC):
            if t + 1 < NTC:
                f2[t + 1] = front2(t + 1)
            back2(t, *f2.pop(t))
        wsb, wres = load_w(VB_OFF, W["w_in"])
        for t in range(NTC):
            ps, psres = next_acc()
            proj_mm(t, wsb, wres, ps, psres)
            slot = 2 + t if t < NT else 20 + (t - NT)
            if t % 2 == 0:
                P.op("act", lambda e, ps=ps, slot=slot: e.copy(vB[:, slot, :], ps), reads=[psres], writes=[("vB", slot)])
            else:
                P.op("dve", lambda e, ps=ps, slot=slot: e.tensor_copy(vB[:, slot, :], ps), reads=[psres], writes=[("vB", slot)])

    def phase_exchange(l):
        cc_out = cc_outs[l]
        PRE["qB"] = load_w(QOFF["B"], LW[l]["w_in"])
        ecp = [
            (cc_st[:, 4096:5120].rearrange("p (j n) -> p j n", j=4), kTB[:, :, 256:512], [("kTB", 2), ("kTB", 3)], "kBf"),
            (cc_st[:, 5120:6144].rearrange("p (t n) -> p t n", t=2), vB[:, 2:4, :], [("vB", 2), ("vB", 3)], "vBf"),
            (cc_st[:, 6144:6272], kTC[:, 128:256], [("kTC", 1)], "kCf"),
            (cc_st[:, 6272:6400], vC[:, 1, :], [("vC", 1)], "vCf"),
            (cc_st[:, 6400:7424].rearrange("p (j n) -> p j n", j=4), kTB[:, :, 2048:2304], [("kTB", 16), ("kTB", 17)], "kBl"),
            (cc_st[:, 7424:8448].rearrange("p (t n) -> p t n", t=2), vB[:, 16:18, :], [("vB", 16), ("vB", 17)], "vBl"),
            (cc_st[:, 8448:8576], kTC[:, 16 * 128:17 * 128], [("kTC", 16)], "kCl"),
            (cc_st[:, 8576:8704], vC[:, 16, :], [("vC", 16)], "vCl"),
        ]
        ccdeps = {"A": [("cc_st", "kA", t) for t in range(NT)] + [("cc_st", "vA", t) for t in range(NT)],
                  "F": [("cc_st", e_[3]) for e_ in ecp[0:4]], "L": [("cc_st", e_[3]) for e_ in ecp[4:8]]}
        ecps = {"F": ecp[0:4], "L": ecp[4:8], "A": []}
        for kk, (c0, c1) in CCP.items():
            for (o_, i_, rd, nm) in ecps[kk]:
                P.op("pool", lambda e, o_=o_, i_=i_: e.tensor_copy(o_, i_), reads=rd, writes=[("cc_st", nm)])
            P.dma("sp", lambda e, kk=kk, c0=c0, c1=c1: e.dma_start(out=cc_ins[kk][:, :], in_=cc_st[:, c0:c1]),
                  reads=ccdeps[kk] + ["tabregion"], writes=["cc_in" + kk], stream="ccin" + kk)
            P.dma("pool", lambda e, kk=kk: e.collective_compute("AllGather", ALU.bypass, replica_groups=[[0, 1, 2, 3], [4, 5, 6, 7]],
                                                                 ins=[cc_ins[kk].ap().opt()], outs=[cc_out[kk].ap().opt()]),
                  reads=["cc_in" + kk], writes=["cc_out" + kk], stream="cc%d%s" % (l, kk), cc=True)
        for k in range(3):
            P.dma("sp", lambda e, k=k: e.dma_start(out=stg[k], in_=cc_out["L"][k * 128:(k + 1) * 128, :]),
                  reads=["cc_outL", "cc_inA", "cc_inF", "cc_inL"], writes=[("stg", k)], stream="stg%d" % k)
            P.dma("sp", lambda e, k=k: e.dma_start(out=stg[3 + k], in_=cc_out["F"][(k + 1) * 128:(k + 2) * 128, :]),
                  reads=["cc_outF"], writes=[("stg", 3 + k)], stream="stg%d" % (3 + k))
        def pieces(base):
            return [
                (lambda a: a[:, 0:1024].rearrange("p (j n) -> p j n", j=4)),
                (lambda a: a[:, 1024:2048].rearrange("p (t n) -> p t n", t=2)),
                (lambda a: a[:, 2048:2176]),
                (lambda a: a[:, 2176:2304]),
            ]
        dsts_prev = [(kTB[:, :, 0:256], [("kTB", 0), ("kTB", 1)]), (vB[:, 0:2, :], [("vB", 0), ("vB", 1)]),
                     (kTC[:, 0:128], [("kTC", 0)]), (vC[:, 0, :], [("vC", 0)])]
        dsts_next = [(kTB[:, :, 2304:2560], [("kTB", 18), ("kTB", 19)]), (vB[:, 18:20, :], [("vB", 18), ("vB", 19)]),
                     (kTC[:, 17 * 128:18 * 128], [("kTC", 17)]), (vC[:, 17, :], [("vC", 17)])]
        for which, dsts in ((0, dsts_prev), (1, dsts_next)):
            for pi_, (dst, wr) in enumerate(dsts):
                view = pieces(0)[pi_]
                for k in range(3):
                    src_ = view(stg[which * 3 + k])
                    sc = selv[:, which * 3 + k:which * 3 + k + 1]
                    if k == 0:
                        P.op("dve", lambda e, dst=dst, src_=src_, sc=sc: e.tensor_scalar(dst, src_, sc, None, ALU.mult),
                             reads=[("stg", which * 3 + k), "selv"], writes=wr + ["halo_st"])
                    else:
                        P.op("dve", lambda e, dst=dst, src_=src_, sc=sc: e.scalar_tensor_tensor(dst, src_, sc, dst, ALU.mult, ALU.add),
                             reads=[("stg", which * 3 + k), "selv"] + wr, writes=wr + ["halo_st"])

    def attend(l, m, pair, q0, N, tiles, sinkcol=None):
        gi = MIXI[m] * 4 + pair
        nt = len(tiles)
        ak = ctr["att"]
        ctr["att"] += 1
        Tps, tres = (PS_T, "PS_T") if ak % 2 == 0 else (PS_X, "PS_X")
        ggk, gek = gg2[ak % 2], ge2[ak % 2]
        gres, eres = ("gg", ak % 2), ("ge", ak % 2)
        wgi = ctr["wg_cur"]

        def fg(e):
            inst = None
            for kc in range(8):
                inst = e.matmul(PS_G[:, 0:N], lhsT=wg[wgi][:, kc, :], rhs=hxT[:, kc, q0:q0 + N], start=(kc == 0), stop=(kc == 7))
            return inst
        P.op("pe", fg, reads=[("wg", wgi)] + [("hxT", t) for t in range(q0 // 128, (q0 + N) // 128)], writes=["PS_G"])
        P.op("act", lambda e: e.activation(gek[:, 0:N], PS_G[:, 0:N], AF.Exp, scale=-1.0), reads=["PS_G"], writes=[eres])
        P.op("dve", lambda e: e.tensor_scalar_add(gek[:, 0:N], gek[:, 0:N], 1.0), reads=[eres], writes=[eres])
        P.op("dve", lambda e: e.reciprocal(gek[:, 0:N], gek[:, 0:N]), reads=[eres], writes=[eres])
        P.op("dve", lambda e: e.tensor_tensor(ggk[:, 0:N], PS_G[:, 0:N], gek[:, 0:N], ALU.mult), reads=["PS_G", eres], writes=[gres])

        sbuf_i = []
        partial = any(("cols" in kt_) for kt_ in tiles)
        if partial:
            full = [kt_ for kt_ in tiles if "cols" not in kt_]
            rest = [kt_ for kt_ in tiles if "cols" in kt_]
            tiles = full[:1] + rest + full[1:]

        def emit_qk(i):
            kt = tiles[i]
            si = ctr["acc"] % 2
            ctr["acc"] += 1
            S = PS_S[si]
            sres = "PS_S%d" % si
            adds = kt.get("adds", [])

            adds = kt.get("adds", [])

            cl, ch = kt.get("cols", (0, N))

            def fqk(e, S=S, kt=kt, adds=adds, cl=cl, ch=ch):
                na = len(adds)
                e.matmul(S[:, cl:ch], lhsT=kt["kT"][0:64, :], rhs=qTm[0:64, pair, q0 + cl:q0 + ch], start=True, stop=(na == 0))
                inst = e.matmul(S[:, 512 + cl:512 + ch], lhsT=kt["kT"][64:128, :], rhs=qTm[64:128, pair, q0 + cl:q0 + ch],
                                start=True, stop=(na == 0))
                for ai, (c0, ncol, rx, ry, _r) in enumerate(adds):
                    lastf = (ai == na - 1)
                    e.matmul(S[:, c0:c0 + ncol], lhsT=ident_b[:, :], rhs=rx, start=False, stop=lastf)
                    inst = e.matmul(S[:, 512 + c0:512 + c0 + ncol], lhsT=ident_b[:, :], rhs=ry, start=False, stop=lastf)
                return inst
            rds = [kt["kres"], ("qTm", pair), "ident_b"] + [a[4] for a in adds]
            P.op("pe", fqk, reads=rds, writes=[sres])
            sbuf_i.append((S, sres))

        emit_qk(0)
        for i, kt in enumerate(tiles):
            if i + 1 < nt:
                emit_qk(i + 1)
            S, sres = sbuf_i[i]
            pi = ctr["pt"] % 2
            ctr["pt"] += 1
            Pt = PT[pi]
            cl, ch = kt.get("cols", (0, N))
            P.op("act", lambda e, S=S, Pt=Pt, cl=cl, ch=ch: e.activation(
                Pt[:, :, cl:ch], S[:, :].rearrange("p (h n) -> p h n", h=2)[:, :, cl:ch], AF.Exp),
                reads=[sres], writes=[("PT", pi)])

            def fpv(e, kt=kt, Pt=Pt, i=i, cl=cl, ch=ch):
                st, sp = (i == 0), (i == nt - 1)
                sk = partial
                e.matmul(Tps[0:64, cl:ch], lhsT=kt["v"][:, 0:64], rhs=Pt[:, 0, cl:ch], start=st, stop=sp, skip_group_check=sk)
                e.matmul(Tps[64:128, cl:ch], lhsT=kt["v"][:, 64:128], rhs=Pt[:, 1, cl:ch], start=st, stop=sp, tile_position=(0, 64),
                         skip_group_check=sk)
                e.matmul(PS_SM[0:64, cl:ch], lhsT=ones_b[:, 0:64], rhs=Pt[:, 0, cl:ch], start=st, stop=sp, skip_group_check=sk)
                return e.matmul(PS_SM[64:128, cl:ch], lhsT=ones_b[:, 64:128], rhs=Pt[:, 1, cl:ch], start=st, stop=sp,
                                tile_position=(0, 64), skip_group_check=sk)
            P.op("pe", fpv, reads=[kt["vres"], ("PT", pi), "ones_b"], writes=[tres, "PS_SM"])
        if sinkcol is not None:
            P.op("dve", lambda e: e.tensor_scalar_add(rr[:, 0:N], PS_SM[:, 0:N], esink[:, sinkcol:sinkcol + 1]),
                 reads=["PS_SM", "esink"], writes=["rr"])
            P.op("dve", lambda e: e.reciprocal(rr[:, 0:N], rr[:, 0:N]), reads=["rr"], writes=["rr"])
        else:
            P.op("dve", lambda e: e.reciprocal(rr[:, 0:N], PS_SM[:, 0:N]), reads=["PS_SM"], writes=["rr"])
        P.op("pool", lambda e: e.tensor_tensor(ggk[:, 0:N], ggk[:, 0:N], rr[:, 0:N], ALU.mult), reads=[gres, "rr"], writes=[gres])
        ui = ctr["uto"] % 2
        ctr["uto"] += 1
        P.op("dve", lambda e: e.tensor_tensor(uTo[ui][:, 0:N], Tps[:, 0:N], ggk[:, 0:N], ALU.mult),
             reads=[tres, gres], writes=[("uTo", ui)])
        t0 = q0 // 128
        ntile = N // 128
        dst = ut_t[t0:t0 + ntile, :, gi, :].rearrange("t f k -> f t k")
        P.dma("sp", lambda e: e.dma_start(out=dst, in_=uTo[ui][:, 0:N].rearrange("p (t k) -> p t k", k=128)),
              reads=[("uTo", ui)], writes=[("ut", t, gi) for t in range(t0, t0 + ntile)], stream="uto%d" % ui)
        if "uT" in dbg_t and l == layers[0]:
            P.dma("sp", lambda e: e.dma_start(out=dbg_t["uT"][gi, :, q0:q0 + N], in_=uTo[ui][:, 0:N]),
                  reads=[("uTo", ui)], writes=[("dbg_uT", gi, q0)], stream="dbg")

    def phase_mixer(l, m, need_ctx):
        W = LW[l]
        cc_out = cc_outs[l]
        ntq = NTC if need_ctx else NT
        wsb, wres = PRE.pop("q" + m) if ("q" + m) in PRE else load_w(QOFF[m], W["w_in"])
        if m == "B":
            P.dma("pool", lambda e: e.dma_start(out=bzb.rearrange("p h n -> p (h n)"), in_=W["bzb"]), writes=["bzb", "tabregion", "halo_st"], stream="tb0")
            P.dma("pool", lambda e: e.dma_start(out=mzb, in_=mzb_d), writes=["mzb", "halo_st"], stream="tb1")
            P.dma("pool", lambda e: e.dma_start(out=mvbf.rearrange("p t n -> p (t n)"), in_=mvbf_d), writes=["mvbf", "halo_st"], stream="tb2")
            P.dma("pool", lambda e: e.dma_start(out=mvbl.rearrange("p t n -> p (t n)"), in_=mvbl_d), writes=["mvbl", "halo_st"], stream="tb3")
        if m == "C":
            P.dma("pool", lambda e: e.dma_start(out=mzc, in_=mzc_d), writes=["mzc"], stream="tb0")
            P.dma("pool", lambda e: e.dma_start(out=mvcf, in_=mvcf_d), writes=["mvcf"], stream="tb1")
            P.dma("pool", lambda e: e.dma_start(out=mvcl, in_=mvcl_d), writes=["mvcl"], stream="tb2")
            P.dma("sp", lambda e: e.dma_start(out=esink[:, :], in_=W["sink"]), writes=["esink"], stream="misc0")
            P.op("act", lambda e: e.activation(esink[:, :], esink[:, :], AF.Exp), reads=["esink"], writes=["esink"])
        if m == "A":
            P.dma("sp", lambda e: e.dma_start(out=kTA.rearrange("p (r n) -> p r n", r=RPB),
                                             in_=cc_out["A"][:, 0:2048].rearrange("(r p) n -> p r n", p=128)),
                  reads=["cc_outA"], writes=["kTA"], stream="ldA0")
            P.dma("sp", lambda e: e.dma_start(out=vA.rearrange("p (r t) n -> p r (t n)", r=RPB),
                                             in_=cc_out["A"][:, 2048:4096].rearrange("(r p) n -> p r n", p=128)),
                  reads=["cc_outA"], writes=["vA"], stream="ldA1")
        accq = {}
        for t in range(min(LOOK, ntq)):
            accq[t] = next_acc()
            proj_mm(t, wsb, wres, accq[t][0], accq[t][1])

        def qfront(t):
            ps, psres = accq.pop(t)
            if t + LOOK < ntq:
                accq[t + LOOK] = next_acc()
                proj_mm(t + LOOK, wsb, wres, accq[t + LOOK][0], accq[t + LOOK][1])
            tmb, tmn = next_tm()
            if m == "A":
                a_, an, t_, tn, u_, un, st_, stn = next_tset()
                rms_heads(ps, psres, 8, qn_bc, "qn_bc", a_, an, 0.125, t_, tn, st_, stn)
                rope_ops(a_, [an], 8, t, tmb, tmn, t_, tn, u_, un)
            elif m == "C":
                a_, an, t_, tn, u_, un, st_, stn = next_tset()
                P.op("act", lambda e, ps=ps, a_=a_: e.mul(a_, ps, 0.125), reads=[psres], writes=[an])
                rope_ops(a_, [an], 8, t, tmb, tmn, t_, tn, u_, un)
            else:
                P.op("act", lambda e, ps=ps, tmb=tmb: e.mul(tmb, ps, 0.125), reads=[psres], writes=[tmn])
            return tmb, tmn

        def qback(t, tmb, tmn):
            trb, trn = next_tr()

            def ftr(e, tmb=tmb, trb=trb):
                inst = None
                for j in range(4):
                    inst = e.transpose(trb[:, j * 128:(j + 1) * 128], tmb[:, j * 128:(j + 1) * 128], ident_b[:, :])
                return inst
            P.op("pe", ftr, reads=[tmn, "ident_b"], writes=[trn])
            if True:
                P.op("act", lambda e, t=t, trb=trb: e.copy(qTm[:, :, t * 128:(t + 1) * 128], trb[:, 0:512].rearrange("p (j n) -> p j n", j=4)),
                     reads=[trn], writes=[("qTm", j) for j in range(4)])
            else:
                P.op("dve", lambda e, t=t, trb=trb: e.tensor_copy(qTm[:, :, t * 128:(t + 1) * 128], trb[:, 0:512].rearrange("p (j n) -> p j n", j=4)),
                     reads=[trn], writes=[("qTm", j) for j in range(4)])
        fq = {0: qfront(0)}
        for t in range(ntq):
            if t + 1 < ntq:
                fq[t + 1] = qfront(t + 1)
            qback(t, *fq.pop(t))
        P.barrier()
        nxtm = {"B": "C", "C": "A"}.get(m)
        if nxtm is not None:
            PRE["q" + nxtm] = load_w(QOFF[nxtm], W["w_in"])
        if m == "A":
            for j in range(3):
                P.dma("pool", lambda e, j=j: e.dma_start(out=wout_sb[:, 4 * j:4 * j + 4, :],
                                                          in_=W["w_out"][j * 512:(j + 1) * 512, :].rearrange("(j p) n -> p j n", p=128)),
                      writes=[("wout", j)], stream="wout%d" % j)
            PRE["gate4"] = load_w(4 * 512, W["ada_w"])
            PRE["gate5"] = load_w(5 * 512, W["ada_w"])
        wgq = {0: load_w(G_OFF + (MIXI[m] * 4) * 128, W["w_in"], slot_kind="wg", ncols=128)}
        for pair in range(4):
            if pair + 1 < 4:
                wgq[pair + 1] = load_w(G_OFF + (MIXI[m] * 4 + pair + 1) * 128, W["w_in"], slot_kind="wg", ncols=128)
            ctr["wg_cur"] = wgq[pair][1][1]
            chunks = [(c * 512, 512, c) for c in range(4)]
            if need_ctx:
                chunks.append((TOK, 256, None))
            for (q0, N, c) in chunks:
                tiles = []
                if m == "A":
                    ctxA = [dict(kT=kTA_ctx[:, j * 128:(j + 1) * 128], kres=("kTA_ctx", NT + j), v=vA_ctx[:, j, :], vres=("vA_ctx", NT + j))
                            for j in range(2)]
                    if c is not None:
                        for kt_i in range(64):
                            tiles.append(dict(kT=kTA[:, kt_i * 128:(kt_i + 1) * 128], kres="kTA", v=vA[:, kt_i, :], vres="vA"))
                    tiles += ctxA
                    sinkcol = None
                elif m == "C":
                    ctxC = [dict(kT=kTC[:, s_ * 128:(s_ + 1) * 128], kres=("kTC", s_), v=vC[:, s_, :], vres=("vC", s_)) for s_ in (18, 19)]
                    if c is not None:
                        for tt in range(6):
                            s_ = 4 * c + tt
                            if c == 0 and tt == 0:
                                rhs, rres = mvcf, "mvcf"
                            elif c == 3 and tt == 5:
                                rhs, rres = mvcl, "mvcl"
                            else:
                                rhs, rres = mzc[:, (5 - tt) * 128:(9 - tt) * 128], "mzc"
                            blo, bhi = max(0, tt - 2), min(3, tt)
                            cl_, ch_ = blo * 128, (bhi + 1) * 128
                            rsl = rhs[:, cl_:ch_]
                            tiles.append(dict(kT=kTC[:, s_ * 128:(s_ + 1) * 128], kres=("kTC", s_), v=vC[:, s_, :], vres=("vC", s_),
                                              adds=[(cl_, ch_ - cl_, rsl, rsl, rres)], cols=(cl_, ch_)))
                    tiles += ctxC
                    sinkcol = pair
                else:
                    ctxB = [dict(kT=kTB[:, pair, s_ * 128:(s_ + 1) * 128], kres=("kTB", s_), v=vB[:, s_, pair * 128:(pair + 1) * 128], vres=("vB", s_))
                            for s_ in (20, 21)]
                    if c is not None:
                        for tt in range(8):
                            s_ = 4 * c + tt
                            blo, bhi = max(0, tt - 4), min(3, tt)
                            if c == 0:
                                blo = max(0, tt - 5)
                            if c == 3:
                                bhi = min(3, tt + 1)
                            cl_, ch_ = blo * 128, (bhi + 1) * 128
                            bx = bzb[:, 2 * pair, (7 - tt) * 128 + cl_:(7 - tt) * 128 + ch_]
                            by = bzb[:, 2 * pair + 1, (7 - tt) * 128 + cl_:(7 - tt) * 128 + ch_]
                            adds = [(cl_, ch_ - cl_, bx, by, "bzb")]

                            def mrange(lo, hi, tab, res_, base):
                                a_, b_ = max(lo, cl_), min(hi, ch_)
                                if b_ > a_:
                                    sl = tab[:, a_ - base:b_ - base]
                                    adds.append((a_, b_ - a_, sl, sl, res_))
                            gfull = mzb[:, (7 - tt) * 128:(11 - tt) * 128]
                            if c == 0:
                                mrange(0, 256, mvbf[:, tt, :], "mvbf", 0)
                                mrange(256, 512, gfull, "mzb", 0)
                            elif c == 3:
                                mrange(0, 256, gfull, "mzb", 0)
                                mrange(256, 512, mvbl[:, tt, :], "mvbl", 256)
                            else:
                                mrange(0, 512, gfull, "mzb", 0)
                            tiles.append(dict(kT=kTB[:, pair, s_ * 128:(s_ + 1) * 128], kres=("kTB", s_),
                                              v=vB[:, s_, pair * 128:(pair + 1) * 128], vres=("vB", s_), adds=adds, cols=(cl_, ch_)))
                    tiles += ctxB
                    sinkcol = None
                attend(l, m, pair, q0, N, tiles, sinkcol)

    def phase_wout(l, src, dst_x, need_ctx, last):
        W = LW[l]
        phase_gate(l, last)
        if last:
            P.op("dve", lambda e: e.memset(ss[:, :], 0.0), writes=[("ss", t) for t in range(NTC)])
        ntw = NTC if need_ctx else NT

        def wloads(t):
            i = t % 2
            P.dma("sp", lambda e, t=t, i=i: e.dma_start(out=utt[i], in_=ut_t[t, :, :, :]),
                  reads=[("ut", t, g) for g in range(12)], writes=[("utt", i)], stream="utt%d" % i)
            P.dma("sp", lambda e, t=t, i=i: e.dma_start(out=xt[i], in_=src[t * 128:(t + 1) * 128, :]),
                  writes=[("xt", i)], stream="xt%d" % i)
        wloads(0)
        for t in range(ntw):
            i = t % 2
            w = 0 if t < NT else 1
            if t + 1 < ntw:
                wloads(t + 1)
            def f(e, i=i):
                inst = None
                for half in range(2):
                    for j in range(12):
                        inst = e.matmul(PS_S[i][:, half * 512:(half + 1) * 512], lhsT=utt[i][:, j, :],
                                        rhs=wout_sb[:, j, half * 512:(half + 1) * 512], start=(j == 0), stop=(j == 11))
                return inst
            P.op("pe", f, reads=[("utt", i)] + [("wout", j) for j in range(3)], writes=["PS_S%d" % i])
            P.op("dve", lambda e, i=i, w=w: e.tensor_tensor(xn[i], PS_S[i][:, :], gate_bc[w], ALU.mult),
                 reads=["PS_S%d" % i, "gate_bc%d" % w], writes=[("xn", i)])
            P.op("pool", lambda e, i=i: e.tensor_tensor(xn[i], xn[i], xt[i], ALU.add),
                 reads=[("xn", i), ("xt", i)], writes=[("xn", i)])
            if not last:
                P.dma("sp", lambda e, t=t, i=i: e.dma_start(out=dst_x[t * 128:(t + 1) * 128, :], in_=xn[i]),
                      reads=[("xn", i)], writes=[("x1", t)], stream="xo%d" % i)
            else:
                if final_norm:
                    P.op("act", lambda e, t=t, i=i: e.activation(junk, xn[i], AF.Square, scale=1.0 / 32.0,
                                                                accum_out=ss[:, t:t + 1]),
                         reads=[("xn", i)], writes=["junk", ("ss", t)])
                    P.op("act", lambda e, t=t: e.activation(rstd[:, t:t + 1], ss[:, t:t + 1], AF.Ln, bias=eps_c[:, 0:1]),
                         reads=[("ss", t)], writes=[("rstd", t)])
                    P.op("act", lambda e, t=t: e.activation(rstd[:, t:t + 1], rstd[:, t:t + 1], AF.Exp, scale=-0.5),
                         reads=[("rstd", t)], writes=[("rstd", t)])
                    P.op("act", lambda e, t=t, i=i: e.activation(xn[i], xn[i], AF.Copy, scale=rstd[:, t:t + 1]),
                         reads=[("xn", i), ("rstd", t)], writes=[("xn", i)])
                    P.op("pool", lambda e, i=i: e.tensor_tensor(xn[i], xn[i], fnw_sb, ALU.mult),
                         reads=[("xn", i), "fnw"], writes=[("xn", i)])
                P.dma("sp", lambda e, t=t, i=i: e.dma_start(out=out_d[t * 128:(t + 1) * 128, :], in_=xn[i]),
                      reads=[("xn", i)], writes=[("out", t)], stream="xo%d" % i)

    def dbg_dump(name, src_ap, reads):
        if name in dbg_t:
            P.dma("sp", lambda e: e.dma_start(out=dbg_t[name], in_=src_ap), reads=reads, writes=["dbg_" + name], stream="dbg")

    setup()
    src = xin
    stopped = False
    for li, l in enumerate(layers):
        last = (li == len(layers) - 1)
        need_ctx = not last
        dst_x = x1_t.ap()
        phase_mod(l)
        prefetch_kvac(l)
        phase_norm(l, src)
        P.barrier()
        if li == 0:
            dbg_dump("hxT", hxT[:, :, :].rearrange("p k n -> p (k n)"), [("hxT", t) for t in range(NTC)])
        if stop_after == "norm":
            stopped = True
            break
        phase_kvproj(l)
        phase_exchange(l)
        if li == 0:
            dbg_dump("kvr", KVR[:, :], [("kTB", s_) for s_ in range(22)] + [("vB", s_) for s_ in range(22)])
            dbg_dump("kc", kTC[:, :], [("kTC", s_) for s_ in range(20)])
            dbg_dump("vc", vC[:, :, :].rearrange("p t n -> p (t n)"), [("vC", s_) for s_ in range(20)])
            dbg_dump("ccst", cc_st, ["cc_inA", "cc_inF", "cc_inL"])
        if stop_after == "kv":
            stopped = True
            break
        for m in ("B", "C", "A"):
            if m == "A":
                P.barrier()
            phase_mixer(l, m, need_ctx)
            if li == 0:
                dbg_dump("qT" + m, qTm.rearrange("p j n -> p (j n)"), [("qTm", j) for j in range(4)])
            P.barrier()
            if stop_after == "mix" + m:
                stopped = True
                break
        if stopped:
            break
        phase_wout(l, src, dst_x, need_ctx, last)
        P.barrier()
        src = x1_t.ap()
    if stopped:
        P.op("pool", lambda e: e.memset(xn[0], 0.0), writes=[("xn", 0)])
        P.dma("sp", lambda e: e.dma_start(out=out_d[0:128, :], in_=xn[0]), reads=[("xn", 0)], writes=["outz"], stream="xo0")
        P.barrier()

    print("sbuf bytes remaining:", nc.sbuf_bytes_remaining)
    with nc.Block() as block:
        P.emit(block)
    return nc


def make_in_maps(inputs, layers=(0, 1)):
    x = np.asarray(inputs["x"], np.float32)
    c = np.asarray(inputs["c"], np.float32)
    ctx = np.asarray(inputs["ctx"], np.float32)
    c_ctx = np.asarray(inputs["c_ctx"], np.float32)
    shared = {}
    for l in layers:
        shared["w_in%d" % l] = _perm_w_in(np.asarray(inputs["w_in"][l], np.float32))
        shared["w_out%d" % l] = _perm_w_out(np.asarray(inputs["w_out"][l], np.float32))
        shared["ada_w%d" % l] = np.ascontiguousarray(np.asarray(inputs["ada_w"][l], np.float32))
        ab = np.asarray(inputs["ada_b"][l], np.float32)
        shared["adabT%d" % l] = np.ascontiguousarray(ab[0:2048].reshape(16, 128).T)
        shared["adabg%d" % l] = np.ascontiguousarray(ab[2048:3072].reshape(1, D))
        shared["normT%d" % l] = np.ascontiguousarray(np.asarray(inputs["norm_w"][l], np.float32).reshape(8, 128).T)
        shared["qn%d" % l] = np.asarray(inputs["q_norm_a"][l], np.float32).reshape(1, 64).copy()
        shared["kn%d" % l] = np.asarray(inputs["k_norm_a"][l], np.float32).reshape(1, 64).copy()
        shared["bzb%d" % l] = _b_bias(np.asarray(inputs["rpb_b"][l], np.float32)).reshape(128, 8 * 11 * 128)
        sk = np.asarray(inputs["sink_c"][l], np.float32)
        st = np.zeros((128, 4), np.float32)
        for j in range(4):
            st[0:64, j] = sk[j]
            st[64:128, j] = sk[j + 4]
        shared["sink%d" % l] = st
    shared["fnw"] = np.asarray(inputs["final_norm_w"], np.float32).reshape(1, D).copy()
    maps = []
    for core in range(NCORE):
        b, r = core // RPB, core % RPB
        m = dict(shared)
        m["xin"] = np.ascontiguousarray(np.concatenate([x[b, r * TOK:(r + 1) * TOK, :], ctx[b]], axis=0))
        cv = np.zeros((128, 8, 2), np.float32)
        cv[:, :, 0] = c[b].reshape(8, 128).T
        cv[:, :, 1] = c_ctx.reshape(8, 128).T
        m["cvec"] = cv.reshape(128, 16)
        m["rope"] = _rope_tables(r).reshape(128, NTC * 2 * 64)
        m["ident"] = np.eye(128, dtype=np.float32)
        sv = np.zeros((128, 8), np.float32)
        if r > 0:
            sv[:, r - 1] = 1.0
        if r < RPB - 1:
            sv[:, 3 + r] = 1.0
        m["selv"] = sv
        gen, first, last = _b_masks(r)
        m["mzb"], m["mvbf"], m["mvbl"] = gen, first, last
        gen, first, last = _c_masks(r)
        m["mzc"], m["mvcf"], m["mvcl"] = gen, first, last
        maps.append(m)
    return maps


_NC_CACHE = {}


def kernel(**inputs):
    if "full" not in _NC_CACHE:
        _NC_CACHE["full"] = build_program()
    nc = _NC_CACHE["full"]
    maps = make_in_maps(inputs)
    res = run_bass_kernel_spmd(nc, maps, core_ids=list(range(NCORE)))
    out = np.zeros((2, SEQ, D), np.float32)
    for core in range(NCORE):
        b, r = core // RPB, core % RPB
        out[b, r * TOK:(r + 1) * TOK, :] = res.results[core]["out"]
    return out
```

```python
import numpy as np
import concourse.bass as bass
import concourse.mybir as mybir
from concourse.bass_utils import run_bass_kernel_spmd

F32 = mybir.dt.float32
BF16 = mybir.dt.bfloat16
I32 = mybir.dt.int32
AF = mybir.ActivationFunctionType
ALU = mybir.AluOpType
AX = mybir.AxisListType

D = 1024
SEQ = 8192
NCORE = 8
RPB = 4
TOK = SEQ // RPB
NT = TOK // 128
NTC = NT + 2
NTOK = NTC * 128
EPS = 1e-6
NEG = -32768.0
CCW = 8704
IN_W = 4608
QOFF = {"A": 0, "B": 512, "C": 1024}
KVAC_OFF = 1536
KB_OFF = 2048
VB_OFF = 2560
G_OFF = 3072
MIXI = {"A": 0, "B": 1, "C": 2}

ENG_CHUNK = 30000


class Op:
    __slots__ = ("eng", "fn", "deps", "signal", "seq", "stream", "sid", "ninst", "cc")

    def __init__(self, eng, fn, stream=None, ninst=1, cc=False):
        self.eng = eng
        self.fn = fn
        self.deps = []
        self.signal = False
        self.seq = None
        self.stream = stream
        self.sid = None
        self.ninst = ninst
        self.cc = cc


class Prog:
    ENGS = ("pe", "act", "dve", "pool", "sp")
    SYNC_SAME = ("act", "dve", "pool")

    def __init__(self, nc):
        self.nc = nc
        self.ops = {e: [] for e in self.ENGS}
        self.order = []
        self.lastw = {}
        self.readers = {}
        self.last_on_stream = {}

    def _deps(self, op, reads, writes):
        reads = list(reads)
        writes = list(writes)
        for r in list(reads):
            if isinstance(r, str) and r.startswith("PS_"):
                reads.remove(r)
                if r not in writes:
                    writes.append(r)
        deps = set()
        for r in reads:
            w = self.lastw.get(r)
            if w is not None:
                deps.add(w)
        for w_ in writes:
            w = self.lastw.get(w_)
            if w is not None:
                deps.add(w)
            for rd in self.readers.get(w_, ()):
                deps.add(rd)
        deps.discard(op)
        op.deps = list(deps)
        for r in reads:
            self.readers.setdefault(r, []).append(op)
        for w_ in writes:
            self.lastw[w_] = op
            self.readers[w_] = []

    def op(self, eng, fn, reads=(), writes=()):
        o = Op(eng, fn)
        self._deps(o, reads, writes)
        self.ops[eng].append(o)
        self.order.append(o)
        return o

    def dma(self, queue, fn, reads=(), writes=(), stream=None, ninst=1, cc=False):
        o = Op(queue, fn, stream=stream, ninst=ninst, cc=cc)
        self._deps(o, reads, writes)
        prev = self.last_on_stream.get(stream)
        if prev is not None and prev not in o.deps:
            o.deps.append(prev)
        self.last_on_stream[stream] = o
        self.ops[queue].append(o)
        self.order.append(o)
        return o

    def barrier(self):
        deps = set()
        for r, w in self.lastw.items():
            if w is not None:
                deps.add(w)
        for r, rl in self.readers.items():
            for rd in rl:
                deps.add(rd)
        deps = list(deps)
        for e in self.ENGS:
            o = Op(e, None)
            o.deps = deps
            self.ops[e].append(o)
            self.order.append(o)
        self.lastw = {}
        self.readers = {}

    def emit(self, block):
        nc = self.nc
        semd = {}

        def sems(sid):
            if sid not in semd:
                semd[sid] = nc.alloc_semaphore("s_%s_%d" % sid)
            return semd[sid]

        for o in self.order:
            for d in o.deps:
                if d.stream is not None:
                    d.signal = True
                elif d.eng != o.eng or o.eng in self.SYNC_SAME or o.stream is not None:
                    d.signal = True
        cnt = {e: 0 for e in self.ENGS}
        scnt = {}
        for o in self.order:
            if o.stream is not None:
                if o.cc:
                    scnt[o.stream] = scnt.get(o.stream, 0) + 1
                    assert scnt[o.stream] == 1, "one collective per stream"
                    o.sid = ("dma_" + o.stream, 0)
                    o.seq = 1
                else:
                    scnt[o.stream] = scnt.get(o.stream, 0) + o.ninst
                    o.sid = ("dma_" + o.stream, 0)
                    o.seq = 16 * scnt[o.stream]
            elif o.signal:
                c = cnt[o.eng]
                o.sid = (o.eng, c // ENG_CHUNK)
                o.seq = c % ENG_CHUNK + 1
                cnt[o.eng] = c + 1

        def run_engine(ename, eng):
            known = {}
            for o in self.ops[ename]:
                need = {}
                for d in o.deps:
                    if d.seq is None:
                        continue
                    if (d.stream is None and d.eng == ename and o.stream is None
                            and ename not in self.SYNC_SAME):
                        continue
                    if need.get(d.sid, 0) < d.seq:
                        need[d.sid] = d.seq
                for sid, v in need.items():
                    if known.get(sid, 0) >= v:
                        continue
                    eng.wait_ge(sems(sid), v)
                    known[sid] = v
                if o.fn is None:
                    continue
                inst = o.fn(eng)
                if o.cc:
                    inst.then_inc(sems(o.sid))
                elif o.stream is not None:
                    insts = inst if isinstance(inst, (list, tuple)) else [inst]
                    assert len(insts) == o.ninst
                    for it in insts:
                        it.then_inc(sems(o.sid), 16)
                elif o.signal:
                    inst.then_inc(sems(o.sid), 1)

        @block.tensor
        def _(e):
            run_engine("pe", e)

        @block.scalar
        def _(e):
            run_engine("act", e)

        @block.vector
        def _(e):
            run_engine("dve", e)

        @block.gpsimd
        def _(e):
            run_engine("pool", e)

        @block.sync
        def _(e):
            run_engine("sp", e)


PAIR_HEADS = [0, 4, 1, 5, 2, 6, 3, 7]


def _perm_heads(w, off, nheads=8):
    cols = []
    for h in PAIR_HEADS:
        cols.append(w[..., off + h * 64: off + (h + 1) * 64])
    return np.concatenate(cols, axis=-1)


def _perm_w_in(w):
    oqa, oka, ova = 0, 512, 640
    oqb, okb, ovb = 768, 1280, 1792
    oqc, okc, ovc = 2304, 2816, 2944
    og = 3072
    parts = [
        _perm_heads(w, oqa), _perm_heads(w, oqb), _perm_heads(w, oqc),
        w[:, oka:oka + 128], w[:, okc:okc + 128], w[:, ova:ova + 128], w[:, ovc:ovc + 128],
        _perm_heads(w, okb), _perm_heads(w, ovb),
        _perm_heads(w, og), _perm_heads(w, og + 512), _perm_heads(w, og + 1024),
    ]
    return np.ascontiguousarray(np.concatenate(parts, axis=1))


def _perm_w_out(w):
    rows = []
    for m in range(3):
        for h in PAIR_HEADS:
            rows.append(w[m * 512 + h * 64: m * 512 + (h + 1) * 64, :])
    return np.ascontiguousarray(np.concatenate(rows, axis=0))


def _rope_tables(rank):
    t = (rank * TOK + np.arange(TOK)).astype(np.int32)
    rows = (t // 64).astype(np.float32)
    cols = (t % 64).astype(np.float32)
    nf = 16
    freq = (np.float32(10000.0) ** (-np.arange(nf, dtype=np.float32) / np.float32(nf))).astype(np.float32)
    ang = np.concatenate([rows[:, None] * freq, cols[:, None] * freq], axis=-1).astype(np.float32)
    c = np.cos(ang).astype(np.float32)
    s = np.sin(ang).astype(np.float32)
    c = np.concatenate([c, np.ones((256, 32), np.float32)], axis=0)
    s = np.concatenate([s, np.zeros((256, 32), np.float32)], axis=0)
    C2 = np.concatenate([c, c], axis=1)
    S2 = np.concatenate([-s, s], axis=1)
    tab = np.stack([C2, S2], axis=1)
    tab = tab.reshape(NTC, 128, 2, 64).transpose(1, 0, 2, 3)
    return np.ascontiguousarray(tab)


def _b_valid(i_q, i_k):
    q = np.arange(128)
    k = np.arange(128)
    rq = 2 * i_q + q // 64
    cq = q % 64
    rk = 2 * i_k + k // 64
    ck = k % 64
    rs = np.clip(rq - 4, 0, 120)
    cs = np.clip(cq - 8, 0, 48)
    ok = ((rk[:, None] >= rs[None, :]) & (rk[:, None] < rs[None, :] + 8) &
          (ck[:, None] >= cs[None, :]) & (ck[:, None] < cs[None, :] + 16) &
          (rk[:, None] >= 0) & (rk[:, None] < 128))
    return np.where(ok, np.float32(0.0), np.float32(NEG)).astype(np.float32)


def _b_masks(rank):
    gen = np.full((128, 11, 128), NEG, np.float32)
    for s in range(11):
        dl = 5 - s
        if abs(dl) <= 2:
            gen[:, s, :] = _b_valid(30, 30 + dl)
    first = np.zeros((128, 8, 2, 128), np.float32)
    last = np.zeros((128, 8, 2, 128), np.float32)
    for t in range(8):
        for b in range(2):
            cg = 4 * rank + 0
            first[:, t, b, :] = _b_valid(4 * cg + b, 4 * cg - 2 + t)
            cg = 4 * rank + 3
            last[:, t, b, :] = _b_valid(4 * cg + 2 + b, 4 * cg - 2 + t)
    return gen.reshape(128, 11 * 128), first.reshape(128, 8 * 256), last.reshape(128, 8 * 256)


def _b_bias(rpb_l):
    out = np.zeros((128, 8, 11, 128), np.float32)
    k = np.arange(128)
    q = np.arange(128)
    kr, kc = k // 64, k % 64
    qr, qc = q // 64, q % 64
    for s in range(11):
        dl = 5 - s
        dr = 2 * dl + kr[:, None] - qr[None, :] + 7
        dc = kc[:, None] - qc[None, :] + 15
        ok = (dr >= 0) & (dr <= 14) & (dc >= 0) & (dc <= 30)
        drc = np.clip(dr, 0, 14)
        dcc = np.clip(dc, 0, 30)
        for hi, h in enumerate(PAIR_HEADS):
            vals = rpb_l[h][drc, dcc]
            out[:, hi, s, :] = np.where(ok, vals, np.float32(0.0))
    return np.ascontiguousarray(out.reshape(128, 8, 11 * 128))


def _c_masks(rank):
    k = np.arange(128)[:, None]
    q = np.arange(128)[None, :]
    d1 = np.where(k <= q, 0.0, NEG).astype(np.float32)
    dm1 = np.where(k >= q, 0.0, NEG).astype(np.float32)
    d0 = np.zeros((128, 128), np.float32)
    M = np.full((128, 128), NEG, np.float32)
    gen = np.stack([M, M, M, d1, d0, dm1, M, M, M], axis=1).reshape(128, 9 * 128)
    first = np.stack([dm1 if rank > 0 else M, M, M, M], axis=1).reshape(128, 512)
    last = np.stack([M, M, M, d1 if rank < RPB - 1 else M], axis=1).reshape(128, 512)
    return gen, first, last


def build_program(layers=(0, 1), final_norm=True, dbg=None, stop_after=None):
    nc = bass.Bass("TRN2", target_bir_lowering=False)
    P = Prog(nc)
    dbg = dbg or ()

    def din(name, shape, dt=F32):
        return nc.dram_tensor(name, list(shape), dt, kind="ExternalInput").ap()

    xin = din("xin", [NTOK, D])
    cvec = din("cvec", [128, 16])
    rope_d = din("rope", [128, NTC * 2 * 64])
    selv_d = din("selv", [128, 8])
    fnw_d = din("fnw", [1, D])
    ident_d = din("ident", [128, 128])
    mzb_d = din("mzb", [128, 11 * 128])
    mvbf_d = din("mvbf", [128, 2048])
    mvbl_d = din("mvbl", [128, 2048])
    mzc_d = din("mzc", [128, 9 * 128])
    mvcf_d = din("mvcf", [128, 512])
    mvcl_d = din("mvcl", [128, 512])
    LW = {}
    for l in layers:
        LW[l] = dict(
            w_in=din("w_in%d" % l, [D, IN_W]),
            w_out=din("w_out%d" % l, [1536, D]),
            ada_w=din("ada_w%d" % l, [D, 3 * D]),
            adabT=din("adabT%d" % l, [128, 16]),
            adabg=din("adabg%d" % l, [1, D]),
            normT=din("normT%d" % l, [128, 8]),
            qn=din("qn%d" % l, [1, 64]),
            kn=din("kn%d" % l, [1, 64]),
            bzb=din("bzb%d" % l, [128, 8 * 11 * 128]),
            sink=din("sink%d" % l, [128, 4]),
        )
    out_d = nc.dram_tensor("out", [TOK, D], F32, kind="ExternalOutput").ap()
    dbg_t = {}
    for name, shape, dt in dbg:
        dbg_t[name] = nc.dram_tensor("dbg_" + name, list(shape), dt, kind="ExternalOutput").ap()

    x1_t = nc.dram_tensor("x1_scr", [NTOK, D], F32)
    CCP = {"F": (4096, 6400), "L": (6400, 8704), "A": (0, 4096)}
    cc_ins = {k: nc.dram_tensor("cc_in%s" % k, [128, c1 - c0], BF16) for k, (c0, c1) in CCP.items()}
    cc_outs = {l: {k: nc.dram_tensor("cc_out%d%s" % (l, k), [RPB * 128, c1 - c0], BF16) for k, (c0, c1) in CCP.items()}
               for l in layers}
    ut_t = nc.dram_tensor("ut_scr", [NTC, 128, 12, 128], BF16)

    def sb(name, shape, dt):
        return nc.alloc_sbuf_tensor(name, list(shape), dt)

    hxT = sb("hxT", [128, 8, NTOK], BF16)
    hx_f = hxT.bitcast(F32)
    hx_f2 = hx_f[:, :, :].rearrange("p a b -> p (a b)")
    gate_bc = [hx_f2[:, 0:1024], hx_f2[:, 1024:2048]]
    adabg = hx_f2[:, 2048:3072]
    fnw_sb = hx_f2[:, 3072:4096]
    wst = [sb("wst%d" % i, [128, 8, 512], BF16) for i in range(2)]
    wg = [sb("wg%d" % i, [128, 8, 128], BF16) for i in range(2)]
    rope = sb("rope_sb", [128, NTC, 2, 64], F32)
    ident_f = sb("ident_f", [128, 128], F32)
    ident_b = sb("ident_b", [128, 128], BF16)
    ones_b = sb("ones_b", [128, 128], BF16)
    cs_f = sb("cs_f", [128, 16], F32)
    cs_e = sb("cs_e", [128, 16], F32)
    cs_b = sb("cs_b", [128, 8, 2], BF16)
    modT = sb("modT", [128, 16, 2], F32)
    adabT = sb("adabT", [128, 16], F32)
    normT = sb("normT", [128, 8], F32)
    A1T = sb("A1T", [128, 8, 2], F32)
    qn_bc = sb("qn_bc", [128, 64], F32)
    kn_bc = sb("kn_bc", [128, 64], F32)
    ss = sb("ss", [128, NTC], F32)
    rstd = sb("rstd", [128, NTC], F32)
    selv = sb("selv_sb", [128, 8], F32)
    kTC = sb("kTC", [128, 20 * 128], BF16)
    vC = sb("vC", [128, 20, 128], BF16)
    kTA_ctx = sb("kTA_ctx", [128, 256], BF16)
    vA_ctx = sb("vA_ctx", [128, 2, 128], BF16)
    esink = sb("esink", [128, 4], F32)
    sm4 = sb("sm4", [128, 16], F32)
    eps_c = sb("eps_c", [128, 1], F32)
    sm4x = sb("sm4x", [128, 48], F32)
    KVR = sb("KVR", [128, 22528], BF16)
    kTB = KVR[:, 0:11264].rearrange("p (j n) -> p j n", j=4)
    vB = KVR[:, 11264:22528].rearrange("p (t n) -> p t n", t=22)
    kTA = KVR[:, 0:64 * 128]
    vA = KVR[:, 8192:8192 + 64 * 128].rearrange("p (t n) -> p t n", t=64)
    ARN = sb("ARN", [128, 26624], BF16)
    qTm = ARN[:, 0:4 * NTOK].rearrange("p (j n) -> p j n", j=4)
    TB0 = 4 * NTOK
    cc_st = ARN[:, TB0:TB0 + CCW]
    stg = [ARN[:, TB0 + k * 2304:TB0 + (k + 1) * 2304] for k in range(3)] + [ARN[:, 17920 + k * 2304:17920 + (k + 1) * 2304] for k in range(3)]
    bzb = ARN[:, TB0:TB0 + 8 * 1408].rearrange("p (h n) -> p h n", h=8)
    mzb = ARN[:, TB0 + 11264:TB0 + 11264 + 1408]
    mvbf = ARN[:, TB0 + 12672:TB0 + 12672 + 2048].rearrange("p (t n) -> p t n", t=8)
    mvbl = ARN[:, TB0 + 14720:TB0 + 14720 + 2048].rearrange("p (t n) -> p t n", t=8)
    mzc = ARN[:, TB0:TB0 + 1152]
    mvcf = ARN[:, TB0 + 1152:TB0 + 1664]
    mvcl = ARN[:, TB0 + 1664:TB0 + 2176]
    wout_sb = ARN[:, TB0:TB0 + 12 * 1024].rearrange("p (j n) -> p j n", j=12)
    utt = [ARN[:, i * 1536:(i + 1) * 1536].rearrange("p (j n) -> p j n", j=12) for i in range(2)]
    X8 = sb("X8", [128, 2048], F32)
    X8b = X8.bitcast(BF16)
    xt = [X8[:, 0:1024], X8[:, 1024:2048]]
    ge = X8[:, 0:512]
    gg = X8[:, 512:1024]
    rr = X8[:, 1024:1536]
    uTo = [X8b[:, 3072:3584], X8b[:, 3584:4096]]
    GB = sb("GB", [128, 2048], F32)
    gg2 = [GB[:, 0:512], GB[:, 512:1024]]
    ge2 = [GB[:, 1024:1536], GB[:, 1536:2048]]
    XN = sb("XN", [128, 2048], F32)
    xn = [XN[:, 0:1024], XN[:, 1024:2048]]
    wk = [XN[:, i * 512:(i + 1) * 512] for i in range(4)]
    TMB = sb("TMB", [128, 1024], BF16)
    tm_b = [TMB[:, 0:512], TMB[:, 512:1024]]
    junk = TMB[:, :]
    PTB = sb("PTB", [128, 2048], BF16)
    PT = [PTB[:, i * 1024:(i + 1) * 1024].rearrange("p (h n) -> p h n", h=2) for i in range(2)]
    cs_rep = PTB[:, :].rearrange("p (a c) -> p a c", a=16)

    PS_S = [nc.alloc_psum_tensor("ps_s%d" % i, [128, 1024], F32) for i in range(2)]
    PS_T = nc.alloc_psum_tensor("ps_t", [128, 512], F32)
    PS_SM = nc.alloc_psum_tensor("ps_sm", [128, 512], F32)
    PS_G = nc.alloc_psum_tensor("ps_g", [128, 512], F32)
    PS_X = nc.alloc_psum_tensor("ps_x", [128, 512], F32)
    PS_Xb = PS_X.bitcast(BF16)

    LOOK = 3
    ORD = [0, 1, 14, 15] + list(range(2, 14)) + [16, 17]
    PRE = {}
    ctr = {"wst": 0, "wg": 0, "acc": 0, "tm": 0, "pt": 0, "uto": 0, "wg_cur": 0, "att": 0, "tr": 0, "ts": 0}

    def load_w(c0, wsrc, slot_kind="wst", ncols=512):
        if slot_kind == "wst":
            i = ctr["wst"] % 2
            ctr["wst"] += 1
            dst = wst[i]
        else:
            i = ctr["wg"] % 2
            ctr["wg"] += 1
            dst = wg[i]
        res = (slot_kind, i)
        srcv = wsrc.rearrange("(kc p) n -> p kc n", p=128)[:, :, c0:c0 + ncols]
        P.dma("pool", lambda e, d=dst, s=srcv, n=ncols: e.dma_start(out=d[:, :, 0:n], in_=s),
              writes=[res], stream="%s%d" % (slot_kind, i))
        return dst, res

    def setup():
        P.dma("sp", lambda e: e.dma_start(out=ident_f[:, :], in_=ident_d), writes=["ident_f"], stream="misc3")
        P.op("pool", lambda e: e.tensor_copy(ident_b[:, :], ident_f[:, :]), reads=["ident_f"], writes=["ident_b"])
        P.op("pool", lambda e: e.memset(ones_b[:, :], 1.0), writes=["ones_b"])
        P.op("pool", lambda e: e.memset(eps_c[:, :], EPS), writes=["eps_c"])
        P.dma("sp", lambda e: e.dma_start(out=cs_f[:, :], in_=cvec), writes=["cs_f"], stream="misc0")
        P.dma("sp", lambda e: e.dma_start(out=rope[:, :, :, :].rearrange("p a b c -> p (a b c)"), in_=rope_d),
              writes=["rope"], stream="misc1")
        P.dma("sp", lambda e: e.dma_start(out=selv[:, :], in_=selv_d), writes=["selv"], stream="misc2")
        P.op("act", lambda e: e.activation(cs_e[:, :], cs_f[:, :], AF.Exp, scale=-1.0), reads=["cs_f"], writes=["cs_e"])
        P.op("dve", lambda e: e.tensor_scalar_add(cs_e[:, :], cs_e[:, :], 1.0), reads=["cs_e"], writes=["cs_e"])
        P.op("dve", lambda e: e.reciprocal(cs_e[:, :], cs_e[:, :]), reads=["cs_e"], writes=["cs_e"])
        P.op("dve", lambda e: e.tensor_tensor(cs_b[:, :, :].rearrange("p a b -> p (a b)"), cs_f[:, :], cs_e[:, :], ALU.mult),
             reads=["cs_e", "cs_f"], writes=["cs_b"])

    def phase_mod(l):
        W = LW[l]
        P.dma("sp", lambda e: e.dma_start(out=adabT[:, :], in_=W["adabT"]), writes=["adabT"], stream="misc0")
        P.dma("sp", lambda e: e.dma_start(out=normT[:, :], in_=W["normT"]), writes=["normT"], stream="misc1")
        P.dma("sp", lambda e: e.dma_start(out=qn_bc[:, :], in_=W["qn"].partition_broadcast(128)),
              writes=["qn_bc"], stream="misc3")
        P.dma("sp", lambda e: e.dma_start(out=kn_bc[:, :], in_=W["kn"].partition_broadcast(128)),
              writes=["kn_bc"], stream="misc4")
        for blk in range(4):
            wsb, wres = load_w(blk * 512, W["ada_w"])

            def f(e, wsb=wsb, blk=blk):
                inst = None
                for sub in range(4):
                    g = blk * 4 + sub
                    for kc in range(8):
                        inst = e.matmul(PS_X[:, g * 2:g * 2 + 2], lhsT=wsb[:, kc, sub * 128:(sub + 1) * 128],
                                        rhs=cs_b[:, kc, :], start=(kc == 0), stop=(kc == 7))
                return inst
            P.op("pe", f, reads=[wres, "cs_b"], writes=["PS_X"])
        P.op("dve", lambda e: e.tensor_tensor(modT[:, :, :], PS_X[:, 0:32].rearrange("p (g w) -> p g w", w=2),
                                              adabT[:, :].unsqueeze(2).broadcast_to([128, 16, 2]), ALU.add),
             reads=["PS_X", "adabT"], writes=["modT"])
        P.op("dve", lambda e: e.scalar_tensor_tensor(A1T[:, :, :], modT[:, 8:16, :], 1.0,
                                                     normT[:, :].unsqueeze(2).broadcast_to([128, 8, 2]),
                                                     ALU.add, ALU.mult),
             reads=["modT", "normT"], writes=["A1T"])

    def prefetch_kvac(l):
        PRE["kvac"] = load_w(KVAC_OFF, LW[l]["w_in"])

    def phase_gate(l, last):
        W = LW[l]
        P.dma("sp", lambda e: e.dma_start(out=adabg, in_=W["adabg"].partition_broadcast(128)),
              writes=["adabg"], stream="misc2")
        if last:
            P.dma("sp", lambda e: e.dma_start(out=fnw_sb, in_=fnw_d.partition_broadcast(128)), writes=["fnw"], stream="misc4")
        P.op("dve", lambda e: e.tensor_copy(cs_rep, cs_b[:, :, :].rearrange("p a b -> p (a b)").unsqueeze(2).broadcast_to([128, 16, 128])),
             reads=["cs_b"], writes=["cs_rep"])
        for blk in (4, 5):
            wsb, wres = PRE.pop("gate%d" % blk) if ("gate%d" % blk) in PRE else load_w(blk * 512, W["ada_w"])

            def f(e, wsb=wsb, blk=blk):
                inst = None
                for which in range(2):
                    for kc in range(8):
                        inst = e.matmul(PS_S[which][:, (blk - 4) * 512:(blk - 3) * 512], lhsT=cs_rep[:, kc * 2 + which, :],
                                        rhs=wsb[:, kc, :], start=(kc == 0), stop=(kc == 7))
                return inst
            P.op("pe", f, reads=[wres, "cs_rep"], writes=["PS_S0", "PS_S1"])
        for which in range(2):
            P.op("dve", lambda e, which=which: e.tensor_tensor(gate_bc[which], PS_S[which][:, :], adabg, ALU.add),
                 reads=["PS_S%d" % which, "adabg"], writes=["gate_bc%d" % which])

    def phase_norm(l, src):
        P.op("dve", lambda e: e.memset(ss[:, :], 0.0), writes=[("ss", t) for t in range(NTC)])
        for t in range(NTC):
            i = t % 2
            w = 0 if t < NT else 1
            P.dma("sp", lambda e, t=t, i=i: e.dma_start(out=xt[i], in_=src[t * 128:(t + 1) * 128, :]),
                  writes=[("xt", i)], stream="xt%d" % i)
            P.op("act", lambda e, t=t, i=i: e.activation(junk, xt[i], AF.Square, scale=1.0 / 32.0,
                                                        accum_out=ss[:, t:t + 1]),
                 reads=[("xt", i)], writes=["junk", ("ss", t)])
            P.op("act", lambda e, t=t: e.activation(rstd[:, t:t + 1], ss[:, t:t + 1], AF.Ln, bias=eps_c[:, 0:1]),
                 reads=[("ss", t)], writes=[("rstd", t)])
            P.op("act", lambda e, t=t: e.activation(rstd[:, t:t + 1], rstd[:, t:t + 1], AF.Exp, scale=-0.5),
                 reads=[("rstd", t)], writes=[("rstd", t)])
            P.op("act", lambda e, t=t, i=i: e.activation(xn[i], xt[i], AF.Copy, scale=rstd[:, t:t + 1]),
                 reads=[("xt", i), ("rstd", t)], writes=[("xn", i)])

            def ftr(e, i=i):
                inst = None
                for kc in range(8):
                    inst = e.transpose(PS_S[i][:, kc * 128:(kc + 1) * 128], xn[i][:, kc * 128:(kc + 1) * 128], ident_f[:, :])
                return inst
            P.op("pe", ftr, reads=[("xn", i), "ident_f"], writes=["PS_S%d" % i])
            P.op("dve", lambda e, i=i, w=w: e.tensor_tensor(
                xn[i].rearrange("p (k n) -> p k n", k=8), PS_S[i][:, :].rearrange("p (k n) -> p k n", k=8),
                A1T[:, :, w:w + 1].broadcast_to([128, 8, 128]), ALU.mult),
                reads=["PS_S%d" % i, "A1T"], writes=[("xn", i)])
            P.op("pool", lambda e, i=i, w=w, t=t: e.tensor_tensor(
                hxT[:, :, t * 128:(t + 1) * 128], xn[i].rearrange("p (k n) -> p k n", k=8),
                modT[:, 0:8, w:w + 1].broadcast_to([128, 8, 128]), ALU.add),
                reads=[("xn", i), "modT"], writes=[("hxT", t)])

    def proj_mm(t, wsb, wres, ps, psres):
        def f(e):
            inst = None
            for kc in range(8):
                inst = e.matmul(ps, lhsT=hxT[:, kc, t * 128:(t + 1) * 128], rhs=wsb[:, kc, :],
                                start=(kc == 0), stop=(kc == 7))
            return inst
        P.op("pe", f, reads=[wres, ("hxT", t)], writes=[psres])

    def rope_ops(xsrc, xres, nh, t, dst, dstres, tt_, ttn, uu_, uun, eng_a="pool", eng_b="dve"):
        C2 = rope[:, t, 0, :].unsqueeze(1).broadcast_to([128, nh, 64])
        S2a = rope[:, t, 1, 0:32].unsqueeze(1).broadcast_to([128, nh, 32])
        S2b = rope[:, t, 1, 32:64].unsqueeze(1).broadcast_to([128, nh, 32])
        x3 = xsrc.rearrange("p (h d) -> p h d", d=64)
        t3 = tt_[:, 0:nh * 64].rearrange("p (h d) -> p h d", d=64)
        u3 = uu_[:, 0:nh * 64].rearrange("p (h d) -> p h d", d=64)
        d3 = dst.rearrange("p (h d) -> p h d", d=64)
        xres = list(xres)
        P.op(eng_a, lambda e: e.tensor_tensor(t3, x3, C2, ALU.mult), reads=xres + ["rope"], writes=[ttn])
        P.op(eng_b, lambda e: e.tensor_tensor(u3[:, :, 0:32], x3[:, :, 32:64], S2a, ALU.mult), reads=xres + ["rope"], writes=[uun])
        P.op(eng_b, lambda e: e.tensor_tensor(u3[:, :, 32:64], x3[:, :, 0:32], S2b, ALU.mult), reads=xres + ["rope", uun], writes=[uun])
        P.op(eng_a, lambda e: e.tensor_tensor(d3, t3, u3, ALU.add), reads=[ttn, uun], writes=[dstres])

    def rms_heads(ps, psres, nh, wbc, wbcres, dst, dstres, scale, tt_, ttn, st_, stn):
        n = nh * 64
        P.op("act", lambda e: e.activation(tt_[:, 0:n], ps, AF.Square), reads=[psres], writes=[ttn])
        P.op("dve", lambda e: e.tensor_reduce(st_[:, 0:nh], tt_[:, 0:n].rearrange("p (h d) -> p h d", d=64), AX.X, ALU.add),
             reads=[ttn], writes=[stn])
        P.op("act", lambda e: e.activation(st_[:, 0:nh], st_[:, 0:nh], AF.Ln, bias=eps_c[:, 0:1], scale=1.0 / 64.0),
             reads=[stn], writes=[stn])
        P.op("act", lambda e: e.activation(st_[:, 0:nh], st_[:, 0:nh], AF.Exp, scale=-0.5),
             reads=[stn], writes=[stn])
        d3 = dst.rearrange("p (h d) -> p h d", d=64)
        P.op("dve", lambda e: e.tensor_tensor(d3, ps.rearrange("p (h d) -> p h d", d=64),
                                              st_[:, 0:nh].unsqueeze(2).broadcast_to([128, nh, 64]), ALU.mult),
             reads=[psres, stn], writes=[dstres])
        P.op("dve", lambda e: e.scalar_tensor_tensor(d3, d3, scale, wbc[:, :].unsqueeze(1).broadcast_to([128, nh, 64]),
                                                      ALU.mult, ALU.mult),
             reads=[dstres, wbcres], writes=[dstres])

    ACCS = [(PS_S[0][:, 0:512], "PS_S0a"), (PS_S[1][:, 0:512], "PS_S1a"), (PS_S[0][:, 512:1024], "PS_S0b"),
            (PS_S[1][:, 512:1024], "PS_S1b"), (PS_T[:, :], "PS_T"), (PS_SM[:, :], "PS_SM")]
    TRS = [(PS_X.bitcast(BF16), "PS_X"), (PS_G.bitcast(BF16), "PS_G")]
    TSETS = [
        (wk[0], "wk0", wk[2], "wk2", wk[3], "wk3"),
        (gg2[0], ("gg", 0), gg2[1], ("gg", 1), ge2[0], ("ge", 0)),
        (ge2[1], ("ge", 1), wk[1], "wk1", rr, "rr"),
    ]
    TMS = [(tm_b[0], ("tm", 0)), (tm_b[1], ("tm", 1)), (uTo[0], ("uTo", 0))]

    def next_acc():
        i = ctr["acc"] % len(ACCS)
        ctr["acc"] += 1
        return ACCS[i]

    def next_tr():
        i = ctr["tr"] % 2
        ctr["tr"] += 1
        return TRS[i]

    def next_tset():
        i = ctr["ts"] % 3
        ctr["ts"] += 1
        a, an, t_, tn, u, un = TSETS[i]
        return a, an, t_, tn, u, un, sm4x[:, i * 16:(i + 1) * 16], ("sm4", i)

    def next_tm():
        i = ctr["tm"] % 3
        ctr["tm"] += 1
        return TMS[i]

    def phase_kvproj(l):
        W = LW[l]
        wsb, wres = PRE.pop("kvac") if "kvac" in PRE else load_w(KVAC_OFF, W["w_in"])
        accq = {}
        for t in range(min(LOOK, NTC)):
            accq[t] = next_acc()
            proj_mm(t, wsb, wres, accq[t][0], accq[t][1])

        def front1(t):
            ps, psres = accq.pop(t)
            if t + LOOK < NTC:
                accq[t + LOOK] = next_acc()
                proj_mm(t + LOOK, wsb, wres, accq[t + LOOK][0], accq[t + LOOK][1])
            a_, an, t_, tn, u_, un, st_, stn = next_tset()
            if t < NT:
                vdst = cc_st[:, 2048 + t * 128:2048 + (t + 1) * 128]
                vres = ("cc_st", "vA", t)
                cslot = 1 + t
            else:
                vdst = vA_ctx[:, t - NT, :]
                vres = ("vA_ctx", t)
                cslot = 18 + (t - NT)
            P.op("act", lambda e, ps=ps, vdst=vdst: e.copy(vdst, ps[:, 256:384]), reads=[psres], writes=[vres])
            P.op("act", lambda e, ps=ps, cslot=cslot: e.copy(vC[:, cslot, :], ps[:, 384:512]), reads=[psres], writes=[("vC", cslot)])
            rms_heads(ps[:, 0:128], psres, 2, kn_bc, "kn_bc", a_[:, 0:128], an, 1.0, t_, tn, st_, stn)
            P.op("act", lambda e, ps=ps, a_=a_: e.copy(a_[:, 128:256], ps[:, 128:256]), reads=[psres, an], writes=[an])
            tmb, tmn = next_tm()
            rope_ops(a_[:, 0:256], [an], 4, t, tmb[:, 0:256], tmn, t_, tn, u_, un)
            return tmb, tmn, cslot

        def back1(t, tmb, tmn, cslot):
            trb, trn = next_tr()

            def ftr(e, tmb=tmb, trb=trb):
                e.transpose(trb[:, 0:128], tmb[:, 0:128], ident_b[:, :])
                return e.transpose(trb[:, 128:256], tmb[:, 128:256], ident_b[:, :])
            P.op("pe", ftr, reads=[tmn, "ident_b"], writes=[trn])
            if t < NT:
                kdst = cc_st[:, t * 128:(t + 1) * 128]
                kres = ("cc_st", "kA", t)
            else:
                kdst = kTA_ctx[:, (t - NT) * 128:(t - NT + 1) * 128]
                kres = ("kTA_ctx", t)
            P.op("act", lambda e, kdst=kdst, trb=trb: e.copy(kdst, trb[:, 0:128]), reads=[trn], writes=[kres])
            P.op("dve", lambda e, cslot=cslot, trb=trb: e.tensor_copy(kTC[:, cslot * 128:(cslot + 1) * 128], trb[:, 128:256]),
                 reads=[trn], writes=[("kTC", cslot)])
        f1 = {0: front1(0)}
        for t in range(NTC):
            if t + 1 < NTC:
                f1[t + 1] = front1(t + 1)
            back1(t, *f1.pop(t))
        wsb, wres = load_w(KB_OFF, W["w_in"])
        accq = {}
        for t in range(min(LOOK, NTC)):
            accq[t] = next_acc()
            proj_mm(ORD[t], wsb, wres, accq[t][0], accq[t][1])

        def front2(t):
            ps, psres = accq.pop(t)
            if t + LOOK < NTC:
                accq[t + LOOK] = next_acc()
                proj_mm(ORD[t + LOOK], wsb, wres, accq[t + LOOK][0], accq[t + LOOK][1])
            tmb, tmn = next_tm()
            if t % 2 == 0:
                P.op("act", lambda e, ps=ps, tmb=tmb: e.copy(tmb, ps), reads=[psres], writes=[tmn])
            else:
                P.op("dve", lambda e, ps=ps, tmb=tmb: e.tensor_copy(tmb, ps), reads=[psres], writes=[tmn])
            return tmb, tmn

        def back2(t, tmb, tmn):
            tl = ORD[t]
            slot = 2 + tl if tl < NT else 20 + (tl - NT)
            trb, trn = next_tr()

            def ftr(e, tmb=tmb, trb=trb):
                inst = None
                for j in range(4):
                    inst = e.transpose(trb[:, j * 128:(j + 1) * 128], tmb[:, j * 128:(j + 1) * 128], ident_b[:, :])
                return inst
            P.op("pe", ftr, reads=[tmn, "ident_b"], writes=[trn])
            if t % 2 == 0:
                P.op("dve", lambda e, slot=slot, trb=trb: e.tensor_copy(kTB[:, :, slot * 128:(slot + 1) * 128],
                                                                       trb[:, 0:512].rearrange("p (j n) -> p j n", j=4)),
                     reads=[trn], writes=[("kTB", slot)])
            else:
                P.op("act", lambda e, slot=slot, trb=trb: e.copy(kTB[:, :, slot * 128:(slot + 1) * 128],
                                                                trb[:, 0:512].rearrange("p (j n) -> p j n", j=4)),
                     reads=[trn], writes=[("kTB", slot)])
        f2 = {0: front2(0)}
        for t in range(NTC):
            if t + 1 < NTC:
                f2[t + 1] = front2(t + 1)
            back2(t, *f2.pop(t))
        wsb, wres = load_w(VB_OFF, W["w_in"])
        for t in ORD:
            ps, psres = next_acc()
            proj_mm(t, wsb, wres, ps, psres)
            slot = 2 + t if t < NT else 20 + (t - NT)
            if t % 2 == 0:
                P.op("act", lambda e, ps=ps, slot=slot: e.copy(vB[:, slot, :], ps), reads=[psres], writes=[("vB", slot)])
            else:
                P.op("dve", lambda e, ps=ps, slot=slot: e.tensor_copy(vB[:, slot, :], ps), reads=[psres], writes=[("vB", slot)])

    def phase_exchange(l):
        cc_out = cc_outs[l]
        PRE["qB"] = load_w(QOFF["B"], LW[l]["w_in"])
        ecp = [
            (cc_st[:, 4096:5120].rearrange("p (j n) -> p j n", j=4), kTB[:, :, 256:512], [("kTB", 2), ("kTB", 3)], "kBf"),
            (cc_st[:, 5120:6144].rearrange("p (t n) -> p t n", t=2), vB[:, 2:4, :], [("vB", 2), ("vB", 3)], "vBf"),
            (cc_st[:, 6144:6272], kTC[:, 128:256], [("kTC", 1)], "kCf"),
            (cc_st[:, 6272:6400], vC[:, 1, :], [("vC", 1)], "vCf"),
            (cc_st[:, 6400:7424].rearrange("p (j n) -> p j n", j=4), kTB[:, :, 2048:2304], [("kTB", 16), ("kTB", 17)], "kBl"),
            (cc_st[:, 7424:8448].rearrange("p (t n) -> p t n", t=2), vB[:, 16:18, :], [("vB", 16), ("vB", 17)], "vBl"),
            (cc_st[:, 8448:8576], kTC[:, 16 * 128:17 * 128], [("kTC", 16)], "kCl"),
            (cc_st[:, 8576:8704], vC[:, 16, :], [("vC", 16)], "vCl"),
        ]
        ccdeps = {"A": [("cc_st", "kA", t) for t in range(NT)] + [("cc_st", "vA", t) for t in range(NT)],
                  "F": [("cc_st", e_[3]) for e_ in ecp[0:4]], "L": [("cc_st", e_[3]) for e_ in ecp[4:8]]}
        ecps = {"F": ecp[0:4], "L": ecp[4:8], "A": []}
        for kk, (c0, c1) in CCP.items():
            for (o_, i_, rd, nm) in ecps[kk]:
                P.op("pool", lambda e, o_=o_, i_=i_: e.tensor_copy(o_, i_), reads=rd, writes=[("cc_st", nm)])
            P.dma("sp", lambda e, kk=kk, c0=c0, c1=c1: e.dma_start(out=cc_ins[kk][:, :], in_=cc_st[:, c0:c1]),
                  reads=ccdeps[kk] + ["tabregion"], writes=["cc_in" + kk], stream="ccin" + kk)
            P.dma("pool", lambda e, kk=kk: e.collective_compute("AllGather", ALU.bypass, replica_groups=[[0, 1, 2, 3], [4, 5, 6, 7]],
                                                                 ins=[cc_ins[kk].ap().opt()], outs=[cc_out[kk].ap().opt()]),
                  reads=["cc_in" + kk], writes=["cc_out" + kk], stream="cc%d%s" % (l, kk), cc=True)
        for k in range(3):
            P.dma("sp", lambda e, k=k: e.dma_start(out=stg[k], in_=cc_out["L"][k * 128:(k + 1) * 128, :]),
                  reads=["cc_outL", "cc_inA", "cc_inF", "cc_inL"], writes=[("stg", k)], stream="stg%d" % k)
            P.dma("sp", lambda e, k=k: e.dma_start(out=stg[3 + k], in_=cc_out["F"][(k + 1) * 128:(k + 2) * 128, :]),
                  reads=["cc_outF"], writes=[("stg", 3 + k)], stream="stg%d" % (3 + k))
        def pieces(base):
            return [
                (lambda a: a[:, 0:1024].rearrange("p (j n) -> p j n", j=4)),
                (lambda a: a[:, 1024:2048].rearrange("p (t n) -> p t n", t=2)),
                (lambda a: a[:, 2048:2176]),
                (lambda a: a[:, 2176:2304]),
            ]
        dsts_prev = [(kTB[:, :, 0:256], [("kTB", 0), ("kTB", 1)]), (vB[:, 0:2, :], [("vB", 0), ("vB", 1)]),
                     (kTC[:, 0:128], [("kTC", 0)]), (vC[:, 0, :], [("vC", 0)])]
        dsts_next = [(kTB[:, :, 2304:2560], [("kTB", 18), ("kTB", 19)]), (vB[:, 18:20, :], [("vB", 18), ("vB", 19)]),
                     (kTC[:, 17 * 128:18 * 128], [("kTC", 17)]), (vC[:, 17, :], [("vC", 17)])]
        for which, dsts in ((0, dsts_prev), (1, dsts_next)):
            for pi_, (dst, wr) in enumerate(dsts):
                view = pieces(0)[pi_]
                for k in range(3):
                    src_ = view(stg[which * 3 + k])
                    sc = selv[:, which * 3 + k:which * 3 + k + 1]
                    if k == 0:
                        P.op("dve", lambda e, dst=dst, src_=src_, sc=sc: e.tensor_scalar(dst, src_, sc, None, ALU.mult),
                             reads=[("stg", which * 3 + k), "selv"], writes=wr + ["halo_st"])
                    else:
                        P.op("dve", lambda e, dst=dst, src_=src_, sc=sc: e.scalar_tensor_tensor(dst, src_, sc, dst, ALU.mult, ALU.add),
                             reads=[("stg", which * 3 + k), "selv"] + wr, writes=wr + ["halo_st"])

    def attend(l, m, pair, q0, N, tiles, sinkcol=None):
        gi = MIXI[m] * 4 + pair
        nt = len(tiles)
        ak = ctr["att"]
        ctr["att"] += 1
        Tps, tres = (PS_T, "PS_T") if ak % 2 == 0 else (PS_X, "PS_X")
        ggk, gek = gg2[ak % 2], ge2[ak % 2]
        gres, eres = ("gg", ak % 2), ("ge", ak % 2)
        wgi = ctr["wg_cur"]

        def fg(e):
            inst = None
            for kc in range(8):
                inst = e.matmul(PS_G[:, 0:N], lhsT=wg[wgi][:, kc, :], rhs=hxT[:, kc, q0:q0 + N], start=(kc == 0), stop=(kc == 7))
            return inst
        P.op("pe", fg, reads=[("wg", wgi)] + [("hxT", t) for t in range(q0 // 128, (q0 + N) // 128)], writes=["PS_G"])
        P.op("act", lambda e: e.activation(gek[:, 0:N], PS_G[:, 0:N], AF.Exp, scale=-1.0), reads=["PS_G"], writes=[eres])
        P.op("dve", lambda e: e.tensor_scalar_add(gek[:, 0:N], gek[:, 0:N], 1.0), reads=[eres], writes=[eres])
        P.op("dve", lambda e: e.reciprocal(gek[:, 0:N], gek[:, 0:N]), reads=[eres], writes=[eres])
        P.op("dve", lambda e: e.tensor_tensor(ggk[:, 0:N], PS_G[:, 0:N], gek[:, 0:N], ALU.mult), reads=["PS_G", eres], writes=[gres])

        sbuf_i = []
        partial = any(("cols" in kt_) for kt_ in tiles)
        if partial:
            full = [kt_ for kt_ in tiles if "cols" not in kt_]
            rest = [kt_ for kt_ in tiles if "cols" in kt_]
            tiles = full[:1] + rest + full[1:]

        def emit_qk(i):
            kt = tiles[i]
            si = ctr["acc"] % 2
            ctr["acc"] += 1
            S = PS_S[si]
            sres = "PS_S%d" % si
            adds = kt.get("adds", [])

            adds = kt.get("adds", [])

            cl, ch = kt.get("cols", (0, N))

            def fqk(e, S=S, kt=kt, adds=adds, cl=cl, ch=ch):
                na = len(adds)
                e.matmul(S[:, cl:ch], lhsT=kt["kT"][0:64, :], rhs=qTm[0:64, pair, q0 + cl:q0 + ch], start=True, stop=(na == 0))
                inst = e.matmul(S[:, 512 + cl:512 + ch], lhsT=kt["kT"][64:128, :], rhs=qTm[64:128, pair, q0 + cl:q0 + ch],
                                start=True, stop=(na == 0))
                for ai, (c0, ncol, rx, ry, _r) in enumerate(adds):
                    lastf = (ai == na - 1)
                    e.matmul(S[:, c0:c0 + ncol], lhsT=ident_b[:, :], rhs=rx, start=False, stop=lastf)
                    inst = e.matmul(S[:, 512 + c0:512 + c0 + ncol], lhsT=ident_b[:, :], rhs=ry, start=False, stop=lastf)
                return inst
            rds = [kt["kres"], ("qTm", pair), "ident_b"] + [a[4] for a in adds]
            P.op("pe", fqk, reads=rds, writes=[sres])
            sbuf_i.append((S, sres))

        emit_qk(0)
        for i, kt in enumerate(tiles):
            if i + 1 < nt:
                emit_qk(i + 1)
            S, sres = sbuf_i[i]
            pi = ctr["pt"] % 2
            ctr["pt"] += 1
            Pt = PT[pi]
            cl, ch = kt.get("cols", (0, N))
            P.op("act", lambda e, S=S, Pt=Pt, cl=cl, ch=ch: e.activation(
                Pt[:, :, cl:ch], S[:, :].rearrange("p (h n) -> p h n", h=2)[:, :, cl:ch], AF.Exp),
                reads=[sres], writes=[("PT", pi)])

            def fpv(e, kt=kt, Pt=Pt, i=i, cl=cl, ch=ch):
                st, sp = (i == 0), (i == nt - 1)
                sk = partial
                e.matmul(Tps[0:64, cl:ch], lhsT=kt["v"][:, 0:64], rhs=Pt[:, 0, cl:ch], start=st, stop=sp, skip_group_check=sk)
                e.matmul(Tps[64:128, cl:ch], lhsT=kt["v"][:, 64:128], rhs=Pt[:, 1, cl:ch], start=st, stop=sp, tile_position=(0, 64),
                         skip_group_check=sk)
                e.matmul(PS_SM[0:64, cl:ch], lhsT=ones_b[:, 0:64], rhs=Pt[:, 0, cl:ch], start=st, stop=sp, skip_group_check=sk)
                return e.matmul(PS_SM[64:128, cl:ch], lhsT=ones_b[:, 64:128], rhs=Pt[:, 1, cl:ch], start=st, stop=sp,
                                tile_position=(0, 64), skip_group_check=sk)
            P.op("pe", fpv, reads=[kt["vres"], ("PT", pi), "ones_b"], writes=[tres, "PS_SM"])
        if sinkcol is not None:
            P.op("dve", lambda e: e.tensor_scalar_add(rr[:, 0:N], PS_SM[:, 0:N], esink[:, sinkcol:sinkcol + 1]),
                 reads=["PS_SM", "esink"], writes=["rr"])
            P.op("dve", lambda e: e.reciprocal(rr[:, 0:N], rr[:, 0:N]), reads=["rr"], writes=["rr"])
        else:
            P.op("dve", lambda e: e.reciprocal(rr[:, 0:N], PS_SM[:, 0:N]), reads=["PS_SM"], writes=["rr"])
        P.op("pool", lambda e: e.tensor_tensor(ggk[:, 0:N], ggk[:, 0:N], rr[:, 0:N], ALU.mult), reads=[gres, "rr"], writes=[gres])
        ui = ctr["uto"] % 2
        ctr["uto"] += 1
        P.op("dve", lambda e: e.tensor_tensor(uTo[ui][:, 0:N], Tps[:, 0:N], ggk[:, 0:N], ALU.mult),
             reads=[tres, gres], writes=[("uTo", ui)])
        t0 = q0 // 128
        ntile = N // 128
        dst = ut_t[t0:t0 + ntile, :, gi, :].rearrange("t f k -> f t k")
        P.dma("sp", lambda e: e.dma_start(out=dst, in_=uTo[ui][:, 0:N].rearrange("p (t k) -> p t k", k=128)),
              reads=[("uTo", ui)], writes=[("ut", t, gi) for t in range(t0, t0 + ntile)], stream="uto%d" % ui)
        if "uT" in dbg_t and l == layers[0]:
            P.dma("sp", lambda e: e.dma_start(out=dbg_t["uT"][gi, :, q0:q0 + N], in_=uTo[ui][:, 0:N]),
                  reads=[("uTo", ui)], writes=[("dbg_uT", gi, q0)], stream="dbg")

    def phase_mixer(l, m, need_ctx):
        W = LW[l]
        cc_out = cc_outs[l]
        ntq = NTC if need_ctx else NT
        wsb, wres = PRE.pop("q" + m) if ("q" + m) in PRE else load_w(QOFF[m], W["w_in"])
        if m == "B":
            P.dma("pool", lambda e: e.dma_start(out=bzb.rearrange("p h n -> p (h n)"), in_=W["bzb"]), writes=["bzb", "tabregion", "halo_st"], stream="tb0")
            P.dma("pool", lambda e: e.dma_start(out=mzb, in_=mzb_d), writes=["mzb", "halo_st"], stream="tb1")
            P.dma("pool", lambda e: e.dma_start(out=mvbf.rearrange("p t n -> p (t n)"), in_=mvbf_d), writes=["mvbf", "halo_st"], stream="tb2")
            P.dma("pool", lambda e: e.dma_start(out=mvbl.rearrange("p t n -> p (t n)"), in_=mvbl_d), writes=["mvbl", "halo_st"], stream="tb3")
        if m == "C":
            P.dma("pool", lambda e: e.dma_start(out=mzc, in_=mzc_d), writes=["mzc"], stream="tb0")
            P.dma("pool", lambda e: e.dma_start(out=mvcf, in_=mvcf_d), writes=["mvcf"], stream="tb1")
            P.dma("pool", lambda e: e.dma_start(out=mvcl, in_=mvcl_d), writes=["mvcl"], stream="tb2")
            P.dma("sp", lambda e: e.dma_start(out=esink[:, :], in_=W["sink"]), writes=["esink"], stream="misc0")
            P.op("act", lambda e: e.activation(esink[:, :], esink[:, :], AF.Exp), reads=["esink"], writes=["esink"])
        if m == "A":
            P.dma("sp", lambda e: e.dma_start(out=kTA.rearrange("p (r n) -> p r n", r=RPB),
                                             in_=cc_out["A"][:, 0:2048].rearrange("(r p) n -> p r n", p=128)),
                  reads=["cc_outA"], writes=["kTA"], stream="ldA0")
            P.dma("sp", lambda e: e.dma_start(out=vA.rearrange("p (r t) n -> p r (t n)", r=RPB),
                                             in_=cc_out["A"][:, 2048:4096].rearrange("(r p) n -> p r n", p=128)),
                  reads=["cc_outA"], writes=["vA"], stream="ldA1")
        accq = {}
        for t in range(min(LOOK, ntq)):
            accq[t] = next_acc()
            proj_mm(t, wsb, wres, accq[t][0], accq[t][1])

        def qfront(t):
            ps, psres = accq.pop(t)
            if t + LOOK < ntq:
                accq[t + LOOK] = next_acc()
                proj_mm(t + LOOK, wsb, wres, accq[t + LOOK][0], accq[t + LOOK][1])
            tmb, tmn = next_tm()
            if m == "A":
                a_, an, t_, tn, u_, un, st_, stn = next_tset()
                rms_heads(ps, psres, 8, qn_bc, "qn_bc", a_, an, 0.125, t_, tn, st_, stn)
                rope_ops(a_, [an], 8, t, tmb, tmn, t_, tn, u_, un)
            elif m == "C":
                a_, an, t_, tn, u_, un, st_, stn = next_tset()
                P.op("act", lambda e, ps=ps, a_=a_: e.mul(a_, ps, 0.125), reads=[psres], writes=[an])
                rope_ops(a_, [an], 8, t, tmb, tmn, t_, tn, u_, un)
            else:
                P.op("act", lambda e, ps=ps, tmb=tmb: e.mul(tmb, ps, 0.125), reads=[psres], writes=[tmn])
            return tmb, tmn

        def qback(t, tmb, tmn):
            trb, trn = next_tr()

            def ftr(e, tmb=tmb, trb=trb):
                inst = None
                for j in range(4):
                    inst = e.transpose(trb[:, j * 128:(j + 1) * 128], tmb[:, j * 128:(j + 1) * 128], ident_b[:, :])
                return inst
            P.op("pe", ftr, reads=[tmn, "ident_b"], writes=[trn])
            if True:
                P.op("act", lambda e, t=t, trb=trb: e.copy(qTm[:, :, t * 128:(t + 1) * 128], trb[:, 0:512].rearrange("p (j n) -> p j n", j=4)),
                     reads=[trn], writes=[("qTm", j) for j in range(4)])
            else:
                P.op("dve", lambda e, t=t, trb=trb: e.tensor_copy(qTm[:, :, t * 128:(t + 1) * 128], trb[:, 0:512].rearrange("p (j n) -> p j n", j=4)),
                     reads=[trn], writes=[("qTm", j) for j in range(4)])
        fq = {0: qfront(0)}
        for t in range(ntq):
            if t + 1 < ntq:
                fq[t + 1] = qfront(t + 1)
            qback(t, *fq.pop(t))
        P.barrier()
        nxtm = {"B": "C", "C": "A"}.get(m)
        if nxtm is not None:
            PRE["q" + nxtm] = load_w(QOFF[nxtm], W["w_in"])
        if m == "A":
            for j in range(3):
                P.dma("pool", lambda e, j=j: e.dma_start(out=wout_sb[:, 4 * j:4 * j + 4, :],
                                                          in_=W["w_out"][j * 512:(j + 1) * 512, :].rearrange("(j p) n -> p j n", p=128)),
                      writes=[("wout", j)], stream="wout%d" % j)
            PRE["gate4"] = load_w(4 * 512, W["ada_w"])
            PRE["gate5"] = load_w(5 * 512, W["ada_w"])
        wgq = {0: load_w(G_OFF + (MIXI[m] * 4) * 128, W["w_in"], slot_kind="wg", ncols=128)}
        for pair in range(4):
            if pair + 1 < 4:
                wgq[pair + 1] = load_w(G_OFF + (MIXI[m] * 4 + pair + 1) * 128, W["w_in"], slot_kind="wg", ncols=128)
            ctr["wg_cur"] = wgq[pair][1][1]
            chunks = [(c * 512, 512, c) for c in range(4)]
            if need_ctx:
                chunks.append((TOK, 256, None))
            for (q0, N, c) in chunks:
                tiles = []
                if m == "A":
                    ctxA = [dict(kT=kTA_ctx[:, j * 128:(j + 1) * 128], kres=("kTA_ctx", NT + j), v=vA_ctx[:, j, :], vres=("vA_ctx", NT + j))
                            for j in range(2)]
                    if c is not None:
                        for kt_i in range(64):
                            tiles.append(dict(kT=kTA[:, kt_i * 128:(kt_i + 1) * 128], kres="kTA", v=vA[:, kt_i, :], vres="vA"))
                    tiles += ctxA
                    sinkcol = None
                elif m == "C":
                    ctxC = [dict(kT=kTC[:, s_ * 128:(s_ + 1) * 128], kres=("kTC", s_), v=vC[:, s_, :], vres=("vC", s_)) for s_ in (18, 19)]
                    if c is not None:
                        for tt in range(6):
                            s_ = 4 * c + tt
                            if c == 0 and tt == 0:
                                rhs, rres = mvcf, "mvcf"
                            elif c == 3 and tt == 5:
                                rhs, rres = mvcl, "mvcl"
                            else:
                                rhs, rres = mzc[:, (5 - tt) * 128:(9 - tt) * 128], "mzc"
                            blo, bhi = max(0, tt - 2), min(3, tt)
                            cl_, ch_ = blo * 128, (bhi + 1) * 128
                            rsl = rhs[:, cl_:ch_]
                            tiles.append(dict(kT=kTC[:, s_ * 128:(s_ + 1) * 128], kres=("kTC", s_), v=vC[:, s_, :], vres=("vC", s_),
                                              adds=[(cl_, ch_ - cl_, rsl, rsl, rres)], cols=(cl_, ch_)))
                    tiles += ctxC
                    sinkcol = pair
                else:
                    ctxB = [dict(kT=kTB[:, pair, s_ * 128:(s_ + 1) * 128], kres=("kTB", s_), v=vB[:, s_, pair * 128:(pair + 1) * 128], vres=("vB", s_))
                            for s_ in (20, 21)]
                    if c is not None:
                        for tt in range(8):
                            s_ = 4 * c + tt
                            blo, bhi = max(0, tt - 4), min(3, tt)
                            if c == 0:
                                blo = max(0, tt - 5)
                            if c == 3:
                                bhi = min(3, tt + 1)
                            cl_, ch_ = blo * 128, (bhi + 1) * 128
                            bx = bzb[:, 2 * pair, (7 - tt) * 128 + cl_:(7 - tt) * 128 + ch_]
                            by = bzb[:, 2 * pair + 1, (7 - tt) * 128 + cl_:(7 - tt) * 128 + ch_]
                            adds = [(cl_, ch_ - cl_, bx, by, "bzb")]

                            def mrange(lo, hi, tab, res_, base):
                                a_, b_ = max(lo, cl_), min(hi, ch_)
                                if b_ > a_:
                                    sl = tab[:, a_ - base:b_ - base]
                                    adds.append((a_, b_ - a_, sl, sl, res_))
                            gfull = mzb[:, (7 - tt) * 128:(11 - tt) * 128]
                            if c == 0:
                                mrange(0, 256, mvbf[:, tt, :], "mvbf", 0)
                                mrange(256, 512, gfull, "mzb", 0)
                            elif c == 3:
                                mrange(0, 256, gfull, "mzb", 0)
                                mrange(256, 512, mvbl[:, tt, :], "mvbl", 256)
                            else:
                                mrange(0, 512, gfull, "mzb", 0)
                            tiles.append(dict(kT=kTB[:, pair, s_ * 128:(s_ + 1) * 128], kres=("kTB", s_),
                                              v=vB[:, s_, pair * 128:(pair + 1) * 128], vres=("vB", s_), adds=adds, cols=(cl_, ch_)))
                    tiles += ctxB
                    sinkcol = None
                attend(l, m, pair, q0, N, tiles, sinkcol)

    def phase_wout(l, src, dst_x, need_ctx, last):
        W = LW[l]
        phase_gate(l, last)
        if last:
            P.op("dve", lambda e: e.memset(ss[:, :], 0.0), writes=[("ss", t) for t in range(NTC)])
        ntw = NTC if need_ctx else NT

        def wloads(t):
            i = t % 2
            P.dma("sp", lambda e, t=t, i=i: e.dma_start(out=utt[i], in_=ut_t[t, :, :, :]),
                  reads=[("ut", t, g) for g in range(12)], writes=[("utt", i)], stream="utt%d" % i)
            P.dma("sp", lambda e, t=t, i=i: e.dma_start(out=xt[i], in_=src[t * 128:(t + 1) * 128, :]),
                  writes=[("xt", i)], stream="xt%d" % i)
        wloads(0)
        for t in range(ntw):
            i = t % 2
            w = 0 if t < NT else 1
            if t + 1 < ntw:
                wloads(t + 1)
            def f(e, i=i):
                inst = None
                for half in range(2):
                    for j in range(12):
                        inst = e.matmul(PS_S[i][:, half * 512:(half + 1) * 512], lhsT=utt[i][:, j, :],
                                        rhs=wout_sb[:, j, half * 512:(half + 1) * 512], start=(j == 0), stop=(j == 11))
                return inst
            P.op("pe", f, reads=[("utt", i)] + [("wout", j) for j in range(3)], writes=["PS_S%d" % i])
            P.op("dve", lambda e, i=i, w=w: e.tensor_tensor(xn[i], PS_S[i][:, :], gate_bc[w], ALU.mult),
                 reads=["PS_S%d" % i, "gate_bc%d" % w], writes=[("xn", i)])
            P.op("pool", lambda e, i=i: e.tensor_tensor(xn[i], xn[i], xt[i], ALU.add),
                 reads=[("xn", i), ("xt", i)], writes=[("xn", i)])
            if not last:
                P.dma("sp", lambda e, t=t, i=i: e.dma_start(out=dst_x[t * 128:(t + 1) * 128, :], in_=xn[i]),
                      reads=[("xn", i)], writes=[("x1", t)], stream="xo%d" % i)
            else:
                if final_norm:
                    P.op("act", lambda e, t=t, i=i: e.activation(junk, xn[i], AF.Square, scale=1.0 / 32.0,
                                                                accum_out=ss[:, t:t + 1]),
                         reads=[("xn", i)], writes=["junk", ("ss", t)])
                    P.op("act", lambda e, t=t: e.activation(rstd[:, t:t + 1], ss[:, t:t + 1], AF.Ln, bias=eps_c[:, 0:1]),
                         reads=[("ss", t)], writes=[("rstd", t)])
                    P.op("act", lambda e, t=t: e.activation(rstd[:, t:t + 1], rstd[:, t:t + 1], AF.Exp, scale=-0.5),
                         reads=[("rstd", t)], writes=[("rstd", t)])
                    P.op("act", lambda e, t=t, i=i: e.activation(xn[i], xn[i], AF.Copy, scale=rstd[:, t:t + 1]),
                         reads=[("xn", i), ("rstd", t)], writes=[("xn", i)])
                    P.op("pool", lambda e, i=i: e.tensor_tensor(xn[i], xn[i], fnw_sb, ALU.mult),
                         reads=[("xn", i), "fnw"], writes=[("xn", i)])
                P.dma("sp", lambda e, t=t, i=i: e.dma_start(out=out_d[t * 128:(t + 1) * 128, :], in_=xn[i]),
                      reads=[("xn", i)], writes=[("out", t)], stream="xo%d" % i)

    def dbg_dump(name, src_ap, reads):
        if name in dbg_t:
            P.dma("sp", lambda e: e.dma_start(out=dbg_t[name], in_=src_ap), reads=reads, writes=["dbg_" + name], stream="dbg")

    setup()
    src = xin
    stopped = False
    for li, l in enumerate(layers):
        last = (li == len(layers) - 1)
        need_ctx = not last
        dst_x = x1_t.ap()
        phase_mod(l)
        prefetch_kvac(l)
        phase_norm(l, src)
        P.barrier()
        if li == 0:
            dbg_dump("hxT", hxT[:, :, :].rearrange("p k n -> p (k n)"), [("hxT", t) for t in range(NTC)])
        if stop_after == "norm":
            stopped = True
            break
        phase_kvproj(l)
        phase_exchange(l)
        if li == 0:
            dbg_dump("kvr", KVR[:, :], [("kTB", s_) for s_ in range(22)] + [("vB", s_) for s_ in range(22)])
            dbg_dump("kc", kTC[:, :], [("kTC", s_) for s_ in range(20)])
            dbg_dump("vc", vC[:, :, :].rearrange("p t n -> p (t n)"), [("vC", s_) for s_ in range(20)])
            dbg_dump("ccst", cc_st, ["cc_inA", "cc_inF", "cc_inL"])
        if stop_after == "kv":
            stopped = True
            break
        for m in ("B", "C", "A"):
            if m == "A":
                P.barrier()
            phase_mixer(l, m, need_ctx)
            if li == 0:
                dbg_dump("qT" + m, qTm.rearrange("p j n -> p (j n)"), [("qTm", j) for j in range(4)])
            P.barrier()
            if stop_after == "mix" + m:
                stopped = True
                break
        if stopped:
            break
        phase_wout(l, src, dst_x, need_ctx, last)
        P.barrier()
        src = x1_t.ap()
    if stopped:
        P.op("pool", lambda e: e.memset(xn[0], 0.0), writes=[("xn", 0)])
        P.dma("sp", lambda e: e.dma_start(out=out_d[0:128, :], in_=xn[0]), reads=[("xn", 0)], writes=["outz"], stream="xo0")
        P.barrier()

    print("sbuf bytes remaining:", nc.sbuf_bytes_remaining)
    with nc.Block() as block:
        P.emit(block)
    return nc


def make_in_maps(inputs, layers=(0, 1)):
    x = np.asarray(inputs["x"], np.float32)
    c = np.asarray(inputs["c"], np.float32)
    ctx = np.asarray(inputs["ctx"], np.float32)
    c_ctx = np.asarray(inputs["c_ctx"], np.float32)
    shared = {}
    for l in layers:
        shared["w_in%d" % l] = _perm_w_in(np.asarray(inputs["w_in"][l], np.float32))
        shared["w_out%d" % l] = _perm_w_out(np.asarray(inputs["w_out"][l], np.float32))
        shared["ada_w%d" % l] = np.ascontiguousarray(np.asarray(inputs["ada_w"][l], np.float32))
        ab = np.asarray(inputs["ada_b"][l], np.float32)
        shared["adabT%d" % l] = np.ascontiguousarray(ab[0:2048].reshape(16, 128).T)
        shared["adabg%d" % l] = np.ascontiguousarray(ab[2048:3072].reshape(1, D))
        shared["normT%d" % l] = np.ascontiguousarray(np.asarray(inputs["norm_w"][l], np.float32).reshape(8, 128).T)
        shared["qn%d" % l] = np.asarray(inputs["q_norm_a"][l], np.float32).reshape(1, 64).copy()
        shared["kn%d" % l] = np.asarray(inputs["k_norm_a"][l], np.float32).reshape(1, 64).copy()
        shared["bzb%d" % l] = _b_bias(np.asarray(inputs["rpb_b"][l], np.float32)).reshape(128, 8 * 11 * 128)
        sk = np.asarray(inputs["sink_c"][l], np.float32)
        st = np.zeros((128, 4), np.float32)
        for j in range(4):
            st[0:64, j] = sk[j]
            st[64:128, j] = sk[j + 4]
        shared["sink%d" % l] = st
    shared["fnw"] = np.asarray(inputs["final_norm_w"], np.float32).reshape(1, D).copy()
    maps = []
    for core in range(NCORE):
        b, r = core // RPB, core % RPB
        m = dict(shared)
        m["xin"] = np.ascontiguousarray(np.concatenate([x[b, r * TOK:(r + 1) * TOK, :], ctx[b]], axis=0))
        cv = np.zeros((128, 8, 2), np.float32)
        cv[:, :, 0] = c[b].reshape(8, 128).T
        cv[:, :, 1] = c_ctx.reshape(8, 128).T
        m["cvec"] = cv.reshape(128, 16)
        m["rope"] = _rope_tables(r).reshape(128, NTC * 2 * 64)
        m["ident"] = np.eye(128, dtype=np.float32)
        sv = np.zeros((128, 8), np.float32)
        if r > 0:
            sv[:, r - 1] = 1.0
        if r < RPB - 1:
            sv[:, 3 + r] = 1.0
        m["selv"] = sv
        gen, first, last = _b_masks(r)
        m["mzb"], m["mvbf"], m["mvbl"] = gen, first, last
        gen, first, last = _c_masks(r)
        m["mzc"], m["mvcf"], m["mvcl"] = gen, first, last
        maps.append(m)
    return maps


_NC_CACHE = {}


def kernel(**inputs):
    if "full" not in _NC_CACHE:
        _NC_CACHE["full"] = build_program()
    nc = _NC_CACHE["full"]
    maps = make_in_maps(inputs)
    res = run_bass_kernel_spmd(nc, maps, core_ids=list(range(NCORE)))
    out = np.zeros((2, SEQ, D), np.float32)
    for core in range(NCORE):
        b, r = core // RPB, core % RPB
        out[b, r * TOK:(r + 1) * TOK, :] = res.results[core]["out"]
    return out
```
